# Optimizing a Trainium2 kernel written in Bass

```python
import math, functools
import jax, jax.numpy as jnp
from jax import lax
import numpy as np

D_MODEL = 1024
BATCH = 2
SEQ = 16384
DEPTH = 2
DEC_BATCH = 16
DEC_SEQ = 32
PAST_LEN = 1024

CHUNK = 64
HEAD_DIM = 64
A_HEADS = 8
A_PREV_CHUNKS = 8
A_BAND = A_PREV_CHUNKS * CHUNK
A_REL_CLIP = 128
B_Q_HEADS = 8
B_KV_HEADS = 2
B_GROUP = B_Q_HEADS // B_KV_HEADS
B_WINDOW = 128
B_PREV_CHUNKS = B_WINDOW // CHUNK
T5_BUCKETS = 32
T5_MAX_DIST = 128
C_HEADS = 16
C_Q_LORA = 384
C_KV_LORA = 256
C_NOPE = 64
C_ROPE = 32
C_V = 64
ROPE_BASE = 10000.0
MLA_Q_BLOCK = 128
D_FF = 2816
N_EVEN = (DEPTH + 1) // 2
N_ODD = DEPTH // 2
EPS = 1e-6
NEG = -1e30
A_W = A_HEADS * HEAD_DIM
BQ_W = B_Q_HEADS * HEAD_DIM
BKV_W = B_KV_HEADS * HEAD_DIM
AB_SPLITS = (A_W, 2 * A_W, 3 * A_W, 3 * A_W + BQ_W, 3 * A_W + BQ_W + BKV_W)
AB_WIDTH = 3 * A_W + BQ_W + 2 * BKV_W
AB_OUT = A_W + BQ_W
C_IN_WIDTH = C_Q_LORA + C_KV_LORA + C_ROPE

kernel_name = "hybrid_chunk_streaming_encoder_step"


def rmsnorm(x, g):
    xf = x.astype(jnp.float32)
    y = xf * lax.rsqrt(jnp.mean(xf * xf, axis=-1, keepdims=True) + EPS)
    return (y * g.astype(jnp.float32)).astype(x.dtype)


def modulate(h, shift, scale):
    return h * (1 + scale[:, None, :]) + shift[:, None, :]


def swiglu(h, wg, wu, wd):
    return (jax.nn.silu(h @ wg) * (h @ wu)) @ wd


def attend(q, k, v, bias=None, mask=None, sink=None):
    scale = q.shape[-1] ** -0.5
    s = jnp.einsum('...qhgd,...khd->...hgqk', q, k).astype(jnp.float32) * scale
    if bias is not None:
        s = s + bias
    if mask is not None:
        s = jnp.where(mask, s, NEG)
    m = jnp.max(s, axis=-1, keepdims=True)
    if sink is not None:
        sk = sink.astype(jnp.float32)[:, :, None, None]
        m = jnp.maximum(m, sk)
    p = jnp.exp(s - m)
    den = jnp.sum(p, axis=-1, keepdims=True)
    if sink is not None:
        den = den + jnp.exp(sk - m)
    p = (p / den).astype(v.dtype)
    return jnp.einsum('...hgqk,...khd->...qhgd', p, v)


def chunk_band(t, n_prev):
    b, s, h, d = t.shape
    nc = s // CHUNK
    tp = jnp.pad(t.reshape(b, nc, CHUNK, h, d), ((0, 0), (n_prev, 0), (0, 0), (0, 0), (0, 0)))
    band = jnp.stack([tp[:, i:i + nc] for i in range(n_prev + 1)], axis=2)
    return band.reshape(b, nc, (n_prev + 1) * CHUNK, h, d)


def band_mask(nc, n_prev):
    kb = (n_prev + 1) * CHUNK
    valid = (jnp.arange(nc)[:, None] - n_prev + jnp.arange(kb)[None, :] // CHUNK) >= 0
    return valid.reshape(nc, 1, 1, 1, kb)


def band_rel(n_prev):
    return jnp.arange((n_prev + 1) * CHUNK)[None, :] - n_prev * CHUNK - jnp.arange(CHUNK)[:, None]


def stream_rel(n_cache, t):
    return jnp.arange(n_cache + t)[None, :] - n_cache - jnp.arange(t)[:, None]


def clipped_bias(table, rel):
    idx = jnp.clip(rel, -A_REL_CLIP, A_REL_CLIP) + A_REL_CLIP
    return jnp.transpose(table[idx].astype(jnp.float32), (2, 0, 1))[:, None]


def t5_bucket(rel):
    nb = T5_BUCKETS // 2
    ret = jnp.where(rel > 0, nb, 0)
    n = jnp.abs(rel)
    max_exact = nb // 2
    large = max_exact + (jnp.log(jnp.maximum(n, 1).astype(jnp.float32) / max_exact)
                         / math.log(T5_MAX_DIST / max_exact) * (nb - max_exact)).astype(jnp.int32)
    large = jnp.minimum(large, nb - 1)
    return ret + jnp.where(n < max_exact, n, large)


def t5_bias_fn(table, rel):
    tq, tk = rel.shape
    b = table[t5_bucket(rel)].astype(jnp.float32)
    return jnp.transpose(b, (2, 0, 1)).reshape(B_KV_HEADS, B_GROUP, tq, tk)


def rope(x, pos):
    half = x.shape[-1] // 2
    inv = ROPE_BASE ** (-jnp.arange(half, dtype=jnp.float32) / half)
    ang = pos.astype(jnp.float32)[:, None] * inv[None, :]
    cos = jnp.cos(ang)[:, None, :]
    sin = jnp.sin(ang)[:, None, :]
    xf = x.astype(jnp.float32)
    x1, x2 = xf[..., :half], xf[..., half:]
    return jnp.concatenate([x1 * cos - x2 * sin, x1 * sin + x2 * cos], axis=-1).astype(x.dtype)


def heads_ab(h, w_in):
    b, t, _ = h.shape
    qa, ka, va, qb, kb, vb = jnp.split(h @ w_in, AB_SPLITS, axis=-1)
    r = lambda z, n: z.reshape(b, t, n, HEAD_DIM)
    return (r(qa, A_HEADS), r(ka, A_HEADS), r(va, A_HEADS),
            r(qb, B_Q_HEADS), r(kb, B_KV_HEADS), r(vb, B_KV_HEADS))


def mixer_ab_prompt(h, w_in, w_out, rel_tab, t5_tab, sinks):
    b, s, _ = h.shape
    nc = s // CHUNK
    qa, ka, va, qb, kb, vb = heads_ab(h, w_in)
    oa = attend(qa.reshape(b, nc, CHUNK, A_HEADS, 1, HEAD_DIM),
                chunk_band(ka, A_PREV_CHUNKS), chunk_band(va, A_PREV_CHUNKS),
                bias=clipped_bias(rel_tab, band_rel(A_PREV_CHUNKS)),
                mask=band_mask(nc, A_PREV_CHUNKS))
    ob = attend(qb.reshape(b, nc, CHUNK, B_KV_HEADS, B_GROUP, HEAD_DIM),
                chunk_band(kb, B_PREV_CHUNKS), chunk_band(vb, B_PREV_CHUNKS),
                bias=t5_bias_fn(t5_tab, band_rel(B_PREV_CHUNKS)),
                mask=band_mask(nc, B_PREV_CHUNKS),
                sink=sinks.reshape(B_KV_HEADS, B_GROUP))
    y = jnp.concatenate([oa.reshape(b, s, A_W), ob.reshape(b, s, BQ_W)], axis=-1) @ w_out
    la, lb = min(A_BAND, s), min(B_WINDOW, s)
    return y, (ka[:, s - la:], va[:, s - la:], kb[:, s - lb:], vb[:, s - lb:])


def mixer_ab_sample(h, cak, cav, cbk, cbv, w_in, w_out, rel_tab, t5_tab, sinks):
    b, t, _ = h.shape
    qa, ka, va, qb, kb, vb = heads_ab(h, w_in)
    la = cak.shape[1]
    oa = attend(qa[:, :, :, None], jnp.concatenate([cak, ka], axis=1), jnp.concatenate([cav, va], axis=1),
                bias=clipped_bias(rel_tab, stream_rel(la, t)))
    lb = cbk.shape[1]
    ob = attend(qb.reshape(b, t, B_KV_HEADS, B_GROUP, HEAD_DIM),
                jnp.concatenate([cbk, kb], axis=1), jnp.concatenate([cbv, vb], axis=1),
                bias=t5_bias_fn(t5_tab, stream_rel(lb, t)),
                sink=sinks.reshape(B_KV_HEADS, B_GROUP))
    y = jnp.concatenate([oa.reshape(b, t, A_W), ob.reshape(b, t, BQ_W)], axis=-1) @ w_out
    return y, (ka, va, kb, vb)


def mla_project(h, pos, w_in, qn_g, kvn_g, w_qb):
    b, t, _ = h.shape
    q_lat, kv_lat, k_r = jnp.split(h @ w_in, (C_Q_LORA, C_Q_LORA + C_KV_LORA), axis=-1)
    q = (rmsnorm(q_lat, qn_g) @ w_qb).reshape(b, t, C_HEADS, C_NOPE + C_ROPE)
    q = jnp.concatenate([q[..., :C_NOPE], rope(q[..., C_NOPE:], pos)], axis=-1)
    kv_lat = rmsnorm(kv_lat, kvn_g)
    k_r = rope(k_r[:, :, None, :], pos)[:, :, 0]
    return q, kv_lat, k_r


def mla_keys(kv_lat, k_r, w_kvb):
    b, l, _ = kv_lat.shape
    kv = (kv_lat @ w_kvb).reshape(b, l, C_HEADS, C_NOPE + C_V)
    k = jnp.concatenate([kv[..., :C_NOPE], jnp.broadcast_to(k_r[:, :, None, :], (b, l, C_HEADS, C_ROPE))], axis=-1)
    return k, kv[..., C_NOPE:]


def mixer_c_prompt(h, w_in, qn_g, kvn_g, w_qb, w_kvb, w_out):
    b, s, _ = h.shape
    q, kv_lat, k_r = mla_project(h, jnp.arange(s), w_in, qn_g, kvn_g, w_qb)
    k, v = mla_keys(kv_lat, k_r, w_kvb)
    nb = s // MLA_Q_BLOCK
    qb = q.reshape(b, nb, MLA_Q_BLOCK, C_HEADS, 1, C_NOPE + C_ROPE).transpose(1, 0, 2, 3, 4, 5)
    kchunk = jnp.arange(s) // CHUNK

    def block(args):
        qi, i = args
        qchunk = (i * MLA_Q_BLOCK + jnp.arange(MLA_Q_BLOCK)) // CHUNK
        return attend(qi, k, v, mask=kchunk[None, :] <= qchunk[:, None])

    o = lax.map(block, (qb, jnp.arange(nb)))
    o = o.transpose(1, 0, 2, 3, 4, 5).reshape(b, s, C_HEADS * C_V)
    return o @ w_out, (kv_lat, k_r)


def mixer_c_sample(h, ckv, ckr, w_in, qn_g, kvn_g, w_qb, w_kvb, w_out):
    b, t, _ = h.shape
    past = ckv.shape[1]
    q, kv_lat, k_r = mla_project(h, past + jnp.arange(t), w_in, qn_g, kvn_g, w_qb)
    k, v = mla_keys(jnp.concatenate([ckv, kv_lat], axis=1), jnp.concatenate([ckr, k_r], axis=1), w_kvb)
    o = attend(q[:, :, :, None], k, v).reshape(b, t, C_HEADS * C_V)
    return o @ w_out, (kv_lat, k_r)


def macaron_layer(x, c, mix, w_ada_l, b_ada_l, norm_g_l, wg, wu, wd):
    mods = jnp.split(jax.nn.silu(c) @ w_ada_l + b_ada_l, 9, axis=-1)
    h = modulate(rmsnorm(x, norm_g_l[0]), mods[0], mods[1])
    x = x + 0.5 * mods[2][:, None, :] * swiglu(h, wg[0], wu[0], wd[0])
    h = modulate(rmsnorm(x, norm_g_l[1]), mods[3], mods[4])
    y, st = mix(h)
    x = x + mods[5][:, None, :] * y
    h = modulate(rmsnorm(x, norm_g_l[2]), mods[6], mods[7])
    x = x + 0.5 * mods[8][:, None, :] * swiglu(h, wg[1], wu[1], wd[1])
    return x, st


def setup_inputs(seed: int = 0) -> dict:
    key = jax.random.key(seed)
    ks = iter(jax.random.split(key, 40))
    nrm = lambda shape, s=1.0: jax.random.normal(next(ks), shape, jnp.float32) * s
    la, lb = min(A_BAND, PAST_LEN), min(B_WINDOW, PAST_LEN)
    return {
        "x_prompt": nrm((BATCH, SEQ, D_MODEL)),
        "x_sample": nrm((DEC_BATCH, DEC_SEQ, D_MODEL)),
        "c_prompt": nrm((BATCH, D_MODEL)),
        "c_sample": nrm((DEC_BATCH, D_MODEL)),
        "cache_a_k": nrm((N_EVEN, DEC_BATCH, la, A_HEADS, HEAD_DIM)),
        "cache_a_v": nrm((N_EVEN, DEC_BATCH, la, A_HEADS, HEAD_DIM)),
        "cache_b_k": nrm((N_EVEN, DEC_BATCH, lb, B_KV_HEADS, HEAD_DIM)),
        "cache_b_v": nrm((N_EVEN, DEC_BATCH, lb, B_KV_HEADS, HEAD_DIM)),
        "cache_c_kv": nrm((N_ODD, DEC_BATCH, PAST_LEN, C_KV_LORA)),
        "cache_c_kr": nrm((N_ODD, DEC_BATCH, PAST_LEN, C_ROPE)),
        "w_ada": nrm((DEPTH, D_MODEL, 9 * D_MODEL), 0.3 * D_MODEL ** -0.5),
        "b_ada": nrm((DEPTH, 9 * D_MODEL), 0.02),
        "norm_g": 1.0 + nrm((DEPTH, 3, D_MODEL), 0.02),
        "final_norm_g": 1.0 + nrm((D_MODEL,), 0.02),
        "ffn_w_gate": nrm((DEPTH, 2, D_MODEL, D_FF), D_MODEL ** -0.5),
        "ffn_w_up": nrm((DEPTH, 2, D_MODEL, D_FF), D_MODEL ** -0.5),
        "ffn_w_down": nrm((DEPTH, 2, D_FF, D_MODEL), D_FF ** -0.5),
        "w_in_ab": nrm((N_EVEN, D_MODEL, AB_WIDTH), D_MODEL ** -0.5),
        "w_out_ab": nrm((N_EVEN, AB_OUT, D_MODEL), AB_OUT ** -0.5),
        "rel_bias_a": nrm((N_EVEN, 2 * A_REL_CLIP + 1, A_HEADS), 0.1),
        "t5_bias": nrm((T5_BUCKETS, B_Q_HEADS), 0.1),
        "sinks_b": nrm((N_EVEN, B_Q_HEADS), 0.5),
        "w_in_c": nrm((N_ODD, D_MODEL, C_IN_WIDTH), D_MODEL ** -0.5),
        "c_q_norm_g": 1.0 + nrm((N_ODD, C_Q_LORA), 0.02),
        "c_kv_norm_g": 1.0 + nrm((N_ODD, C_KV_LORA), 0.02),
        "w_qb": nrm((N_ODD, C_Q_LORA, C_HEADS * (C_NOPE + C_ROPE)), C_Q_LORA ** -0.5),
        "w_kvb": nrm((N_ODD, C_KV_LORA, C_HEADS * (C_NOPE + C_V)), C_KV_LORA ** -0.5),
        "w_out_c": nrm((N_ODD, C_HEADS * C_V, D_MODEL), (C_HEADS * C_V) ** -0.5),
    }


def reference(x_prompt, x_sample, c_prompt, c_sample, cache_a_k, cache_a_v, cache_b_k, cache_b_v,
              cache_c_kv, cache_c_kr, w_ada, b_ada, norm_g, final_norm_g, ffn_w_gate, ffn_w_up, ffn_w_down,
              w_in_ab, w_out_ab, rel_bias_a, t5_bias, sinks_b, w_in_c, c_q_norm_g, c_kv_norm_g, w_qb, w_kvb,
              w_out_c):
    xp, xs = x_prompt, x_sample
    ab_p, ab_s, c_p, c_s = [], [], [], []
    for l in range(DEPTH):
        lw = (w_ada[l], b_ada[l], norm_g[l], ffn_w_gate[l], ffn_w_up[l], ffn_w_down[l])
        if l % 2 == 0:
            e = l // 2
            shared = dict(w_in=w_in_ab[e], w_out=w_out_ab[e], rel_tab=rel_bias_a[e], t5_tab=t5_bias, sinks=sinks_b[e])
            mix_p = functools.partial(mixer_ab_prompt, **shared)
            mix_s = functools.partial(mixer_ab_sample, cak=cache_a_k[e], cav=cache_a_v[e],
                                      cbk=cache_b_k[e], cbv=cache_b_v[e], **shared)
            xp, st = macaron_layer(xp, c_prompt, mix_p, *lw)
            ab_p.append(st)
            xs, st = macaron_layer(xs, c_sample, mix_s, *lw)
            ab_s.append(st)
        else:
            o = l // 2
            shared = dict(w_in=w_in_c[o], qn_g=c_q_norm_g[o], kvn_g=c_kv_norm_g[o], w_qb=w_qb[o],
                          w_kvb=w_kvb[o], w_out=w_out_c[o])
            mix_p = functools.partial(mixer_c_prompt, **shared)
            mix_s = functools.partial(mixer_c_sample, ckv=cache_c_kv[o], ckr=cache_c_kr[o], **shared)
            xp, st = macaron_layer(xp, c_prompt, mix_p, *lw)
            c_p.append(st)
            xs, st = macaron_layer(xs, c_sample, mix_s, *lw)
            c_s.append(st)
    y_prompt = rmsnorm(xp, final_norm_g)
    y_sample = rmsnorm(xs, final_norm_g)
    stk = lambda lst, i: jnp.stack([st[i] for st in lst], axis=0)
    return (y_prompt, y_sample,
            stk(ab_p, 0), stk(ab_p, 1), stk(ab_p, 2), stk(ab_p, 3), stk(c_p, 0), stk(c_p, 1),
            stk(ab_s, 0), stk(ab_s, 1), stk(ab_s, 2), stk(ab_s, 3), stk(c_s, 0), stk(c_s, 1))
```

```python
import numpy as np
import concourse.bass as bass
import concourse.mybir as mybir

F32 = mybir.dt.float32
BF16 = mybir.dt.bfloat16
AF = mybir.ActivationFunctionType
ALU = mybir.AluOpType

EPOCH = 16000
DMA_EPOCH = 1000


class Sched:
    ENGS = ("pe", "act", "dve", "pool", "sp")

    def __init__(self, nc, same_engine_sync=("act", "dve", "pool")):
        self.nc = nc
        self.ops = []
        self.lastw = {}
        self.lastr = {}
        self.same = set(same_engine_sync)
        self.dma_count = {}

    def _deps(self, reads, writes):
        deps = set()
        for k in reads:
            for d in self.lastw.get(k, {}).values():
                deps.add(d)
        for k in writes:
            for d in self.lastw.get(k, {}).values():
                deps.add(d)
            for d in self.lastr.get(k, {}).values():
                deps.add(d)
        return deps

    def _record(self, idx, agent, reads, writes):
        for k in reads:
            self.lastr.setdefault(k, {})[agent] = idx
        for k in writes:
            self.lastw.setdefault(k, {})[agent] = idx

    def op(self, eng, fn, reads=(), writes=()):
        idx = len(self.ops)
        deps = self._deps(reads, writes)
        self.ops.append(dict(kind="c", eng=eng, fn=fn, deps=deps))
        self._record(idx, eng, reads, writes)
        return idx

    def dma(self, queue, fn, key, reads=(), writes=(), final=False):
        idx = len(self.ops)
        key = key + "_" + queue
        deps = self._deps(reads, writes)
        n = self.dma_count.get(key, 0) + 1
        self.dma_count[key] = n
        self.ops.append(dict(kind="d", eng=queue, fn=fn, deps=deps, key=key, n=n, final=final))
        self._record(idx, "dma:" + key, reads, writes)
        return idx

    def emit(self, stack):
        nc = self.nc
        ops = self.ops
        needed = set()
        for i, o in enumerate(ops):
            for d in o["deps"]:
                od = ops[d]
                if od["kind"] == "c":
                    if od["eng"] == o["eng"] and od["eng"] not in self.same:
                        continue
                    needed.add(d)
        cnt = {e: 0 for e in self.ENGS}
        for i, o in enumerate(ops):
            if o["kind"] == "c" and i in needed:
                cnt[o["eng"]] += 1
                o["ms"] = cnt[o["eng"]]
        sems = {}

        def sem(name):
            if name not in sems:
                sems[name] = stack.enter_context(nc.semaphore(name))
            return sems[name]

        def target(d):
            od = ops[d]
            if od["kind"] == "c":
                m = od["ms"]
                return ("c_%s_%d" % (od["eng"], (m - 1) // EPOCH), (m - 1) % EPOCH + 1)
            n = od["n"]
            return ("d_%s_%d" % (od["key"], (n - 1) // DMA_EPOCH), ((n - 1) % DMA_EPOCH + 1) * 16)

        per_eng = {e: [] for e in self.ENGS}
        for i, o in enumerate(ops):
            per_eng[o["eng"]].append(i)
        for i, o in enumerate(ops):
            if o["kind"] == "c":
                if "ms" in o:
                    sem(target(i)[0])
            else:
                sem(target(i)[0])
        self.n_sems = len(sems)

        def run(engname, eng):
            waited = {}
            for i in per_eng[engname]:
                o = ops[i]
                wl = {}
                for d in o["deps"]:
                    od = ops[d]
                    if od["kind"] == "c" and od["eng"] == engname and engname not in self.same:
                        continue
                    s, v = target(d)
                    if waited.get(s, 0) >= v:
                        continue
                    wl[s] = max(wl.get(s, 0), v)
                for s, v in wl.items():
                    eng.wait_ge(sem(s), v)
                    waited[s] = v
                ins = o["fn"](eng)
                if o["kind"] == "c":
                    if "ms" in o:
                        ins.then_inc(sem(target(i)[0]), 1)
                else:
                    ins.then_inc(sem(target(i)[0]), 16)
            for i in per_eng[engname]:
                o = ops[i]
                if o["kind"] == "d" and o.get("final"):
                    s, v = target(i)
                    if waited.get(s, 0) < v:
                        eng.wait_ge(sem(s), v)
                        waited[s] = v

        with nc.Block() as block:
            @block.sync
            def _(e):
                run("sp", e)

            @block.scalar
            def _(e):
                run("act", e)

            @block.vector
            def _(e):
                run("dve", e)

            @block.gpsimd
            def _(e):
                run("pool", e)

            @block.tensor
            def _(e):
                run("pe", e)

from contextlib import ExitStack
import ml_dtypes
import jax
import jax.numpy as jnp
from concourse.bass_utils import run_bass_kernel_spmd

D = 1024
FF = 2816
NFC = 22
TOK = 512
ST = 4
NSEG = 2
EPS = 1e-6
BIG = 16384.0
MAXDESC = 512
NEG = -1e30
MLA_SCALE = 96 ** -0.5
NQT = NSEG * ST


def seg_of(j, k):
    return j if k == 0 else 7 - j


def nv_of(qi):
    k, a = divmod(qi, ST)
    return (ST * (3 if k == 0 else 7)) + a + 1


def gloc(v):
    s, a = divmod(v, ST)
    jj = s if s < 4 else 7 - s
    kk = 0 if s < 4 else 1
    return jj, (kk * ST + a) * TOK


class Rot:
    def __init__(self, items):
        self.items = list(items)
        self.i = 0

    def next(self):
        r = self.items[self.i % len(self.items)]
        self.i += 1
        return r


def build_program():
    nc = bass.Bass("TRN2", target_bir_lowering=False)
    st = ExitStack()
    S = Sched(nc)

    def din(name, shape, dt=F32):
        return nc.dram_tensor(name, list(shape), dt, kind="ExternalInput").ap()

    def dout(name, shape, dt=F32):
        return nc.dram_tensor(name, list(shape), dt, kind="ExternalOutput").ap()

    def dint(name, shape, dt):
        return nc.dram_tensor(name, list(shape), dt).ap()

    def sb(name, shape, dt):
        return st.enter_context(nc.sbuf_tensor(name, list(shape), dt))

    xp = din("xp", [NSEG, (ST + 1) * TOK, D])
    xs = din("xs", [2, 32, D])
    cT = din("cT", [128, 8, 3])
    cak = din("cak", [2, 512, 512]); cav = din("cav", [2, 512, 512])
    cbk = din("cbk", [2, 128, 128]); cbv = din("cbv", [2, 128, 128])
    cckv = din("cckv", [2, 1024, 256]); cckr = din("cckr", [2, 1024, 32])
    w_ada = din("w_ada", [2, D, 9 * D]); b_adaT = din("b_adaT", [128, 2, 72])
    gT = din("gT", [128, 7, 8])
    wg_d = din("ffn_w_gate", [2, 2, D, FF]); wu_d = din("ffn_w_up", [2, 2, D, FF]); wd_d = din("ffn_w_down", [2, 2, FF, D])
    w_in_ab = din("w_in_ab", [D, 2304]); w_out_ab = din("w_out_ab", [D, D])
    w_in_c = din("w_in_c", [D, 672]); w_qb = din("w_qb", [384, 1536]); w_kvb = din("w_kvb", [256, 2048]); w_out_c = din("w_out_c", [D, D])
    gq_d = din("gq", [128, 384]); gkv_d = din("gkv", [128, 256])
    FA_d = din("FA", [8, 128, 640]); FB_d = din("FB", [8, 128, 256])
    FAs_d = din("FAs", [8, 128, 5, 32]); FBs_d = din("FBs", [8, 128, 2, 32])
    sinks_d = din("sinks", [128, 8]); hv_d = din("hv", [128, 4])
    cosP = din("cosP", [NQT * 4, 128, 256]); sinP = din("sinP", [NQT * 4, 128, 256])
    cosS = din("cosS", [32, 256]); sinS = din("sinS", [32, 256])
    kaug_d = din("kaug", [NQT, 32, 9, TOK], BF16); kaugs_d = din("kaugs", [3, 9, TOK], BF16)
    qaug_d = din("qaug", [9, TOK], BF16)
    identf_d = din("identf", [128, 128]); identb_d = din("identb", [128, 128], BF16)

    y_p = dout("y_p", [NQT * TOK, D]); y_s = dout("y_s", [64, D])
    o_ak = dout("o_ak", [512, 512]); o_av = dout("o_av", [512, 512]); o_bk = dout("o_bk", [128, 128]); o_bv = dout("o_bv", [128, 128])
    o_ckv = dout("o_ckv", [NQT * TOK, 256]); o_ckr = dout("o_ckr", [NQT * TOK, 32])
    o_aks = dout("o_aks", [64, 512]); o_avs = dout("o_avs", [64, 512]); o_bks = dout("o_bks", [64, 128]); o_bvs = dout("o_bvs", [64, 128])
    o_ckvs = dout("o_ckvs", [64, 256]); o_ckrs = dout("o_ckrs", [64, 32])

    x_sp = dint("x_sp", [NQT + 2, 128, 8, TOK], F32)
    q_sp = dint("q_sp", [NQT + 2, 96, 16, TOK], BF16)
    NGC = NQT // 2
    g_in = [dint("g_in%d" % i, [288, 2 * TOK], BF16) for i in range(NGC)]
    g_all = [dint("g_all%d" % i, [4 * 288, 2 * TOK], BF16) for i in range(NGC)]
    g_s = dint("g_s", [2, 288, 3 * TOK], BF16)

    xT = sb("xT", [128, 8, TOK], F32)
    hT = sb("hT", [128, 8, TOK], BF16)
    act = sb("act", [128, 24 * TOK], BF16)
    actc = lambda c: act[:, c * TOK:(c + 1) * TOK]
    xin = act[:, 0:16 * TOK].bitcast(F32).rearrange("p (b d) -> p b d", d=D)
    sq = act[:, 16 * TOK:24 * TOK].rearrange("p (c t) -> p c t", t=TOK)
    yfin = act[:, 0:16 * TOK].bitcast(F32).rearrange("p (c t) -> p c t", t=TOK)
    scr = sb("scr", [128, 6, TOK], F32)
    qo = sb("qo", [128, 16, TOK], BF16)
    kTA = sb("kTA", [64, 2, 8, TOK], BF16)
    kTB = sb("kTB", [64, 2, 2, TOK], BF16)
    VA = sb("VA", [128, 2, 4, 8, 80], BF16)
    VB = sb("VB", [128, 2, 4, 2, 80], BF16)
    Sb = sb("Sb", [128, 2, TOK], F32)
    PT = sb("PT", [128, 4, TOK], BF16)
    fa = sb("fa", [128, 2, 640], F32)
    fb = sb("fb", [128, 2, 256], F32)
    NWP = 5
    wp = sb("wp", [128, NWP, 3072], BF16)
    M = sb("M", [128, 2, 9, 8, 3], F32)
    gTs = sb("gTs", [128, 7, 8], F32)
    scT = sb("scT", [128, 8, 3], F32)
    bT = sb("bT", [128, 2, 72], F32)
    identf = sb("identf_s", [128, 128], F32)
    identb = sb("identb_s", [128, 128], BF16)
    onesb = sb("onesb", [128, 128], BF16)
    onesf = sb("onesf", [128, 64], F32)
    zcol = sb("zcol", [128, 1], F32)
    sinkexp = sb("sinkexp", [128, 8], F32)
    hv = sb("hv_s", [128, 4], F32)
    gq = sb("gq_s", [128, 384], F32)
    gkv = sb("gkv_s", [128, 256], F32)
    den = sb("den", [128, TOK], F32)
    rb = sb("rb", [64, TOK], F32)
    small = sb("small", [128, 8], F32)
    qn_bf = sb("qn_bf", [128, 384], BF16)
    kvn_f = sb("kvn_f", [128, 256], F32)
    kvn_b = sb("kvn_b", [128, 288], BF16)
    kr_f = sb("kr_f", [128, 32], F32)
    q_bf = sb("q_bf", [128, 16, 96], BF16)
    qnT = sb("qnT", [128, 3, TOK], BF16)
    cs = sb("cs", [128, 2, 256], F32)
    rtmp = sb("rtmp", [128, 4, 256], F32)
    stg = sb("stg", [128, 2, 1024], F32)
    gT_sb = sb("gT_sb", [128, 3, TOK], BF16)
    VAflat = VA[:, :, :, :, :].rearrange("p a b c d -> p (a b c d)")
    KT = VAflat[0:105, 0:4 * TOK].rearrange("p (i t) -> p i t", t=TOK)
    Vh = VAflat[:, 4 * TOK:4 * TOK + 1280].rearrange("p (i b d) -> p i b d", b=4, d=80)
    qh = sb("qh", [105, 4, TOK], BF16)
    kvT = sb("kvT", [128, 2, 2, TOK], BF16)
    wkvb = sb("wkvb", [128, 2, 2048], BF16)
    cstage = scr[:, 0:4, :].rearrange("p a b -> p (a b)").rearrange("p (a b) -> p a b", b=256)

    ps = st.enter_context(nc.psum_tensor("ps", [128, 8 * 512], F32))
    bank = lambda i: ps[:, i * 512:(i + 1) * 512]
    Sbanks = Rot([0, 1]); Abanks = Rot([2, 3]); Gbanks = Rot([4, 5, 6, 7])
    wrot = Rot(range(NWP))

    def MM(out, lhsT, rhs, start, stop, R, W):
        S.op("pe", lambda e: e.matmul(out, lhsT, rhs, start=start, stop=stop), R, W)

    def TR(out, in_, ident, R, W):
        S.op("pe", lambda e: e.transpose(out, in_, ident), R, W)

    def ACT(out, in_, func, R, W, bias=None, scale=None, accum=None):
        kw = {}
        if bias is not None:
            kw["bias"] = bias
        if scale is not None:
            kw["scale"] = scale
        if accum is not None:
            kw["accum_out"] = accum
        S.op("act", lambda e: e.activation(out, in_, func, **kw), R, W)

    def TT(eng, out, in0, in1, op, R, W):
        S.op(eng, lambda e: e.tensor_tensor(out, in0, in1, op), R, W)

    def STT(eng, out, in0, scalar, in1, op0, op1, R, W):
        S.op(eng, lambda e: e.scalar_tensor_tensor(out, in0, scalar, in1, op0, op1), R, W)

    def TS(eng, out, in0, s1, s2, op0, op1, R, W):
        S.op(eng, lambda e: e.tensor_scalar(out, in0, s1, 0.0, op0, ALU.add), R, W)

    def CP(eng, out, in_, R, W):
        if eng == "act":
            S.op("act", lambda e: e.copy(out, in_), R, W)
        else:
            S.op(eng, lambda e: e.tensor_copy(out, in_), R, W)

    def RCP(out, in_, R, W):
        S.op("dve", lambda e: e.reciprocal(out, in_), R, W)

    def MSET(eng, ap, val, W):
        S.op(eng, lambda e: e.memset(ap, val), (), W)

    def DMA(q, out, in_, key, R, W, final=False):
        if type(out.tensor).__name__ == "DRamTensorHandle":
            q = "pool"
        S.dma(q, lambda e: e.dma_start(out=out, in_=in_), key, R, W, final=final)

    cp_rot = Rot(["act", "dve"])

    WREG = {}
    wscr_off = [0]
    wscr = dint("wscr", [48 * 1024 * 1024], BF16)

    def wreg(gkey, idx, src_ap):
        shp = list(src_ap.shape)
        n = 1
        for d in shp:
            n *= d
        off = wscr_off[0]
        wscr_off[0] += n
        flat = wscr[off:off + n]
        if len(shp) == 3:
            dst = flat.rearrange("(p a b) -> p a b", a=shp[1], b=shp[2])
            step = max(1, MAXDESC // shp[0])
            for a0 in range(0, shp[1], step):
                a1 = min(shp[1], a0 + step)
                DMA("pool", dst[:, a0:a1, :], src_ap[:, a0:a1, :], "cv_" + gkey, [], ["W_" + gkey])
        else:
            dst = flat.rearrange("(p a) -> p a", a=shp[1])
            DMA("pool", dst, src_ap, "cv_" + gkey, [], ["W_" + gkey])
        WREG[(gkey, idx)] = (flat.rearrange("(p a) -> p a", p=shp[0]), shp)

    def wpiece(gkey, idx):
        src2, shp = WREG[(gkey, idx)]
        i = wrot.next()
        n = src2.shape[1]
        assert n <= 3072, n
        dst = wp[0:shp[0], i, 0:n]
        DMA("sp", dst, src2, "wp%d" % i, ["W_" + gkey], ["wp%d" % i])
        if len(shp) == 3:
            dst = dst.rearrange("p (a b) -> p a b", b=shp[2])
        return dst, "wp%d" % i

    w_ab = w_in_ab.rearrange("(k p) n -> p k n", p=128)
    w_c = w_in_c.rearrange("(k p) n -> p k n", p=128)
    w_qbv = w_qb.rearrange("(k p) n -> p k n", p=128)

    def reg_ffn(l, i):
        wg = wg_d[l, i].rearrange("(k p) f -> p k f", p=128)
        wu = wu_d[l, i].rearrange("(k p) f -> p k f", p=128)
        wd = wd_d[l, i].rearrange("(f p) d -> p f d", p=128)
        for pc in range(11):
            wreg("wg%d%d" % (l, i), pc, wg[:, :, pc * 256:(pc + 1) * 256])
        for pc in range(11):
            wreg("wu%d%d" % (l, i), pc, wu[:, :, pc * 256:(pc + 1) * 256])
        for dc in range(8):
            wreg("wd%d%d" % (l, i), dc, wd[:, :, dc * 128:(dc + 1) * 128])

    def reg_all():
        reg_ffn(0, 0)
        for p0 in range(0, 2304, 256):
            wreg("wab", p0 // 256, w_ab[:, :, p0:p0 + 256])
        wv = w_out_ab.rearrange("(h p) d -> p h d", p=64)
        for dc in range(8):
            wreg("woab", dc, wv[:, :, dc * 128:(dc + 1) * 128])
        reg_ffn(0, 1)
        reg_ffn(1, 0)
        wreg("wc", 0, w_c[:, :, 0:384])
        wreg("wc", 1, w_c[:, :, 384:672])
        for cb in range(3):
            wreg("wqb", cb, w_qbv[:, :, cb * 512:(cb + 1) * 512])
        wv = w_out_c.rearrange("(h p) d -> p h d", p=64)
        for dc in range(8):
            wreg("woc", dc, wv[:, :, dc * 128:(dc + 1) * 128])
        reg_ffn(1, 1)

    reg_all()

    DMA("sp", identf[:], identf_d, "identf", [], ["identf"])
    DMA("sp", identb[:], identb_d, "identb", [], ["identb"])
    DMA("sp", gTs[:], gT, "gTs", [], ["gTs"])
    DMA("sp", bT[:], b_adaT, "bT", [], ["bT"])
    DMA("sp", scT[:], cT, "scT", [], ["scT"])
    DMA("sp", hv[:], hv_d, "hv", [], ["hv"])
    DMA("sp", gq[:], gq_d, "gq", [], ["gq"])
    DMA("sp", gkv[:], gkv_d, "gkv", [], ["gkv"])
    DMA("sp", sinkexp[:], sinks_d, "sinkexp", [], ["sinkexp"])
    MSET("dve", onesb[:], 1.0, ["onesb"])
    MSET("dve", onesf[:], 1.0, ["onesf"])
    MSET("dve", zcol[:], 0.0, ["zcol"])
    MSET("dve", VA[:, :, :, :, 64:65], 1.0, ["VAones"])
    MSET("dve", VB[:, :, :, :, 64:65], 1.0, ["VBones"])
    ACT(sinkexp[:], sinkexp[:], AF.Exp, ["sinkexp"], ["sinkexp"])
    ACT(scT[:], scT[:], AF.Silu, ["scT"], ["scT"])
    for hh in range(4):
        DMA("sp", qh[96:105, hh, :], qaug_d, "qhaug", [], ["qhaug"])
    DMA("pool", wkvb[:], w_kvb.rearrange("(c p) n -> p c n", p=128), "wkvb", [], ["wkvb"])

    for l in range(2):
        for m in range(9):
            for half in range(2):
                if (m * 2 + half) % 2 == 0:
                    wst = act[:, 0:16 * TOK].bitcast(F32).rearrange("p (k n) -> p k n", n=512)
                    wkeys = ["act%d" % c for c in range(16)]; wsk = "wstA"
                else:
                    wst = qo[:, :, :].rearrange("p a b -> p (a b)").bitcast(F32).rearrange("p (k n) -> p k n", n=512)
                    wkeys = ["qo%d" % c for c in range(16)]; wsk = "wstB"
                src = w_ada[l].rearrange("(k p) n -> p k n", p=128)[:, :, m * 1024 + half * 512: m * 1024 + half * 512 + 512]
                DMA("sp", wst, src, wsk, [], wkeys)
                bk = Gbanks.next()
                for oc in range(4):
                    for k in range(8):
                        MM(bank(bk)[:, oc * 4:oc * 4 + 3], wst[:, k, oc * 128:(oc + 1) * 128], scT[:, k, :], k == 0, k == 7,
                           wkeys + ["scT"], ["ps%d" % bk])
                for oc in range(4):
                    ch = half * 4 + oc
                    TS("dve", M[:, l, m, ch, :], bank(bk)[:, oc * 4:oc * 4 + 3], bT[:, l, m * 8 + ch:m * 8 + ch + 1], None, ALU.add, None,
                       ["ps%d" % bk, "bT"], ["M"])
    for l in range(2):
        for n in range(3):
            for si in range(3):
                STT("dve", M[:, l, 3 * n + 1, :, si], M[:, l, 3 * n + 1, :, si], 1.0, gTs[:, 3 * l + n, :], ALU.add, ALU.mult, ["M", "gTs"], ["M"])
        for m in (2, 8):
            TS("dve", M[:, l, m, :, :], M[:, l, m, :, :], 0.5, None, ALU.mult, None, ["M"], ["M"])

    def norm_mod(ncol, scale_of, shift_of, out_of, out_keys):
        ACT(sq[:, :, 0:ncol], xT[:, :, 0:ncol], AF.Square, ["xT"], ["act%d" % (16 + c) for c in range(8)])
        bk = Gbanks.next()
        for c in range(8):
            MM(bank(bk)[:, 0:ncol], onesb[:], sq[:, c, 0:ncol], c == 0, c == 7, ["onesb", "act%d" % (16 + c)], ["ps%d" % bk])
        ACT(scr[:, 0, 0:ncol], bank(bk)[:, 0:ncol], AF.Sqrt, ["ps%d" % bk], ["scr0"], bias=epscol[:], scale=1.0 / D)
        RCP(scr[:, 1, 0:ncol], scr[:, 0, 0:ncol], ["scr0"], ["scr1"])
        for c in range(8):
            tb_ = 2 + (c % 2)
            TT("dve", scr[:, tb_, 0:ncol], xT[:, c, 0:ncol], scr[:, 1, 0:ncol], ALU.mult, ["xT", "scr1"], ["scr%d" % tb_])
            kw = dict(scale=scale_of(c))
            sh = shift_of(c)
            ACT(out_of(c), scr[:, tb_, 0:ncol], AF.Identity, ["scr%d" % tb_, "M", "gTs"], out_keys(c), bias=(sh if sh is not None else zcol[:]), **kw)

    def ffn(l, i, ncol, si, gate_m):
        for pc in range(11):
            g_ap, gk = wpiece("wg%d%d" % (l, i), pc)
            u_ap, uk = wpiece("wu%d%d" % (l, i), pc)
            for sub in range(2):
                fc = pc * 2 + sub
                bg = Gbanks.next(); bu = Gbanks.next()
                for k in range(8):
                    MM(bank(bg)[:, 0:ncol], g_ap[:, k, sub * 128:(sub + 1) * 128], hT[:, k, 0:ncol], k == 0, k == 7, [gk, "hT"], ["ps%d" % bg])
                for k in range(8):
                    MM(bank(bu)[:, 0:ncol], u_ap[:, k, sub * 128:(sub + 1) * 128], hT[:, k, 0:ncol], k == 0, k == 7, [uk, "hT"], ["ps%d" % bu])
                sgi = 4 + (fc % 2)
                ACT(scr[:, sgi, 0:ncol], bank(bg)[:, 0:ncol], AF.Silu, ["ps%d" % bg], ["scr%d" % sgi])
                TT("dve", actc(fc)[:, 0:ncol], bank(bu)[:, 0:ncol], scr[:, sgi, 0:ncol], ALU.mult, ["ps%d" % bu, "scr%d" % sgi], ["act%d" % fc])
        for dc in range(8):
            d_ap, dk = wpiece("wd%d%d" % (l, i), dc)
            by = Gbanks.next()
            for fc in range(NFC):
                MM(bank(by)[:, 0:ncol], d_ap[:, fc, :], actc(fc)[:, 0:ncol], fc == 0, fc == NFC - 1, [dk, "act%d" % fc], ["ps%d" % by])
            STT("dve", xT[:, dc, 0:ncol], bank(by)[:, 0:ncol], M[:, l, gate_m, dc, si:si + 1], xT[:, dc, 0:ncol], ALU.mult, ALU.add,
                ["ps%d" % by, "M", "xT"], ["xT"])

    def mod_norm(l, n, ncol, si):
        norm_mod(ncol,
                 lambda c: M[:, l, 3 * n + 1, c, si:si + 1],
                 lambda c: M[:, l, 3 * n + 0, c, si:si + 1],
                 lambda c: hT[:, c, 0:ncol],
                 lambda c: ["hT"])

    def attn_block(kT_ap, q_ap, nk, ncols, scale, bias_ap, pbias_ap, v_ap, acc_ap, first, R, accW, last=False):
        bs = Sbanks.next()
        MM(bank(bs)[0:nk, 0:ncols], kT_ap, q_ap, True, True, R, ["ps%d" % bs])
        pi = PTrot.next()
        if bias_ap is not None:
            si_ = Sbrot.next()
            STT("dve", Sb[0:nk, si_, 0:ncols], bank(bs)[0:nk, 0:ncols], scale, bias_ap, ALU.mult, ALU.add, ["ps%d" % bs] + R, ["Sb%d" % si_])
            ACT(PT[0:nk, pi, 0:ncols], Sb[0:nk, si_, 0:ncols], AF.Exp, ["Sb%d" % si_, "hv"], ["PT%d" % pi], bias=pbias_ap)
        else:
            ACT(PT[0:nk, pi, 0:ncols], bank(bs)[0:nk, 0:ncols], AF.Exp, ["ps%d" % bs], ["PT%d" % pi], scale=scale)
        MM(acc_ap, v_ap, PT[0:nk, pi, 0:ncols], first, last, ["PT%d" % pi] + R, accW)

    def attn_finish(ba, ncol, dest_ap, destW, sink_ap=None):
        if sink_ap is not None:
            TS("dve", den[64:65, 0:ncol], bank(ba)[64:65, 0:ncol], sink_ap, None, ALU.add, None, ["ps%d" % ba, "sinkexp"], ["den"])
        else:
            CP("dve", den[64:65, 0:ncol], bank(ba)[64:65, 0:ncol], ["ps%d" % ba], ["den"])
        bb = Gbanks.next()
        MM(bank(bb)[0:64, 0:ncol], onesf[64:65, 0:64], den[64:65, 0:ncol], True, True, ["onesf", "den"], ["ps%d" % bb])
        RCP(rb[:, 0:ncol], bank(bb)[0:64, 0:ncol], ["ps%d" % bb], ["rb"])
        TT("dve", dest_ap, bank(ba)[0:64, 0:ncol], rb[:, 0:ncol], ALU.mult, ["ps%d" % ba, "rb"], destW)

    CS_ALL = ["scr0", "scr1", "scr2", "scr3"]
    PTrot = Rot(range(3)); Sbrot = Rot(range(2)); farot = Rot(range(2)); fbrot = Rot(range(2)); stgrot = Rot(range(2))
    epscol = sb("epscol", [128, 1], F32)
    MSET("dve", epscol[:], EPS, ["epscol"])

    def l0_mixer(kind, ncol, si, slot, tblocks, seg, is_last, samp_idx):
        do_q = kind != "halo"
        plan = []
        if do_q:
            plan += [("q", h, 0 + 64 * h) for h in range(8)]
            plan += [("q", 8 + h, 1536 + 64 * h) for h in range(8)]
        plan += [("ka", h, 512 + 64 * h) for h in range(8)]
        plan += [("kb", h, 2048 + 64 * h) for h in range(2)]
        cur_piece = None
        for (typ, h, col) in plan:
            p0 = (col // 256) * 256
            if cur_piece is None or cur_piece[0] != p0:
                ap_, k_ = wpiece("wab", p0 // 256)
                cur_piece = (p0, ap_, k_)
            _, w_ap, wk = cur_piece
            bk = Gbanks.next()
            for k in range(8):
                MM(bank(bk)[0:64, 0:ncol], w_ap[:, k, col - p0:col - p0 + 64], hT[:, k, 0:ncol], k == 0, k == 7, [wk, "hT"], ["ps%d" % bk])
            if typ == "q":
                dst, W = qo[0:64, h, 0:ncol], ["qo%d" % h]
            elif typ == "ka":
                dst, W = kTA[:, slot, h, 0:ncol], ["kTA%d" % slot]
            else:
                dst, W = kTB[:, slot, h, 0:ncol], ["kTB%d" % slot]
            CP(cp_rot.next(), dst, bank(bk)[0:64, 0:ncol], ["ps%d" % bk], W)
        if SUB <= 4.2:
            return
        want_out = (is_last or kind == "samp") and SUB != 4.41
        for (tbi, (t0, nt)) in enumerate(tblocks):
            jobs = [("va", 1024, 512)]
            jobs += [("vb", 2176, 128)]
            if want_out:
                jobs += [("ka_o", 512, 512), ("kb_o", 2048, 128)]
            for (typ, col, wdt) in jobs:
                bk = Gbanks.next()
                for half in range(wdt // 256 if wdt >= 256 else 1):
                    cw = min(256, wdt)
                    p0 = ((col + half * 256) // 256) * 256
                    w_ap, wk = wpiece("wab", p0 // 256)
                    off = col + half * 256 - p0
                    for k in range(8):
                        MM(bank(bk)[0:nt, half * 256:half * 256 + cw], hT[:, k, t0:t0 + nt], w_ap[:, k, off:off + cw], k == 0, k == 7,
                           [wk, "hT"], ["ps%d" % bk])
                if typ == "va":
                    CP("act", VA[0:nt, slot, tbi, :, 0:64], bank(bk)[0:nt, 0:512].rearrange("p (h d) -> p h d", d=64), ["ps%d" % bk], ["VA%d" % slot])
                elif typ == "vb":
                    CP("dve", VB[0:nt, slot, tbi, :, 0:64], bank(bk)[0:nt, 0:128].rearrange("p (h d) -> p h d", d=64), ["ps%d" % bk], ["VB%d" % slot])
                if want_out:
                    sgi = stgrot.next()
                    CP("act" if typ in ("va", "ka_o") else "dve", stg[0:nt, sgi, 0:wdt], bank(bk)[0:nt, 0:wdt], ["ps%d" % bk], ["stg%d" % sgi])
                    if SUB == 4.42:
                        continue
                    if kind == "samp":
                        r0 = samp_idx * 32
                        dst = {"va": o_avs, "vb": o_bvs, "ka_o": o_aks, "kb_o": o_bks}[typ][r0:r0 + 32, :]
                        DMA(OUTQ, dst, stg[0:nt, sgi, 0:wdt], "stg%d" % sgi, ["stg%d" % sgi], [], final=True)
                    else:
                        if typ in ("va", "ka_o"):
                            dst = (o_av if typ == "va" else o_ak)[t0:t0 + nt, :]
                            DMA(OUTQ, dst, stg[0:nt, sgi, 0:wdt], "stg%d" % sgi, ["stg%d" % sgi], [], final=True)
                        elif tbi == 3:
                            dst = (o_bv if typ == "vb" else o_bk)[:, :]
                            DMA(OUTQ, dst, stg[0:nt, sgi, 0:wdt], "stg%d" % sgi, ["stg%d" % sgi], [], final=True)
        if not do_q or SUB <= 4.42:
            return
        if kind == "prompt":
            prev = 1 - slot
            for h in range(8):
                fi = farot.next()
                DMA("sp", fa[:, fi, :], FA_d[h], "fa%d" % fi, [], ["fa%d" % fi])
                ba = Abanks.next()
                first = True
                for m in (3, 4, 0, 1, 2, 5, 6, 7):
                    q0 = max(0, 128 * m - 512); q1 = min(512, 128 * m + 128)
                    sl = prev if m < 4 else slot
                    blk = m % 4
                    pb = hv[:, seg:seg + 1] if (m < 4) else zcol[:]
                    if m < 4 and not first_own[0]:
                        pb = zcol[:]
                    attn_block(kTA[:, sl, h, blk * 128:(blk + 1) * 128], qo[0:64, h, q0:q1], 128, q1 - q0, 0.125,
                               fa[:, fi, q0 - 128 * m + 512:q1 - 128 * m + 512], pb, VA[:, sl, blk, h, 0:65], bank(ba)[0:65, q0:q1], first,
                               ["kTA%d" % sl, "qo%d" % h, "fa%d" % fi, "VA%d" % sl, "VAones"], ["ps%d" % ba], last=(m == 7))
                    first = False
                attn_finish(ba, 512, qo[0:64, h, :], ["qo%d" % h])
            for h in range(8):
                fi = fbrot.next()
                DMA("sp", fb[:, fi, :], FB_d[h], "fb%d" % fi, [], ["fb%d" % fi])
                ba = Abanks.next()
                g = h // 4
                first = True
                for m in (4, 6, 3, 5, 7):
                    q0 = max(0, 128 * m - 512); q1 = min(512, 128 * m - 256)
                    sl = prev if m < 4 else slot
                    blk = m % 4
                    pb = hv[:, seg:seg + 1] if (m < 4 and first_own[0]) else zcol[:]
                    attn_block(kTB[:, sl, g, blk * 128:(blk + 1) * 128], qo[0:64, 8 + h, q0:q1], 128, q1 - q0, 0.125,
                               fb[:, fi, q0 - 128 * m + 512:q1 - 128 * m + 512], pb, VB[:, sl, blk, g, 0:65], bank(ba)[0:65, q0:q1], first,
                               ["kTB%d" % sl, "qo%d" % (8 + h), "fb%d" % fi, "VB%d" % sl, "VBones"], ["ps%d" % ba], last=(m == 7))
                    first = False
                attn_finish(ba, 512, qo[0:64, 8 + h, :], ["qo%d" % (8 + h)], sink_ap=sinkexp[64:65, h:h + 1])
        else:
            s_ = samp_idx
            csl = 1 - slot
            ck = cstage[:, 0:8, :].rearrange("p a b -> p (a b)")[:, 0:2048].rearrange("p (b n) -> p b n", n=512)
            DMA("sp", ck, cak[s_].rearrange("(b p) n -> p b n", p=128), "cstage", [], CS_ALL)
            for blk in range(4):
                for h in range(8):
                    bk = Gbanks.next()
                    TR(bank(bk)[0:64, 0:128], ck[:, blk, h * 64:(h + 1) * 64], identf[:], CS_ALL + ["identf"], ["ps%d" % bk])
                    CP(cp_rot.next(), kTA[:, csl, h, blk * 128:(blk + 1) * 128], bank(bk)[0:64, 0:128], ["ps%d" % bk], ["kTA%d" % csl])
            for blk in range(4):
                for hh in range(2):
                    DMA("pool", VA[:, csl, blk, hh * 4:(hh + 1) * 4, 0:64],
                        cav[s_, blk * 128:(blk + 1) * 128, hh * 256:(hh + 1) * 256].rearrange("p (h d) -> p h d", d=64), "VA%d" % csl, [], ["VA%d" % csl])
            for h in range(8):
                fi = farot.next()
                DMA("sp", fa[:, fi, 0:160].rearrange("p (b q) -> p b q", q=32), FAs_d[h], "fa%d" % fi, [], ["fa%d" % fi])
                fav = fa[:, fi, 0:160].rearrange("p (b q) -> p b q", q=32)
                ba = Abanks.next()
                for blk in range(5):
                    if blk < 4:
                        kT_ap = kTA[:, csl, h, blk * 128:(blk + 1) * 128]; v_ap = VA[:, csl, blk, h, 0:65]; nk = 128
                        R = ["kTA%d" % csl, "VA%d" % csl]
                    else:
                        kT_ap = kTA[:, slot, h, 0:32]; v_ap = VA[0:32, slot, 0, h, 0:65]; nk = 32
                        R = ["kTA%d" % slot, "VA%d" % slot]
                    attn_block(kT_ap, qo[0:64, h, 0:32], nk, 32, 0.125, fav[0:nk, blk, :], zcol[0:nk, :], v_ap, bank(ba)[0:65, 0:32], blk == 0,
                               R + ["qo%d" % h, "fa%d" % fi, "VAones"], ["ps%d" % ba], last=(blk == 4))
                attn_finish(ba, 32, qo[0:64, h, 0:32], ["qo%d" % h])
            if SUB <= 4.6:
                return
            ckb = cstage[:, 0, 0:128]
            DMA("sp", ckb, cbk[s_], "cstage", [], CS_ALL)
            for g in range(2):
                bk = Gbanks.next()
                TR(bank(bk)[0:64, 0:128], ckb[:, g * 64:(g + 1) * 64], identf[:], CS_ALL + ["identf"], ["ps%d" % bk])
                CP(cp_rot.next(), kTB[:, csl, g, 0:128], bank(bk)[0:64, 0:128], ["ps%d" % bk], ["kTB%d" % csl])
            DMA("pool", VB[:, csl, 0, :, 0:64], cbv[s_].rearrange("p (h d) -> p h d", d=64), "VB%d" % csl, [], ["VB%d" % csl])
            for h in range(8):
                fi = fbrot.next()
                DMA("sp", fb[:, fi, 0:64].rearrange("p (b q) -> p b q", q=32), FBs_d[h], "fb%d" % fi, [], ["fb%d" % fi])
                fbv = fb[:, fi, 0:64].rearrange("p (b q) -> p b q", q=32)
                ba = Abanks.next()
                g = h // 4
                for blk in range(2):
                    if blk == 0:
                        kT_ap = kTB[:, csl, g, 0:128]; v_ap = VB[:, csl, 0, g, 0:65]; nk = 128; R = ["kTB%d" % csl, "VB%d" % csl]
                    else:
                        kT_ap = kTB[:, slot, g, 0:32]; v_ap = VB[0:32, slot, 0, g, 0:65]; nk = 32; R = ["kTB%d" % slot, "VB%d" % slot]
                    attn_block(kT_ap, qo[0:64, 8 + h, 0:32], nk, 32, 0.125, fbv[0:nk, blk, :], zcol[0:nk, :], v_ap, bank(ba)[0:65, 0:32], blk == 0,
                               R + ["qo%d" % (8 + h), "fb%d" % fi, "VBones"], ["ps%d" % ba], last=(blk == 1))
                attn_finish(ba, 32, qo[0:64, 8 + h, 0:32], ["qo%d" % (8 + h)], sink_ap=sinkexp[64:65, h:h + 1])

    def out_proj(w_dram, ncol, l, si, gate_m):
        for dc in range(8):
            w_ap, wk = wpiece(w_dram, dc)
            by = Gbanks.next()
            for h in range(16):
                MM(bank(by)[:, 0:ncol], w_ap[:, h, :], qo[0:64, h, 0:ncol], h == 0, h == 15, [wk, "qo%d" % h], ["ps%d" % by])
            STT("dve", xT[:, dc, 0:ncol], bank(by)[:, 0:ncol], M[:, l, gate_m, dc, si:si + 1], xT[:, dc, 0:ncol], ALU.mult, ALU.add,
                ["ps%d" % by, "M", "xT"], ["xT"])

    def l1_prep(kind, ncol, tblocks, tile_idx, samp_idx):
        for (tbi, (t0, nt)) in enumerate(tblocks):
            wq_ap, wqk = wpiece("wc", 0)
            bq = Gbanks.next()
            for k in range(8):
                MM(bank(bq)[0:nt, 0:384], hT[:, k, t0:t0 + nt], wq_ap[:, k, :], k == 0, k == 7, [wqk, "hT"], ["ps%d" % bq])
            wk_ap, wkk = wpiece("wc", 1)
            bk2 = Gbanks.next()
            for k in range(8):
                MM(bank(bk2)[0:nt, 0:288], hT[:, k, t0:t0 + nt], wk_ap[:, k, :], k == 0, k == 7, [wkk, "hT"], ["ps%d" % bk2])
            ACT(scr[0:nt, 2, 0:384], bank(bq)[0:nt, 0:384], AF.Square,
                ["ps%d" % bq], ["scr2", "small"], accum=small[0:nt, 0:1])
            ACT(small[0:nt, 1:2], small[0:nt, 0:1], AF.Sqrt, ["small"], ["small"], bias=epscol[0:nt, :], scale=1.0 / 384)
            RCP(small[0:nt, 2:3], small[0:nt, 1:2], ["small"], ["small"])
            STT("dve", qn_bf[0:nt, :], bank(bq)[0:nt, 0:384], small[0:nt, 2:3], gq[0:nt, :], ALU.mult, ALU.mult, ["ps%d" % bq, "small", "gq"], ["qn_bf"])
            ACT(scr[0:nt, 3, 0:256], bank(bk2)[0:nt, 0:256], AF.Square, ["ps%d" % bk2], ["scr3", "small"], accum=small[0:nt, 3:4])
            ACT(small[0:nt, 4:5], small[0:nt, 3:4], AF.Sqrt, ["small"], ["small"], bias=epscol[0:nt, :], scale=1.0 / 256)
            RCP(small[0:nt, 5:6], small[0:nt, 4:5], ["small"], ["small"])
            STT("dve", kvn_f[0:nt, :], bank(bk2)[0:nt, 0:256], small[0:nt, 5:6], gkv[0:nt, :], ALU.mult, ALU.mult, ["ps%d" % bk2, "small", "gkv"], ["kvn_f"])
            CP("act", kvn_b[0:nt, 0:256], kvn_f[0:nt, :], ["kvn_f"], ["kvn_b"])
            if kind == "samp":
                DMA("sp", cs[0:nt, 0, :], cosS, "cs", [], ["cs"])
                DMA("sp", cs[0:nt, 1, :], sinS, "cs", [], ["cs"])
            else:
                DMA("sp", cs[:, 0, :], cosP[tile_idx * 4 + tbi], "cs", [], ["cs"])
                DMA("sp", cs[:, 1, :], sinP[tile_idx * 4 + tbi], "cs", [], ["cs"])
            x1 = bank(bk2)[0:nt, 256:272]; x2 = bank(bk2)[0:nt, 272:288]
            c16 = cs[0:nt, 0, 0:16]; s16 = cs[0:nt, 1, 0:16]
            TT("dve", rtmp[0:nt, 0, 0:16], x1, c16, ALU.mult, ["ps%d" % bk2, "cs"], ["rtmp0"])
            TT("dve", rtmp[0:nt, 1, 0:16], x2, s16, ALU.mult, ["ps%d" % bk2, "cs"], ["rtmp1"])
            TT("dve", kr_f[0:nt, 0:16], rtmp[0:nt, 0, 0:16], rtmp[0:nt, 1, 0:16], ALU.subtract, ["rtmp0", "rtmp1"], ["kr_f"])
            TT("dve", rtmp[0:nt, 2, 0:16], x1, s16, ALU.mult, ["ps%d" % bk2, "cs"], ["rtmp2"])
            TT("dve", rtmp[0:nt, 3, 0:16], x2, c16, ALU.mult, ["ps%d" % bk2, "cs"], ["rtmp3"])
            TT("dve", kr_f[0:nt, 16:32], rtmp[0:nt, 2, 0:16], rtmp[0:nt, 3, 0:16], ALU.add, ["rtmp2", "rtmp3"], ["kr_f"])
            CP("act", kvn_b[0:nt, 256:288], kr_f[0:nt, :], ["kr_f"], ["kvn_b"])
            if kind == "samp":
                r0 = samp_idx * 32
                DMA("sp", o_ckvs[r0:r0 + 32, :], kvn_f[0:nt, :], "kvn_f", ["kvn_f"], [], final=True)
                DMA("sp", o_ckrs[r0:r0 + 32, :], kr_f[0:nt, :], "kr_f", ["kr_f"], [], final=True)
            else:
                r0 = tile_idx * TOK + t0
                DMA("sp", o_ckv[r0:r0 + nt, :], kvn_f[0:nt, :], "kvn_f", ["kvn_f"], [], final=True)
                DMA("sp", o_ckr[r0:r0 + nt, :], kr_f[0:nt, :], "kr_f", ["kr_f"], [], final=True)
            bt = Gbanks.next()
            btv = bank(bt).bitcast(BF16)
            for c in range(3):
                w_ = 128 if c < 2 else 32
                TR(btv[0:w_, c * 128:c * 128 + nt], kvn_b[0:nt, c * 128:c * 128 + w_], identb[0:nt, 0:nt], ["kvn_b", "identb"], ["ps%d" % bt])
            CP("dve", gT_sb[:, 0:2, t0:t0 + nt], btv[:, 0:256].rearrange("p (c t) -> p c t", t=128)[:, :, 0:nt], ["ps%d" % bt], ["gT_sb"])
            CP("dve", gT_sb[0:32, 2, t0:t0 + nt], btv[0:32, 256:256 + nt], ["ps%d" % bt], ["gT_sb"])
            bt2 = Gbanks.next()
            bt2v = bank(bt2).bitcast(BF16)
            for c in range(3):
                TR(bt2v[:, c * 128:c * 128 + nt], qn_bf[0:nt, c * 128:(c + 1) * 128], identb[0:nt, 0:nt], ["qn_bf", "identb"], ["ps%d" % bt2])
            CP("act", qnT[:, :, t0:t0 + nt], bt2v[:, 0:384].rearrange("p (c t) -> p c t", t=128)[:, :, 0:nt], ["ps%d" % bt2], ["qnT"])
            for cb in range(3):
                wq2, wq2k = wpiece("wqb", cb)
                bqq = Gbanks.next()
                for k in range(3):
                    MM(bank(bqq)[0:nt, 0:512], qnT[:, k, t0:t0 + nt], wq2[:, k, :], k == 0, k == 2, [wq2k, "qnT"], ["ps%d" % bqq])
                CP(cp_rot.next(), qraw[0:nt, cb * 512:(cb + 1) * 512],
                   bank(bqq)[0:nt, 0:512], ["ps%d" % bqq], ["stg0", "stg1"])
            qv = qraw[0:nt, :].rearrange("p (h d) -> p h d", d=96)
            CP("act", q_bf[0:nt, :, 0:64], qv[:, :, 0:64], ["stg0", "stg1"], ["q_bf"])
            cosv = cs[0:nt, 0, :].rearrange("p (h d) -> p h d", d=16); sinv = cs[0:nt, 1, :].rearrange("p (h d) -> p h d", d=16)
            r3 = lambda i: rtmp[0:nt, i, :].rearrange("p (h d) -> p h d", d=16)
            TT("dve", r3(0), qv[:, :, 64:80], cosv, ALU.mult, ["stg0", "stg1", "cs"], ["rtmp0"])
            TT("dve", r3(1), qv[:, :, 80:96], sinv, ALU.mult, ["stg0", "stg1", "cs"], ["rtmp1"])
            TT("dve", q_bf[0:nt, :, 64:80], r3(0), r3(1), ALU.subtract, ["rtmp0", "rtmp1"], ["q_bf"])
            TT("dve", r3(2), qv[:, :, 64:80], sinv, ALU.mult, ["stg0", "stg1", "cs"], ["rtmp2"])
            TT("dve", r3(3), qv[:, :, 80:96], cosv, ALU.mult, ["stg0", "stg1", "cs"], ["rtmp3"])
            TT("dve", q_bf[0:nt, :, 80:96], r3(2), r3(3), ALU.add, ["rtmp2", "rtmp3"], ["q_bf"])
            for hg in range(2):
                bt3 = Gbanks.next()
                bt3v = bank(bt3).bitcast(BF16)
                for hh in range(8):
                    h = hg * 8 + hh
                    TR(bt3v[0:96, hh * 128:hh * 128 + nt], q_bf[0:nt, h, :], identb[0:nt, 0:nt], ["q_bf", "identb"], ["ps%d" % bt3])
                CP(cp_rot.next(), qo[0:96, hg * 8:(hg + 1) * 8, t0:t0 + nt], bt3v[0:96, :].rearrange("p (h t) -> p h t", t=128)[:, :, 0:nt],
                   ["ps%d" % bt3], ["qo%d" % h_ for h_ in range(hg * 8, hg * 8 + 8)])

    qraw = stg[:, :, :].rearrange("p a b -> p (a b)")[:, 0:1536]
    first_own = [False]

    def phase1_tile(kind, ncol, si, slot, seg, a, tile_idx, samp_idx):
        tblocks = [(i * 128, 128) for i in range(4)] if kind != "samp" else [(0, 32)]
        if kind == "samp":
            DMA("sp", xin[0:32, 0, :], xs[samp_idx], "xin", [], ["act%d" % c for c in range(4)])
        else:
            r0 = (a + 1) * TOK if kind == "prompt" else 0
            DMA("sp", xin[:, :, :], xp[seg, r0:r0 + TOK, :].rearrange("(b p) d -> p b d", p=128), "xin", [], ["act%d" % c for c in range(16)])
        for (tbi, (t0, nt)) in enumerate(tblocks):
            for c2 in range(2):
                bk = Gbanks.next()
                for cc in range(4):
                    c = c2 * 4 + cc
                    TR(bank(bk)[:, cc * 128:cc * 128 + nt], xin[0:nt, tbi, c * 128:(c + 1) * 128], identf[0:nt, 0:nt],
                       ["act%d" % c_ for c_ in range(16)] + ["identf"], ["ps%d" % bk])
                CP(cp_rot.next(), xT[:, c2 * 4:(c2 + 1) * 4, t0:t0 + nt], bank(bk).rearrange("p (c t) -> p c t", t=128)[:, :, 0:nt], ["ps%d" % bk], ["xT"])
        if SUB <= 1:
            return
        mod_norm(0, 0, ncol, si)
        if SUB <= 2:
            return
        ffn(0, 0, ncol, si, 2)
        if SUB <= 3:
            return
        mod_norm(0, 1, ncol, si)
        if SUB <= 4:
            return
        l0_mixer(kind, ncol, si, slot, tblocks, seg, kind == "prompt" and seg == 1 and a == ST - 1, samp_idx)
        if kind == "halo" or SUB <= 5:
            return
        out_proj("woab", ncol, 0, si, 5)
        mod_norm(0, 2, ncol, si)
        ffn(0, 1, ncol, si, 8)
        if SUB <= 6:
            return
        mod_norm(1, 0, ncol, si)
        ffn(1, 0, ncol, si, 2)
        mod_norm(1, 1, ncol, si)
        if SUB <= 7:
            return
        l1_prep(kind, ncol, tblocks, tile_idx, samp_idx)
        if SUB <= 8:
            return
        sp_i = tile_idx if kind == "prompt" else NQT + samp_idx
        DMA("sp", x_sp[sp_i][:, :, 0:ncol], xT[:, :, 0:ncol], "xT", ["xT"], ["x_sp%d" % sp_i])
        DMA("sp", q_sp[sp_i][:, :, 0:ncol], qo[0:96, :, 0:ncol], "qo", ["qo%d" % h for h in range(16)], ["q_sp%d" % sp_i])
        if kind == "prompt":
            gi_ = g_in[tile_idx // 2]; c0_ = (tile_idx % 2) * TOK
            DMA("sp", gi_[0:256, c0_:c0_ + TOK].rearrange("(c p) t -> p c t", p=128), gT_sb[:, 0:2, :], "gT_sb", ["gT_sb"], ["g_in%d" % (tile_idx // 2)])
            DMA("sp", gi_[256:288, c0_:c0_ + TOK], gT_sb[0:32, 2, :], "gT_sb", ["gT_sb"], ["g_in%d" % (tile_idx // 2)])
        else:
            DMA("sp", g_s[samp_idx, 0:256, 1024:1056].rearrange("(c p) t -> p c t", p=128), gT_sb[:, 0:2, 0:32], "gT_sb", ["gT_sb"], ["g_s%d" % samp_idx])
            DMA("sp", g_s[samp_idx, 256:288, 1024:1056], gT_sb[0:32, 2, 0:32], "gT_sb", ["gT_sb"], ["g_s%d" % samp_idx])

    def sample_cache_latents(s_):
        for half in range(2):
            DMA("sp", cstage[:, 0:4, :], cckv[s_, half * 512:(half + 1) * 512, :].rearrange("(b p) n -> p b n", p=128), "cstage", [], CS_ALL)
            for c in range(2):
                bk = Gbanks.next()
                for blk in range(4):
                    TR(bank(bk)[:, blk * 128:(blk + 1) * 128], cstage[:, blk, c * 128:(c + 1) * 128], identf[:], CS_ALL + ["identf"], ["ps%d" % bk])
                CP(cp_rot.next(), gT_sb[:, c, :], bank(bk), ["ps%d" % bk], ["gT_sb"])
            DMA("sp", g_s[s_, 0:256, half * 512:(half + 1) * 512].rearrange("(c p) t -> p c t", p=128), gT_sb[:, 0:2, :], "gT_sb", ["gT_sb"], ["g_s%d" % s_])
        for half in range(2):
            crv = cstage[:, 4, :].rearrange("p (b n) -> p b n", n=32)[:, 0:4, :]
            DMA("sp", crv, cckr[s_, half * 512:(half + 1) * 512, :].rearrange("(b p) n -> p b n", p=128), "cstage4", [], ["scr2"])
            bk = Gbanks.next()
            for blk in range(4):
                TR(bank(bk)[0:32, blk * 128:(blk + 1) * 128], crv[:, blk, :], identf[:], ["scr2", "identf"], ["ps%d" % bk])
            CP(cp_rot.next(), gT_sb[0:32, 2, :], bank(bk)[0:32, :], ["ps%d" % bk], ["gT_sb"])
            DMA("sp", g_s[s_, 256:288, half * 512:(half + 1) * 512], gT_sb[0:32, 2, :], "gT_sb", ["gT_sb"], ["g_s%d" % s_])

    KTrot = Rot(range(4)); kvTrot = Rot(range(2))

    def phase2_tile(kind, ncol, si, sp_i, key_tiles, out_ap_of):
        DMA("sp", xT[:, :, 0:ncol], x_sp[sp_i][:, :, 0:ncol], "xT", ["x_sp%d" % sp_i], ["xT"])
        SB = [0, 1, 4, 5]
        for hp in range(8):
            par = hp % 2
            qkey = "qh%d" % par
            DMA("sp", qh[0:96, 2 * par:2 * par + 2, 0:ncol], q_sp[sp_i][:, hp * 2:hp * 2 + 2, 0:ncol], qkey, ["q_sp%d" % sp_i], [qkey])
            bas = [2, 3]
            nv = len(key_tiles)
            units = [(vi, hh) for vi in range(nv) for hh in range(2)]
            blocks = [(ui, blk) for ui in range(len(units)) for blk in range(4)]
            nb = len(blocks)
            kv_of = {}
            kt_of = {}

            def stageA(ui):
                vi, hh = units[ui]
                lat2, krs, kaug_ap, gkeys = key_tiles[vi]
                h = hp * 2 + hh
                if hh == 0:
                    kvi = kvTrot.next()
                    kv_of[vi] = kvi
                    DMA("sp", kvT[:, kvi, :, :], lat2, "kvT%d" % kvi, gkeys, ["kvT%d" % kvi])
                kvi = kv_of[vi]
                kti = KTrot.next()
                DMA("sp", KT[64:96, kti, :], krs, "KTr%d" % kti, gkeys, ["KTr%d" % kti])
                DMA("sp", KT[96:105, kti, :], kaug_ap, "KTa%d" % kti, [], ["KTa%d" % kti])
                for c in range(2):
                    MM(bank(6)[0:64, :], wkvb[:, c, h * 128:h * 128 + 64], kvT[:, kvi, c, :], c == 0, c == 1, ["wkvb", "kvT%d" % kvi], ["ps6"])
                CP("dve", KT[0:64, kti, :], bank(6)[0:64, :], ["ps6"], ["KTn%d" % kti])
                for blk in range(4):
                    for c in range(2):
                        MM(bank(7)[:, blk * 64:(blk + 1) * 64], kvT[:, kvi, c, blk * 128:(blk + 1) * 128], wkvb[:, c, h * 128 + 64:h * 128 + 128],
                           c == 0, c == 1, ["wkvb", "kvT%d" % kvi], ["ps7"])
                CP("dve", Vh[:, kti, :, 0:64], bank(7)[:, 0:256].rearrange("p (b d) -> p b d", d=64), ["ps7"], ["Vh%d" % kti])
                kt_of[ui] = kti

            nq = len(units) * 2

            def Sp(q):
                ui = q // 2
                vi, hh = units[ui]
                kti = kt_of[ui]
                b0 = 0 if q % 2 == 0 else 4
                for t in range(2):
                    blk = 2 * (q % 2) + t
                    MM(bank(b0 + t)[:, 0:ncol], KT[0:105, kti, blk * 128:(blk + 1) * 128], qh[0:105, 2 * par + hh, 0:ncol], True, True,
                       ["KTn%d" % kti, "KTr%d" % kti, "KTa%d" % kti, qkey, "qhaug"], ["ps%d" % (b0 + t)])

            def E2(q):
                b0 = 0 if q % 2 == 0 else 4
                p0 = 2 * (q % 2)
                src = ps[:, b0 * 512:(b0 + 2) * 512].rearrange("p (t n) -> p t n", n=512)[:, :, 0:ncol]
                ACT(PT[:, p0:p0 + 2, 0:ncol], src, AF.Exp, ["ps%d" % b0, "ps%d" % (b0 + 1)], ["PT%d" % p0, "PT%d" % (p0 + 1)], scale=MLA_SCALE)

            def Pp(q):
                ui = q // 2
                vi, hh = units[ui]
                kti = kt_of[ui]
                p0 = 2 * (q % 2)
                for t in range(2):
                    blk = 2 * (q % 2) + t
                    MM(bank(bas[hh])[0:65, 0:ncol], Vh[:, kti, blk, 0:65], PT[:, p0 + t, 0:ncol], vi == 0 and blk == 0, vi == nv - 1 and blk == 3,
                       ["PT%d" % (p0 + t), "Vh%d" % kti, "Vhones"], ["ps%d" % bas[hh]])

            stageA(0)
            Sp(0)
            for q in range(nq):
                ui = q // 2
                if q % 2 == 0 and ui + 1 < len(units):
                    stageA(ui + 1)
                if q + 1 < nq:
                    Sp(q + 1)
                E2(q)
                Pp(q)
            for hh in range(2):
                h = hp * 2 + hh
                attn_finish_p2(bas[hh], ncol, qo[0:64, h, 0:ncol], ["qo%d" % h])
        out_proj("woc", ncol, 1, si, 5)
        mod_norm(1, 2, ncol, si)
        ffn(1, 1, ncol, si, 8)
        norm_mod(ncol, lambda c: gTs[:, 6, c:c + 1], lambda c: None, lambda c: yfin[:, c, 0:ncol], lambda c: ["act%d" % (2 * c), "act%d" % (2 * c + 1)])
        nblk = (ncol + 127) // 128
        for tb in range(nblk):
            nt = min(128, ncol - tb * 128)
            sgi = stgrot.next()
            for c2 in range(2):
                bk = Gbanks.next()
                for cc in range(4):
                    c = c2 * 4 + cc
                    TR(bank(bk)[0:nt, cc * 128:(cc + 1) * 128], yfin[:, c, tb * 128:tb * 128 + nt], identf[:], ["act%d" % (2 * c), "act%d" % (2 * c + 1), "identf"], ["ps%d" % bk])
                CP(cp_rot.next(), stg[0:nt, sgi, c2 * 512:(c2 + 1) * 512], bank(bk)[0:nt, :], ["ps%d" % bk], ["stg%d" % sgi])
            DMA("sp", out_ap_of(tb, nt), stg[0:nt, sgi, :], "stg%d" % sgi, ["stg%d" % sgi], [], final=True)

    def attn_finish_p2(ba, ncol, dest_ap, destW):
        CP("dve", den[64:65, 0:ncol], bank(ba)[64:65, 0:ncol], ["ps%d" % ba], ["den"])
        bb = Sbanks.next()
        MM(bank(bb)[0:64, 0:ncol], onesf[64:65, 0:64], den[64:65, 0:ncol], True, True, ["onesf", "den"], ["ps%d" % bb])
        RCP(rb[:, 0:ncol], bank(bb)[0:64, 0:ncol], ["ps%d" % bb], ["rb"])
        TT("dve", dest_ap, bank(ba)[0:64, 0:ncol], rb[:, 0:ncol], ALU.mult, ["ps%d" % ba, "rb"], destW)

    zb = PT[:, 0, 0:480]
    MSET("dve", zb, 0.0, ["PT0"])
    for s_ in range(2):
        DMA("sp", g_s[s_, 0:256, 1056:1536].rearrange("(c p) t -> p c t", p=128)[:, 0, :], zb, "zb%d" % s_, ["PT0"], ["g_s%d" % s_])
        DMA("sp", g_s[s_, 0:256, 1056:1536].rearrange("(c p) t -> p c t", p=128)[:, 1, :], zb, "zb%d" % s_, ["PT0"], ["g_s%d" % s_])
        DMA("sp", g_s[s_, 256:288, 1056:1536], zb[0:32, :], "zb%d" % s_, ["PT0"], ["g_s%d" % s_])
    for s_ in range(2 if 's1' in STAGES else 0):
        phase1_tile("samp", 32, 1 + s_, 0, 0, 0, 0, s_)
        if SUB >= 10:
            sample_cache_latents(s_)
    tile_idx = 0
    for seg in range(NSEG if 'p1' in STAGES else 0):
        for a in range(-1, ST):
            slot = (a + 1) % 2
            if a < 0:
                phase1_tile("halo", TOK, 0, slot, seg, a, -1, 0)
            else:
                first_own[0] = (a == 0)
                phase1_tile("prompt", TOK, 0, slot, seg, a, tile_idx, 0)
                tile_idx += 1
    if 'ag' in STAGES:
        for gc in range(NGC):
            S.op("pool", lambda e, gc=gc: e.collective_compute("AllGather", ALU.bypass, replica_groups=[[0, 1, 2, 3], [4, 5, 6, 7]],
                                                               ins=[g_in[gc].opt()], outs=[g_all[gc].opt()]), ["g_in%d" % gc], ["g_all%d" % gc])
    MSET("dve", Vh[:, :, :, 64:65], 1.0, ["VA0", "VA1", "VAones", "Vhones"])
    for s_ in range(2 if 's2' in STAGES else 0):
        kts = []
        for v in range(3):
            lat2 = g_s[s_, 0:256, v * TOK:(v + 1) * TOK].rearrange("(c p) t -> p c t", p=128)
            kts.append((lat2, g_s[s_, 256:288, v * TOK:(v + 1) * TOK], kaugs_d[v], ["g_s%d" % s_]))
        phase2_tile("samp", 32, 1 + s_, NQT + s_, kts, lambda tb, nt, s_=s_: y_s[s_ * 32:s_ * 32 + 32, :])
    for qi in range(NQT if 'p2' in STAGES else 0):
        kts = []
        for v in range(nv_of(qi)):
            jj, off = gloc(v)
            lt_ = off // TOK
            ga_ = g_all[lt_ // 2]; c0_ = (lt_ % 2) * TOK
            lat2 = ga_[jj * 288:jj * 288 + 256, c0_:c0_ + TOK].rearrange("(c p) t -> p c t", p=128)
            kts.append((lat2, ga_[jj * 288 + 256:jj * 288 + 288, c0_:c0_ + TOK], kaug_d[qi, v], ["g_all%d" % (lt_ // 2)]))
        phase2_tile("prompt", TOK, 0, qi, kts, lambda tb, nt, qi=qi: y_p[qi * TOK + tb * 128:qi * TOK + tb * 128 + nt, :])

    S.emit(st)
    st.close()
    return nc


_NC_CACHE = {}
STOP = 9
STAGES = {'s1', 'p1', 'ag', 's2', 'p2'}
OUTQ = 'sp'
SUB = 99


def _cpu():
    return jax.default_device(jax.devices("cpu")[0])


def _t5_bucket(rel):
    nb = 16
    rel = jnp.asarray(rel)
    ret = jnp.where(rel > 0, nb, 0)
    n = jnp.abs(rel)
    max_exact = nb // 2
    large = max_exact + (jnp.log(jnp.maximum(n, 1).astype(jnp.float32) / max_exact)
                         / np.log(128 / max_exact) * (nb - max_exact)).astype(jnp.int32)
    large = jnp.minimum(large, nb - 1)
    return np.asarray(ret + jnp.where(n < max_exact, n, large))


def _tables(rel_bias_a, t5_bias):
    tab = np.asarray(rel_bias_a[0], np.float32)
    t5 = np.asarray(t5_bias, np.float32)
    kk = np.arange(128)[:, None]
    col = np.arange(640)[None, :]
    u = col - 512
    rel = kk - col
    d = (kk >= 64).astype(np.int64) - np.floor_divide(u, 64)
    idx = np.clip(rel, -128, 128) + 128
    FA = np.where(((d >= 0) & (d <= 8))[None], tab[idx].transpose(2, 0, 1), np.float32(NEG)).astype(np.float32)
    col = np.arange(256)[None, :]
    u = col - 512
    rel = kk - col
    d = (kk >= 64).astype(np.int64) - np.floor_divide(u, 64)
    FB = np.where(((d >= 6) & (d <= 8))[None], t5[_t5_bucket(rel)].transpose(2, 0, 1), np.float32(NEG)).astype(np.float32)
    q = np.arange(32)[None, None, :]
    blk = np.arange(5)[None, :, None]
    kk3 = np.arange(128)[:, None, None]
    rel = 128 * blk + kk3 - 512 - q
    FAs = tab[np.clip(rel, -128, 128) + 128].transpose(3, 0, 1, 2).astype(np.float32)
    blk = np.arange(2)[None, :, None]
    rel = 128 * blk + kk3 - 128 - q
    FBs = t5[_t5_bucket(rel)].transpose(3, 0, 1, 2).astype(np.float32)
    return FA, FB, FAs, FBs


def _rope_tab(pos):
    half = 16
    inv = 10000.0 ** (-jnp.arange(half, dtype=jnp.float32) / half)
    ang = jnp.asarray(pos).astype(jnp.float32)[:, None] * inv[None, :]
    return np.asarray(jnp.cos(ang)), np.asarray(jnp.sin(ang))


def kernel(x_prompt, x_sample, c_prompt, c_sample, cache_a_k, cache_a_v, cache_b_k, cache_b_v,
           cache_c_kv, cache_c_kr, w_ada, b_ada, norm_g, final_norm_g, ffn_w_gate, ffn_w_up, ffn_w_down,
           w_in_ab, w_out_ab, rel_bias_a, t5_bias, sinks_b, w_in_c, c_q_norm_g, c_kv_norm_g, w_qb, w_kvb,
           w_out_c):
    if "nc" not in _NC_CACHE:
        _NC_CACHE["nc"] = build_program()
    nc = _NC_CACHE["nc"]
    in_maps = make_in_maps(x_prompt, x_sample, c_prompt, c_sample, cache_a_k, cache_a_v, cache_b_k, cache_b_v,
                           cache_c_kv, cache_c_kr, w_ada, b_ada, norm_g, final_norm_g, ffn_w_gate, ffn_w_up, ffn_w_down,
                           w_in_ab, w_out_ab, rel_bias_a, t5_bias, sinks_b, w_in_c, c_q_norm_g, c_kv_norm_g, w_qb, w_kvb,
                           w_out_c)
    res = run_bass_kernel_spmd(nc, in_maps, core_ids=list(range(8)))
    return assemble(res.results)


def make_in_maps(x_prompt, x_sample, c_prompt, c_sample, cache_a_k, cache_a_v, cache_b_k, cache_b_v,
                 cache_c_kv, cache_c_kr, w_ada, b_ada, norm_g, final_norm_g, ffn_w_gate, ffn_w_up, ffn_w_down,
                 w_in_ab, w_out_ab, rel_bias_a, t5_bias, sinks_b, w_in_c, c_q_norm_g, c_kv_norm_g, w_qb, w_kvb,
                 w_out_c):
    f = lambda a: np.ascontiguousarray(np.asarray(a, np.float32))
    x_prompt = f(x_prompt); x_sample = f(x_sample)
    with _cpu():
        FA, FB, FAs, FBs = _tables(f(rel_bias_a), f(t5_bias))
    b_adaT = f(b_ada).reshape(2, 72, 128).transpose(2, 0, 1)
    gT = np.concatenate([f(norm_g).reshape(6, 8, 128), f(final_norm_g).reshape(1, 8, 128)], 0).transpose(2, 0, 1)
    qaug = np.zeros((9, TOK), np.float32)
    qc = np.arange(TOK) // 64
    for r in range(8):
        qaug[r] = -(qc < r).astype(np.float32)
    qaug[8] = -1.0
    kaugs = np.zeros((3, 9, TOK), np.float32)
    kidx = np.arange(3 * TOK).reshape(3, TOK)
    kaugs[:, 8, :] = np.where(kidx >= 1056, BIG, 0.0)
    with _cpu():
        cS, sS = _rope_tab(1024 + np.arange(32))
    shared = dict(
        w_ada=f(w_ada), b_adaT=f(b_adaT), gT=f(gT), ffn_w_gate=f(ffn_w_gate), ffn_w_up=f(ffn_w_up), ffn_w_down=f(ffn_w_down),
        w_in_ab=f(w_in_ab)[0], w_out_ab=f(w_out_ab)[0], w_in_c=f(w_in_c)[0], w_qb=f(w_qb)[0], w_kvb=f(w_kvb)[0], w_out_c=f(w_out_c)[0],
        gq=f(np.broadcast_to(f(c_q_norm_g)[0][None], (128, 384))), gkv=f(np.broadcast_to(f(c_kv_norm_g)[0][None], (128, 256))),
        FA=FA, FB=FB, FAs=FAs, FBs=FBs, sinks=f(np.broadcast_to(f(sinks_b)[0][None], (128, 8))),
        cosS=f(np.tile(cS, (1, 16))), sinS=f(np.tile(sS, (1, 16))),
        kaugs=kaugs.astype(ml_dtypes.bfloat16), qaug=qaug.astype(ml_dtypes.bfloat16),
        identf=np.eye(128, dtype=np.float32), identb=np.eye(128, dtype=np.float32).astype(ml_dtypes.bfloat16),
    )
    in_maps = []
    for c in range(8):
        b, j = divmod(c, 4)
        xp = np.zeros((NSEG, (ST + 1) * TOK, D), np.float32)
        hvv = np.zeros((128, 4), np.float32)
        pos = np.zeros(NQT * TOK, np.int64)
        for k in range(NSEG):
            s0 = seg_of(j, k) * ST * TOK
            if s0 == 0:
                xp[k, TOK:] = x_prompt[b, 0:ST * TOK]
                hvv[:, k] = NEG
            else:
                xp[k] = x_prompt[b, s0 - TOK:s0 + ST * TOK]
            pos[k * ST * TOK:(k + 1) * ST * TOK] = s0 + np.arange(ST * TOK)
        with _cpu():
            cP, sP = _rope_tab(pos)
        cosP = np.tile(cP.reshape(NQT * 4, 128, 1, 16), (1, 1, 16, 1)).reshape(NQT * 4, 128, 256)
        sinP = np.tile(sP.reshape(NQT * 4, 128, 1, 16), (1, 1, 16, 1)).reshape(NQT * 4, 128, 256)
        cvec = np.stack([f(c_prompt)[b], f(c_sample)[2 * c], f(c_sample)[2 * c + 1]], -1)
        kaug = np.zeros((NQT, 32, 9, TOK), np.float32)
        kc = np.arange(TOK) // 64
        for qi in range(NQT):
            k, a = divmod(qi, ST)
            T = seg_of(j, k) * ST + a
            for v in range(nv_of(qi)):
                if v == T:
                    for r in range(8):
                        kaug[qi, v, r] = (kc == r) * BIG
                elif v > T:
                    kaug[qi, v, 8] = BIG
        m = dict(shared)
        m.update(
            xp=xp, xs=f(x_sample[2 * c:2 * c + 2]), cT=f(cvec.reshape(8, 128, 3).transpose(1, 0, 2)),
            cak=f(cache_a_k)[0, 2 * c:2 * c + 2].reshape(2, 512, 512), cav=f(cache_a_v)[0, 2 * c:2 * c + 2].reshape(2, 512, 512),
            cbk=f(cache_b_k)[0, 2 * c:2 * c + 2].reshape(2, 128, 128), cbv=f(cache_b_v)[0, 2 * c:2 * c + 2].reshape(2, 128, 128),
            cckv=f(cache_c_kv)[0, 2 * c:2 * c + 2], cckr=f(cache_c_kr)[0, 2 * c:2 * c + 2],
            hv=hvv, cosP=f(cosP), sinP=f(sinP), kaug=kaug.astype(ml_dtypes.bfloat16),
        )
        in_maps.append({k_: np.ascontiguousarray(v_) for k_, v_ in m.items()})
    return in_maps


def assemble(R):
    y_prompt = np.zeros((2, 16384, D), np.float32)
    ckv_p = np.zeros((1, 2, 16384, 256), np.float32)
    ckr_p = np.zeros((1, 2, 16384, 32), np.float32)
    for c in range(8):
        b, j = divmod(c, 4)
        for k in range(NSEG):
            s0 = seg_of(j, k) * ST * TOK
            sl = slice(k * ST * TOK, (k + 1) * ST * TOK)
            y_prompt[b, s0:s0 + ST * TOK] = R[c]["y_p"][sl]
            ckv_p[0, b, s0:s0 + ST * TOK] = R[c]["o_ckv"][sl]
            ckr_p[0, b, s0:s0 + ST * TOK] = R[c]["o_ckr"][sl]
    cat = lambda name: np.concatenate([R[c][name] for c in range(8)], 0)
    y_sample = cat("y_s").reshape(16, 32, D)
    last = [0, 4]
    a_k_p = np.stack([R[c]["o_ak"] for c in last], 0).reshape(1, 2, 512, 8, 64)
    a_v_p = np.stack([R[c]["o_av"] for c in last], 0).reshape(1, 2, 512, 8, 64)
    b_k_p = np.stack([R[c]["o_bk"] for c in last], 0).reshape(1, 2, 128, 2, 64)
    b_v_p = np.stack([R[c]["o_bv"] for c in last], 0).reshape(1, 2, 128, 2, 64)
    return (y_prompt, y_sample, a_k_p, a_v_p, b_k_p, b_v_p, ckv_p, ckr_p,
            cat("o_aks").reshape(1, 16, 32, 8, 64), cat("o_avs").reshape(1, 16, 32, 8, 64),
            cat("o_bks").reshape(1, 16, 32, 2, 64), cat("o_bvs").reshape(1, 16, 32, 2, 64),
            cat("o_ckvs").reshape(1, 16, 32, 256), cat("o_ckrs").reshape(1, 16, 32, 32))
```

```python
import numpy as np
import concourse.bass as bass
import concourse.mybir as mybir

F32 = mybir.dt.float32
BF16 = mybir.dt.bfloat16
AF = mybir.ActivationFunctionType
ALU = mybir.AluOpType

EPOCH = 16000
DMA_EPOCH = 1000


class Sched:
    ENGS = ("pe", "act", "dve", "pool", "sp")

    def __init__(self, nc, same_engine_sync=("act", "dve", "pool")):
        self.nc = nc
        self.ops = []
        self.lastw = {}
        self.lastr = {}
        self.same = set(same_engine_sync)
        self.dma_count = {}

    def _deps(self, reads, writes):
        deps = set()
        for k in reads:
            for d in self.lastw.get(k, {}).values():
                deps.add(d)
        for k in writes:
            for d in self.lastw.get(k, {}).values():
                deps.add(d)
            for d in self.lastr.get(k, {}).values():
                deps.add(d)
        return deps

    def _record(self, idx, agent, reads, writes):
        for k in reads:
            self.lastr.setdefault(k, {})[agent] = idx
        for k in writes:
            self.lastw.setdefault(k, {})[agent] = idx

    def op(self, eng, fn, reads=(), writes=()):
        idx = len(self.ops)
        deps = self._deps(reads, writes)
        self.ops.append(dict(kind="c", eng=eng, fn=fn, deps=deps))
        self._record(idx, eng, reads, writes)
        return idx

    def dma(self, queue, fn, key, reads=(), writes=(), final=False):
        idx = len(self.ops)
        key = key + "_" + queue
        deps = self._deps(reads, writes)
        n = self.dma_count.get(key, 0) + 1
        self.dma_count[key] = n
        self.ops.append(dict(kind="d", eng=queue, fn=fn, deps=deps, key=key, n=n, final=final))
        self._record(idx, "dma:" + key, reads, writes)
        return idx

    def emit(self, stack):
        nc = self.nc
        ops = self.ops
        needed = set()
        for i, o in enumerate(ops):
            for d in o["deps"]:
                od = ops[d]
                if od["kind"] == "c":
                    if od["eng"] == o["eng"] and od["eng"] not in self.same:
                        continue
                    needed.add(d)
        cnt = {e: 0 for e in self.ENGS}
        for i, o in enumerate(ops):
            if o["kind"] == "c" and i in needed:
                cnt[o["eng"]] += 1
                o["ms"] = cnt[o["eng"]]
        sems = {}

        def sem(name):
            if name not in sems:
                sems[name] = stack.enter_context(nc.semaphore(name))
            return sems[name]

        def target(d):
            od = ops[d]
            if od["kind"] == "c":
                m = od["ms"]
                return ("c_%s_%d" % (od["eng"], (m - 1) // EPOCH), (m - 1) % EPOCH + 1)
            n = od["n"]
            return ("d_%s_%d" % (od["key"], (n - 1) // DMA_EPOCH), ((n - 1) % DMA_EPOCH + 1) * 16)

        per_eng = {e: [] for e in self.ENGS}
        for i, o in enumerate(ops):
            per_eng[o["eng"]].append(i)
        for i, o in enumerate(ops):
            if o["kind"] == "c":
                if "ms" in o:
                    sem(target(i)[0])
            else:
                sem(target(i)[0])
        self.n_sems = len(sems)

        def run(engname, eng):
            waited = {}
            for i in per_eng[engname]:
                o = ops[i]
                wl = {}
                for d in o["deps"]:
                    od = ops[d]
                    if od["kind"] == "c" and od["eng"] == engname and engname not in self.same:
                        continue
                    s, v = target(d)
                    if waited.get(s, 0) >= v:
                        continue
                    wl[s] = max(wl.get(s, 0), v)
                for s, v in wl.items():
                    eng.wait_ge(sem(s), v)
                    waited[s] = v
                ins = o["fn"](eng)
                if o["kind"] == "c":
                    if "ms" in o:
                        ins.then_inc(sem(target(i)[0]), 1)
                else:
                    ins.then_inc(sem(target(i)[0]), 16)
            for i in per_eng[engname]:
                o = ops[i]
                if o["kind"] == "d" and o.get("final"):
                    s, v = target(i)
                    if waited.get(s, 0) < v:
                        eng.wait_ge(sem(s), v)
                        waited[s] = v

        with nc.Block() as block:
            @block.sync
            def _(e):
                run("sp", e)

            @block.scalar
            def _(e):
                run("act", e)

            @block.vector
            def _(e):
                run("dve", e)

            @block.gpsimd
            def _(e):
                run("pool", e)

            @block.tensor
            def _(e):
                run("pe", e)

from contextlib import ExitStack
import ml_dtypes
import jax
import jax.numpy as jnp
from concourse.bass_utils import run_bass_kernel_spmd

D = 1024
FF = 2816
NFC = 22
TOK = 512
ST = 4
NSEG = 2
EPS = 1e-6
BIG = 16384.0
MAXDESC = 512
NEG = -1e30
MLA_SCALE = 96 ** -0.5
NQT = NSEG * ST


def seg_of(j, k):
    return j if k == 0 else 7 - j


def nv_of(qi):
    k, a = divmod(qi, ST)
    return (ST * (3 if k == 0 else 7)) + a + 1


def gloc(v):
    s, a = divmod(v, ST)
    jj = s if s < 4 else 7 - s
    kk = 0 if s < 4 else 1
    return jj, (kk * ST + a) * TOK


class Rot:
    def __init__(self, items):
        self.items = list(items)
        self.i = 0

    def next(self):
        r = self.items[self.i % len(self.items)]
        self.i += 1
        return r


def build_program():
    nc = bass.Bass("TRN2", target_bir_lowering=False)
    st = ExitStack()
    S = Sched(nc)

    def din(name, shape, dt=F32):
        return nc.dram_tensor(name, list(shape), dt, kind="ExternalInput").ap()

    def dout(name, shape, dt=F32):
        return nc.dram_tensor(name, list(shape), dt, kind="ExternalOutput").ap()

    def dint(name, shape, dt):
        return nc.dram_tensor(name, list(shape), dt).ap()

    def sb(name, shape, dt):
        return st.enter_context(nc.sbuf_tensor(name, list(shape), dt))

    xp = din("xp", [NSEG, (ST + 1) * TOK, D])
    xs = din("xs", [2, 32, D])
    cT = din("cT", [128, 8, 3])
    cak = din("cak", [2, 512, 512]); cav = din("cav", [2, 512, 512])
    cbk = din("cbk", [2, 128, 128]); cbv = din("cbv", [2, 128, 128])
    cckv = din("cckv", [2, 1024, 256]); cckr = din("cckr", [2, 1024, 32])
    w_ada = din("w_ada", [2, D, 9 * D]); b_adaT = din("b_adaT", [128, 2, 72])
    gT = din("gT", [128, 7, 8])
    wg_d = din("ffn_w_gate", [2, 2, D, FF]); wu_d = din("ffn_w_up", [2, 2, D, FF]); wd_d = din("ffn_w_down", [2, 2, FF, D])
    w_in_ab = din("w_in_ab", [D, 2304]); w_out_ab = din("w_out_ab", [D, D])
    w_in_c = din("w_in_c", [D, 672]); w_qb = din("w_qb", [384, 1536]); w_kvb = din("w_kvb", [256, 2048]); w_out_c = din("w_out_c", [D, D])
    gq_d = din("gq", [128, 384]); gkv_d = din("gkv", [128, 256])
    FA_d = din("FA", [8, 128, 640]); FB_d = din("FB", [8, 128, 256])
    FAs_d = din("FAs", [8, 128, 5, 32]); FBs_d = din("FBs", [8, 128, 2, 32])
    sinks_d = din("sinks", [128, 8]); hv_d = din("hv", [128, 4])
    cosP = din("cosP", [NQT * 4, 128, 256]); sinP = din("sinP", [NQT * 4, 128, 256])
    cosS = din("cosS", [32, 256]); sinS = din("sinS", [32, 256])
    kaug_d = din("kaug", [NQT, 32, 9, TOK], BF16); kaugs_d = din("kaugs", [3, 9, TOK], BF16)
    qaug_d = din("qaug", [9, TOK], BF16)
    identf_d = din("identf", [128, 128]); identb_d = din("identb", [128, 128], BF16)

    y_p = dout("y_p", [NQT * TOK, D]); y_s = dout("y_s", [64, D])
    o_ak = dout("o_ak", [512, 512]); o_av = dout("o_av", [512, 512]); o_bk = dout("o_bk", [128, 128]); o_bv = dout("o_bv", [128, 128])
    o_ckv = dout("o_ckv", [NQT * TOK, 256]); o_ckr = dout("o_ckr", [NQT * TOK, 32])
    o_aks = dout("o_aks", [64, 512]); o_avs = dout("o_avs", [64, 512]); o_bks = dout("o_bks", [64, 128]); o_bvs = dout("o_bvs", [64, 128])
    o_ckvs = dout("o_ckvs", [64, 256]); o_ckrs = dout("o_ckrs", [64, 32])

    x_sp = dint("x_sp", [NQT + 2, 128, 8, TOK], F32)
    q_sp = dint("q_sp", [NQT + 2, 96, 16, TOK], BF16)
    NGC = NQT // 2
    g_in = [dint("g_in%d" % i, [288, 2 * TOK], BF16) for i in range(NGC)]
    g_all = [dint("g_all%d" % i, [4 * 288, 2 * TOK], BF16) for i in range(NGC)]
    g_s = dint("g_s", [2, 288, 3 * TOK], BF16)

    xT = sb("xT", [128, 8, TOK], F32)
    hT = sb("hT", [128, 8, TOK], BF16)
    act = sb("act", [128, 24 * TOK], BF16)
    actc = lambda c: act[:, c * TOK:(c + 1) * TOK]
    xin = act[:, 0:16 * TOK].bitcast(F32).rearrange("p (b d) -> p b d", d=D)
    sq = act[:, 16 * TOK:24 * TOK].rearrange("p (c t) -> p c t", t=TOK)
    yfin = act[:, 0:16 * TOK].bitcast(F32).rearrange("p (c t) -> p c t", t=TOK)
    scr = sb("scr", [128, 6, TOK], F32)
    qo = sb("qo", [128, 16, TOK], BF16)
    kTA = sb("kTA", [64, 2, 8, TOK], BF16)
    kTB = sb("kTB", [64, 2, 2, TOK], BF16)
    VA = sb("VA", [128, 2, 4, 8, 80], BF16)
    VB = sb("VB", [128, 2, 4, 2, 80], BF16)
    Sb = sb("Sb", [128, 2, TOK], F32)
    PT = sb("PT", [128, 3, TOK], BF16)
    fa = sb("fa", [128, 2, 640], F32)
    fb = sb("fb", [128, 2, 256], F32)
    NWP = 5
    wp = sb("wp", [128, NWP, 3072], BF16)
    M = sb("M", [128, 2, 9, 8, 3], F32)
    gTs = sb("gTs", [128, 7, 8], F32)
    scT = sb("scT", [128, 8, 3], F32)
    bT = sb("bT", [128, 2, 72], F32)
    identf = sb("identf_s", [128, 128], F32)
    identb = sb("identb_s", [128, 128], BF16)
    onesb = sb("onesb", [128, 128], BF16)
    onesf = sb("onesf", [128, 64], F32)
    zcol = sb("zcol", [128, 1], F32)
    sinkexp = sb("sinkexp", [128, 8], F32)
    hv = sb("hv_s", [128, 4], F32)
    gq = sb("gq_s", [128, 384], F32)
    gkv = sb("gkv_s", [128, 256], F32)
    den = sb("den", [128, TOK], F32)
    rb = sb("rb", [64, TOK], F32)
    small = sb("small", [128, 8], F32)
    qn_bf = sb("qn_bf", [128, 384], BF16)
    kvn_f = sb("kvn_f", [128, 256], F32)
    kvn_b = sb("kvn_b", [128, 288], BF16)
    kr_f = sb("kr_f", [128, 32], F32)
    q_bf = sb("q_bf", [128, 16, 96], BF16)
    qnT = sb("qnT", [128, 3, TOK], BF16)
    cs = sb("cs", [128, 2, 256], F32)
    rtmp = sb("rtmp", [128, 4, 256], F32)
    stg = sb("stg", [128, 2, 1024], F32)
    gT_sb = sb("gT_sb", [128, 3, TOK], BF16)
    VAflat = VA[:, :, :, :, :].rearrange("p a b c d -> p (a b c d)")
    KT = VAflat[0:105, 0:4 * TOK].rearrange("p (i t) -> p i t", t=TOK)
    Vh = VAflat[:, 4 * TOK:4 * TOK + 1280].rearrange("p (i b d) -> p i b d", b=4, d=80)
    qh = sb("qh", [105, 4, TOK], BF16)
    kvT = sb("kvT", [128, 2, 2, TOK], BF16)
    wkvb = sb("wkvb", [128, 2, 2048], BF16)
    cstage = scr[:, 0:4, :].rearrange("p a b -> p (a b)").rearrange("p (a b) -> p a b", b=256)

    ps = st.enter_context(nc.psum_tensor("ps", [128, 8 * 512], F32))
    bank = lambda i: ps[:, i * 512:(i + 1) * 512]
    Sbanks = Rot([0, 1]); Abanks = Rot([2, 3]); Gbanks = Rot([4, 5, 6, 7])
    wrot = Rot(range(NWP))

    def MM(out, lhsT, rhs, start, stop, R, W):
        S.op("pe", lambda e: e.matmul(out, lhsT, rhs, start=start, stop=stop), R, W)

    def TR(out, in_, ident, R, W):
        S.op("pe", lambda e: e.transpose(out, in_, ident), R, W)

    def ACT(out, in_, func, R, W, bias=None, scale=None, accum=None):
        kw = {}
        if bias is not None:
            kw["bias"] = bias
        if scale is not None:
            kw["scale"] = scale
        if accum is not None:
            kw["accum_out"] = accum
        S.op("act", lambda e: e.activation(out, in_, func, **kw), R, W)

    def TT(eng, out, in0, in1, op, R, W):
        S.op(eng, lambda e: e.tensor_tensor(out, in0, in1, op), R, W)

    def STT(eng, out, in0, scalar, in1, op0, op1, R, W):
        S.op(eng, lambda e: e.scalar_tensor_tensor(out, in0, scalar, in1, op0, op1), R, W)

    def TS(eng, out, in0, s1, s2, op0, op1, R, W):
        S.op(eng, lambda e: e.tensor_scalar(out, in0, s1, 0.0, op0, ALU.add), R, W)

    def CP(eng, out, in_, R, W):
        if eng == "act":
            S.op("act", lambda e: e.copy(out, in_), R, W)
        else:
            S.op(eng, lambda e: e.tensor_copy(out, in_), R, W)

    def RCP(out, in_, R, W):
        S.op("dve", lambda e: e.reciprocal(out, in_), R, W)

    def MSET(eng, ap, val, W):
        S.op(eng, lambda e: e.memset(ap, val), (), W)

    def DMA(q, out, in_, key, R, W, final=False):
        if type(out.tensor).__name__ == "DRamTensorHandle":
            q = "pool"
        S.dma(q, lambda e: e.dma_start(out=out, in_=in_), key, R, W, final=final)

    cp_rot = Rot(["act", "dve"])

    WREG = {}
    wscr_off = [0]
    wscr = dint("wscr", [48 * 1024 * 1024], BF16)

    def wreg(gkey, idx, src_ap):
        shp = list(src_ap.shape)
        n = 1
        for d in shp:
            n *= d
        off = wscr_off[0]
        wscr_off[0] += n
        flat = wscr[off:off + n]
        if len(shp) == 3:
            dst = flat.rearrange("(p a b) -> p a b", a=shp[1], b=shp[2])
            step = max(1, MAXDESC // shp[0])
            for a0 in range(0, shp[1], step):
                a1 = min(shp[1], a0 + step)
                DMA("pool", dst[:, a0:a1, :], src_ap[:, a0:a1, :], "cv_" + gkey, [], ["W_" + gkey])
        else:
            dst = flat.rearrange("(p a) -> p a", a=shp[1])
            DMA("pool", dst, src_ap, "cv_" + gkey, [], ["W_" + gkey])
        WREG[(gkey, idx)] = (flat.rearrange("(p a) -> p a", p=shp[0]), shp)

    def wpiece(gkey, idx):
        src2, shp = WREG[(gkey, idx)]
        i = wrot.next()
        n = src2.shape[1]
        assert n <= 3072, n
        dst = wp[0:shp[0], i, 0:n]
        DMA("sp", dst, src2, "wp%d" % i, ["W_" + gkey], ["wp%d" % i])
        if len(shp) == 3:
            dst = dst.rearrange("p (a b) -> p a b", b=shp[2])
        return dst, "wp%d" % i

    w_ab = w_in_ab.rearrange("(k p) n -> p k n", p=128)
    w_c = w_in_c.rearrange("(k p) n -> p k n", p=128)
    w_qbv = w_qb.rearrange("(k p) n -> p k n", p=128)

    def reg_ffn(l, i):
        wg = wg_d[l, i].rearrange("(k p) f -> p k f", p=128)
        wu = wu_d[l, i].rearrange("(k p) f -> p k f", p=128)
        wd = wd_d[l, i].rearrange("(f p) d -> p f d", p=128)
        for pc in range(11):
            wreg("wg%d%d" % (l, i), pc, wg[:, :, pc * 256:(pc + 1) * 256])
        for pc in range(11):
            wreg("wu%d%d" % (l, i), pc, wu[:, :, pc * 256:(pc + 1) * 256])
        for dc in range(8):
            wreg("wd%d%d" % (l, i), dc, wd[:, :, dc * 128:(dc + 1) * 128])

    def reg_all():
        reg_ffn(0, 0)
        for p0 in range(0, 2304, 256):
            wreg("wab", p0 // 256, w_ab[:, :, p0:p0 + 256])
        wv = w_out_ab.rearrange("(h p) d -> p h d", p=64)
        for dc in range(8):
            wreg("woab", dc, wv[:, :, dc * 128:(dc + 1) * 128])
        reg_ffn(0, 1)
        reg_ffn(1, 0)
        wreg("wc", 0, w_c[:, :, 0:384])
        wreg("wc", 1, w_c[:, :, 384:672])
        for cb in range(3):
            wreg("wqb", cb, w_qbv[:, :, cb * 512:(cb + 1) * 512])
        wv = w_out_c.rearrange("(h p) d -> p h d", p=64)
        for dc in range(8):
            wreg("woc", dc, wv[:, :, dc * 128:(dc + 1) * 128])
        reg_ffn(1, 1)

    reg_all()

    DMA("sp", identf[:], identf_d, "identf", [], ["identf"])
    DMA("sp", identb[:], identb_d, "identb", [], ["identb"])
    DMA("sp", gTs[:], gT, "gTs", [], ["gTs"])
    DMA("sp", bT[:], b_adaT, "bT", [], ["bT"])
    DMA("sp", scT[:], cT, "scT", [], ["scT"])
    DMA("sp", hv[:], hv_d, "hv", [], ["hv"])
    DMA("sp", gq[:], gq_d, "gq", [], ["gq"])
    DMA("sp", gkv[:], gkv_d, "gkv", [], ["gkv"])
    DMA("sp", sinkexp[:], sinks_d, "sinkexp", [], ["sinkexp"])
    MSET("dve", onesb[:], 1.0, ["onesb"])
    MSET("dve", onesf[:], 1.0, ["onesf"])
    MSET("dve", zcol[:], 0.0, ["zcol"])
    MSET("dve", VA[:, :, :, :, 64:65], 1.0, ["VAones"])
    MSET("dve", VB[:, :, :, :, 64:65], 1.0, ["VBones"])
    ACT(sinkexp[:], sinkexp[:], AF.Exp, ["sinkexp"], ["sinkexp"])
    ACT(scT[:], scT[:], AF.Silu, ["scT"], ["scT"])
    for hh in range(4):
        DMA("sp", qh[96:105, hh, :], qaug_d, "qhaug", [], ["qhaug"])
    DMA("pool", wkvb[:], w_kvb.rearrange("(c p) n -> p c n", p=128), "wkvb", [], ["wkvb"])

    for l in range(2):
        for m in range(9):
            for half in range(2):
                if (m * 2 + half) % 2 == 0:
                    wst = act[:, 0:16 * TOK].bitcast(F32).rearrange("p (k n) -> p k n", n=512)
                    wkeys = ["act%d" % c for c in range(16)]; wsk = "wstA"
                else:
                    wst = qo[:, :, :].rearrange("p a b -> p (a b)").bitcast(F32).rearrange("p (k n) -> p k n", n=512)
                    wkeys = ["qo%d" % c for c in range(16)]; wsk = "wstB"
                src = w_ada[l].rearrange("(k p) n -> p k n", p=128)[:, :, m * 1024 + half * 512: m * 1024 + half * 512 + 512]
                DMA("sp", wst, src, wsk, [], wkeys)
                bk = Gbanks.next()
                for oc in range(4):
                    for k in range(8):
                        MM(bank(bk)[:, oc * 4:oc * 4 + 3], wst[:, k, oc * 128:(oc + 1) * 128], scT[:, k, :], k == 0, k == 7,
                           wkeys + ["scT"], ["ps%d" % bk])
                for oc in range(4):
                    ch = half * 4 + oc
                    TS("dve", M[:, l, m, ch, :], bank(bk)[:, oc * 4:oc * 4 + 3], bT[:, l, m * 8 + ch:m * 8 + ch + 1], None, ALU.add, None,
                       ["ps%d" % bk, "bT"], ["M"])
    for l in range(2):
        for n in range(3):
            for si in range(3):
                STT("dve", M[:, l, 3 * n + 1, :, si], M[:, l, 3 * n + 1, :, si], 1.0, gTs[:, 3 * l + n, :], ALU.add, ALU.mult, ["M", "gTs"], ["M"])
        for m in (2, 8):
            TS("dve", M[:, l, m, :, :], M[:, l, m, :, :], 0.5, None, ALU.mult, None, ["M"], ["M"])

    def norm_mod(ncol, scale_of, shift_of, out_of, out_keys):
        ACT(sq[:, :, 0:ncol], xT[:, :, 0:ncol], AF.Square, ["xT"], ["act%d" % (16 + c) for c in range(8)])
        bk = Gbanks.next()
        for c in range(8):
            MM(bank(bk)[:, 0:ncol], onesb[:], sq[:, c, 0:ncol], c == 0, c == 7, ["onesb", "act%d" % (16 + c)], ["ps%d" % bk])
        ACT(scr[:, 0, 0:ncol], bank(bk)[:, 0:ncol], AF.Sqrt, ["ps%d" % bk], ["scr0"], bias=epscol[:], scale=1.0 / D)
        RCP(scr[:, 1, 0:ncol], scr[:, 0, 0:ncol], ["scr0"], ["scr1"])
        for c in range(8):
            tb_ = 2 + (c % 2)
            TT("dve", scr[:, tb_, 0:ncol], xT[:, c, 0:ncol], scr[:, 1, 0:ncol], ALU.mult, ["xT", "scr1"], ["scr%d" % tb_])
            kw = dict(scale=scale_of(c))
            sh = shift_of(c)
            ACT(out_of(c), scr[:, tb_, 0:ncol], AF.Identity, ["scr%d" % tb_, "M", "gTs"], out_keys(c), bias=(sh if sh is not None else zcol[:]), **kw)

    def ffn(l, i, ncol, si, gate_m):
        for pc in range(11):
            g_ap, gk = wpiece("wg%d%d" % (l, i), pc)
            u_ap, uk = wpiece("wu%d%d" % (l, i), pc)
            for sub in range(2):
                fc = pc * 2 + sub
                bg = Gbanks.next(); bu = Gbanks.next()
                for k in range(8):
                    MM(bank(bg)[:, 0:ncol], g_ap[:, k, sub * 128:(sub + 1) * 128], hT[:, k, 0:ncol], k == 0, k == 7, [gk, "hT"], ["ps%d" % bg])
                for k in range(8):
                    MM(bank(bu)[:, 0:ncol], u_ap[:, k, sub * 128:(sub + 1) * 128], hT[:, k, 0:ncol], k == 0, k == 7, [uk, "hT"], ["ps%d" % bu])
                sgi = 4 + (fc % 2)
                ACT(scr[:, sgi, 0:ncol], bank(bg)[:, 0:ncol], AF.Silu, ["ps%d" % bg], ["scr%d" % sgi])
                TT("dve", actc(fc)[:, 0:ncol], bank(bu)[:, 0:ncol], scr[:, sgi, 0:ncol], ALU.mult, ["ps%d" % bu, "scr%d" % sgi], ["act%d" % fc])
        for dc in range(8):
            d_ap, dk = wpiece("wd%d%d" % (l, i), dc)
            by = Gbanks.next()
            for fc in range(NFC):
                MM(bank(by)[:, 0:ncol], d_ap[:, fc, :], actc(fc)[:, 0:ncol], fc == 0, fc == NFC - 1, [dk, "act%d" % fc], ["ps%d" % by])
            STT("dve", xT[:, dc, 0:ncol], bank(by)[:, 0:ncol], M[:, l, gate_m, dc, si:si + 1], xT[:, dc, 0:ncol], ALU.mult, ALU.add,
                ["ps%d" % by, "M", "xT"], ["xT"])

    def mod_norm(l, n, ncol, si):
        norm_mod(ncol,
                 lambda c: M[:, l, 3 * n + 1, c, si:si + 1],
                 lambda c: M[:, l, 3 * n + 0, c, si:si + 1],
                 lambda c: hT[:, c, 0:ncol],
                 lambda c: ["hT"])

    def attn_block(kT_ap, q_ap, nk, ncols, scale, bias_ap, pbias_ap, v_ap, acc_ap, first, R, accW, last=False):
        bs = Sbanks.next()
        MM(bank(bs)[0:nk, 0:ncols], kT_ap, q_ap, True, True, R, ["ps%d" % bs])
        pi = PTrot.next()
        if bias_ap is not None:
            si_ = Sbrot.next()
            STT("dve", Sb[0:nk, si_, 0:ncols], bank(bs)[0:nk, 0:ncols], scale, bias_ap, ALU.mult, ALU.add, ["ps%d" % bs] + R, ["Sb%d" % si_])
            ACT(PT[0:nk, pi, 0:ncols], Sb[0:nk, si_, 0:ncols], AF.Exp, ["Sb%d" % si_, "hv"], ["PT%d" % pi], bias=pbias_ap)
        else:
            ACT(PT[0:nk, pi, 0:ncols], bank(bs)[0:nk, 0:ncols], AF.Exp, ["ps%d" % bs], ["PT%d" % pi], scale=scale)
        MM(acc_ap, v_ap, PT[0:nk, pi, 0:ncols], first, last, ["PT%d" % pi] + R, accW)

    def attn_finish(ba, ncol, dest_ap, destW, sink_ap=None):
        if sink_ap is not None:
            TS("dve", den[64:65, 0:ncol], bank(ba)[64:65, 0:ncol], sink_ap, None, ALU.add, None, ["ps%d" % ba, "sinkexp"], ["den"])
        else:
            CP("dve", den[64:65, 0:ncol], bank(ba)[64:65, 0:ncol], ["ps%d" % ba], ["den"])
        bb = Gbanks.next()
        MM(bank(bb)[0:64, 0:ncol], onesf[64:65, 0:64], den[64:65, 0:ncol], True, True, ["onesf", "den"], ["ps%d" % bb])
        RCP(rb[:, 0:ncol], bank(bb)[0:64, 0:ncol], ["ps%d" % bb], ["rb"])
        TT("dve", dest_ap, bank(ba)[0:64, 0:ncol], rb[:, 0:ncol], ALU.mult, ["ps%d" % ba, "rb"], destW)

    CS_ALL = ["scr0", "scr1", "scr2", "scr3"]
    PTrot = Rot(range(3)); Sbrot = Rot(range(2)); farot = Rot(range(2)); fbrot = Rot(range(2)); stgrot = Rot(range(2))
    epscol = sb("epscol", [128, 1], F32)
    MSET("dve", epscol[:], EPS, ["epscol"])

    def l0_mixer(kind, ncol, si, slot, tblocks, seg, is_last, samp_idx):
        do_q = kind != "halo"
        plan = []
        if do_q:
            plan += [("q", h, 0 + 64 * h) for h in range(8)]
            plan += [("q", 8 + h, 1536 + 64 * h) for h in range(8)]
        plan += [("ka", h, 512 + 64 * h) for h in range(8)]
        plan += [("kb", h, 2048 + 64 * h) for h in range(2)]
        cur_piece = None
        for (typ, h, col) in plan:
            p0 = (col // 256) * 256
            if cur_piece is None or cur_piece[0] != p0:
                ap_, k_ = wpiece("wab", p0 // 256)
                cur_piece = (p0, ap_, k_)
            _, w_ap, wk = cur_piece
            bk = Gbanks.next()
            for k in range(8):
                MM(bank(bk)[0:64, 0:ncol], w_ap[:, k, col - p0:col - p0 + 64], hT[:, k, 0:ncol], k == 0, k == 7, [wk, "hT"], ["ps%d" % bk])
            if typ == "q":
                dst, W = qo[0:64, h, 0:ncol], ["qo%d" % h]
            elif typ == "ka":
                dst, W = kTA[:, slot, h, 0:ncol], ["kTA%d" % slot]
            else:
                dst, W = kTB[:, slot, h, 0:ncol], ["kTB%d" % slot]
            CP(cp_rot.next(), dst, bank(bk)[0:64, 0:ncol], ["ps%d" % bk], W)
        if SUB <= 4.2:
            return
        want_out = (is_last or kind == "samp") and SUB != 4.41
        for (tbi, (t0, nt)) in enumerate(tblocks):
            jobs = [("va", 1024, 512)]
            jobs += [("vb", 2176, 128)]
            if want_out:
                jobs += [("ka_o", 512, 512), ("kb_o", 2048, 128)]
            for (typ, col, wdt) in jobs:
                bk = Gbanks.next()
                for half in range(wdt // 256 if wdt >= 256 else 1):
                    cw = min(256, wdt)
                    p0 = ((col + half * 256) // 256) * 256
                    w_ap, wk = wpiece("wab", p0 // 256)
                    off = col + half * 256 - p0
                    for k in range(8):
                        MM(bank(bk)[0:nt, half * 256:half * 256 + cw], hT[:, k, t0:t0 + nt], w_ap[:, k, off:off + cw], k == 0, k == 7,
                           [wk, "hT"], ["ps%d" % bk])
                if typ == "va":
                    CP("act", VA[0:nt, slot, tbi, :, 0:64], bank(bk)[0:nt, 0:512].rearrange("p (h d) -> p h d", d=64), ["ps%d" % bk], ["VA%d" % slot])
                elif typ == "vb":
                    CP("dve", VB[0:nt, slot, tbi, :, 0:64], bank(bk)[0:nt, 0:128].rearrange("p (h d) -> p h d", d=64), ["ps%d" % bk], ["VB%d" % slot])
                if want_out:
                    sgi = stgrot.next()
                    CP("act" if typ in ("va", "ka_o") else "dve", stg[0:nt, sgi, 0:wdt], bank(bk)[0:nt, 0:wdt], ["ps%d" % bk], ["stg%d" % sgi])
                    if SUB == 4.42:
                        continue
                    if kind == "samp":
                        r0 = samp_idx * 32
                        dst = {"va": o_avs, "vb": o_bvs, "ka_o": o_aks, "kb_o": o_bks}[typ][r0:r0 + 32, :]
                        DMA(OUTQ, dst, stg[0:nt, sgi, 0:wdt], "stg%d" % sgi, ["stg%d" % sgi], [], final=True)
                    else:
                        if typ in ("va", "ka_o"):
                            dst = (o_av if typ == "va" else o_ak)[t0:t0 + nt, :]
                            DMA(OUTQ, dst, stg[0:nt, sgi, 0:wdt], "stg%d" % sgi, ["stg%d" % sgi], [], final=True)
                        elif tbi == 3:
                            dst = (o_bv if typ == "vb" else o_bk)[:, :]
                            DMA(OUTQ, dst, stg[0:nt, sgi, 0:wdt], "stg%d" % sgi, ["stg%d" % sgi], [], final=True)
        if not do_q or SUB <= 4.42:
            return
        if kind == "prompt":
            prev = 1 - slot
            for h in range(8):
                fi = farot.next()
                DMA("sp", fa[:, fi, :], FA_d[h], "fa%d" % fi, [], ["fa%d" % fi])
                ba = Abanks.next()
                first = True
                for m in (3, 4, 0, 1, 2, 5, 6, 7):
                    q0 = max(0, 128 * m - 512); q1 = min(512, 128 * m + 128)
                    sl = prev if m < 4 else slot
                    blk = m % 4
                    pb = hv[:, seg:seg + 1] if (m < 4) else zcol[:]
                    if m < 4 and not first_own[0]:
                        pb = zcol[:]
                    attn_block(kTA[:, sl, h, blk * 128:(blk + 1) * 128], qo[0:64, h, q0:q1], 128, q1 - q0, 0.125,
                               fa[:, fi, q0 - 128 * m + 512:q1 - 128 * m + 512], pb, VA[:, sl, blk, h, 0:65], bank(ba)[0:65, q0:q1], first,
                               ["kTA%d" % sl, "qo%d" % h, "fa%d" % fi, "VA%d" % sl, "VAones"], ["ps%d" % ba], last=(m == 7))
                    first = False
                attn_finish(ba, 512, qo[0:64, h, :], ["qo%d" % h])
            for h in range(8):
                fi = fbrot.next()
                DMA("sp", fb[:, fi, :], FB_d[h], "fb%d" % fi, [], ["fb%d" % fi])
                ba = Abanks.next()
                g = h // 4
                first = True
                for m in (4, 6, 3, 5, 7):
                    q0 = max(0, 128 * m - 512); q1 = min(512, 128 * m - 256)
                    sl = prev if m < 4 else slot
                    blk = m % 4
                    pb = hv[:, seg:seg + 1] if (m < 4 and first_own[0]) else zcol[:]
                    attn_block(kTB[:, sl, g, blk * 128:(blk + 1) * 128], qo[0:64, 8 + h, q0:q1], 128, q1 - q0, 0.125,
                               fb[:, fi, q0 - 128 * m + 512:q1 - 128 * m + 512], pb, VB[:, sl, blk, g, 0:65], bank(ba)[0:65, q0:q1], first,
                               ["kTB%d" % sl, "qo%d" % (8 + h), "fb%d" % fi, "VB%d" % sl, "VBones"], ["ps%d" % ba], last=(m == 7))
                    first = False
                attn_finish(ba, 512, qo[0:64, 8 + h, :], ["qo%d" % (8 + h)], sink_ap=sinkexp[64:65, h:h + 1])
        else:
            s_ = samp_idx
            csl = 1 - slot
            ck = cstage[:, 0:8, :].rearrange("p a b -> p (a b)")[:, 0:2048].rearrange("p (b n) -> p b n", n=512)
            DMA("sp", ck, cak[s_].rearrange("(b p) n -> p b n", p=128), "cstage", [], CS_ALL)
            for blk in range(4):
                for h in range(8):
                    bk = Gbanks.next()
                    TR(bank(bk)[0:64, 0:128], ck[:, blk, h * 64:(h + 1) * 64], identf[:], CS_ALL + ["identf"], ["ps%d" % bk])
                    CP(cp_rot.next(), kTA[:, csl, h, blk * 128:(blk + 1) * 128], bank(bk)[0:64, 0:128], ["ps%d" % bk], ["kTA%d" % csl])
            for blk in range(4):
                for hh in range(2):
                    DMA("pool", VA[:, csl, blk, hh * 4:(hh + 1) * 4, 0:64],
                        cav[s_, blk * 128:(blk + 1) * 128, hh * 256:(hh + 1) * 256].rearrange("p (h d) -> p h d", d=64), "VA%d" % csl, [], ["VA%d" % csl])
            for h in range(8):
                fi = farot.next()
                DMA("sp", fa[:, fi, 0:160].rearrange("p (b q) -> p b q", q=32), FAs_d[h], "fa%d" % fi, [], ["fa%d" % fi])
                fav = fa[:, fi, 0:160].rearrange("p (b q) -> p b q", q=32)
                ba = Abanks.next()
                for blk in range(5):
                    if blk < 4:
                        kT_ap = kTA[:, csl, h, blk * 128:(blk + 1) * 128]; v_ap = VA[:, csl, blk, h, 0:65]; nk = 128
                        R = ["kTA%d" % csl, "VA%d" % csl]
                    else:
                        kT_ap = kTA[:, slot, h, 0:32]; v_ap = VA[0:32, slot, 0, h, 0:65]; nk = 32
                        R = ["kTA%d" % slot, "VA%d" % slot]
                    attn_block(kT_ap, qo[0:64, h, 0:32], nk, 32, 0.125, fav[0:nk, blk, :], zcol[0:nk, :], v_ap, bank(ba)[0:65, 0:32], blk == 0,
                               R + ["qo%d" % h, "fa%d" % fi, "VAones"], ["ps%d" % ba], last=(blk == 4))
                attn_finish(ba, 32, qo[0:64, h, 0:32], ["qo%d" % h])
            if SUB <= 4.6:
                return
            ckb = cstage[:, 0, 0:128]
            DMA("sp", ckb, cbk[s_], "cstage", [], CS_ALL)
            for g in range(2):
                bk = Gbanks.next()
                TR(bank(bk)[0:64, 0:128], ckb[:, g * 64:(g + 1) * 64], identf[:], CS_ALL + ["identf"], ["ps%d" % bk])
                CP(cp_rot.next(), kTB[:, csl, g, 0:128], bank(bk)[0:64, 0:128], ["ps%d" % bk], ["kTB%d" % csl])
            DMA("pool", VB[:, csl, 0, :, 0:64], cbv[s_].rearrange("p (h d) -> p h d", d=64), "VB%d" % csl, [], ["VB%d" % csl])
            for h in range(8):
                fi = fbrot.next()
                DMA("sp", fb[:, fi, 0:64].rearrange("p (b q) -> p b q", q=32), FBs_d[h], "fb%d" % fi, [], ["fb%d" % fi])
                fbv = fb[:, fi, 0:64].rearrange("p (b q) -> p b q", q=32)
                ba = Abanks.next()
                g = h // 4
                for blk in range(2):
                    if blk == 0:
                        kT_ap = kTB[:, csl, g, 0:128]; v_ap = VB[:, csl, 0, g, 0:65]; nk = 128; R = ["kTB%d" % csl, "VB%d" % csl]
                    else:
                        kT_ap = kTB[:, slot, g, 0:32]; v_ap = VB[0:32, slot, 0, g, 0:65]; nk = 32; R = ["kTB%d" % slot, "VB%d" % slot]
                    attn_block(kT_ap, qo[0:64, 8 + h, 0:32], nk, 32, 0.125, fbv[0:nk, blk, :], zcol[0:nk, :], v_ap, bank(ba)[0:65, 0:32], blk == 0,
                               R + ["qo%d" % (8 + h), "fb%d" % fi, "VBones"], ["ps%d" % ba], last=(blk == 1))
                attn_finish(ba, 32, qo[0:64, 8 + h, 0:32], ["qo%d" % (8 + h)], sink_ap=sinkexp[64:65, h:h + 1])

    def out_proj(w_dram, ncol, l, si, gate_m):
        for dc in range(8):
            w_ap, wk = wpiece(w_dram, dc)
            by = Gbanks.next()
            for h in range(16):
                MM(bank(by)[:, 0:ncol], w_ap[:, h, :], qo[0:64, h, 0:ncol], h == 0, h == 15, [wk, "qo%d" % h], ["ps%d" % by])
            STT("dve", xT[:, dc, 0:ncol], bank(by)[:, 0:ncol], M[:, l, gate_m, dc, si:si + 1], xT[:, dc, 0:ncol], ALU.mult, ALU.add,
                ["ps%d" % by, "M", "xT"], ["xT"])

    def l1_prep(kind, ncol, tblocks, tile_idx, samp_idx):
        for (tbi, (t0, nt)) in enumerate(tblocks):
            wq_ap, wqk = wpiece("wc", 0)
            bq = Gbanks.next()
            for k in range(8):
                MM(bank(bq)[0:nt, 0:384], hT[:, k, t0:t0 + nt], wq_ap[:, k, :], k == 0, k == 7, [wqk, "hT"], ["ps%d" % bq])
            wk_ap, wkk = wpiece("wc", 1)
            bk2 = Gbanks.next()
            for k in range(8):
                MM(bank(bk2)[0:nt, 0:288], hT[:, k, t0:t0 + nt], wk_ap[:, k, :], k == 0, k == 7, [wkk, "hT"], ["ps%d" % bk2])
            ACT(scr[0:nt, 2, 0:384], bank(bq)[0:nt, 0:384], AF.Square,
                ["ps%d" % bq], ["scr2", "small"], accum=small[0:nt, 0:1])
            ACT(small[0:nt, 1:2], small[0:nt, 0:1], AF.Sqrt, ["small"], ["small"], bias=epscol[0:nt, :], scale=1.0 / 384)
            RCP(small[0:nt, 2:3], small[0:nt, 1:2], ["small"], ["small"])
            STT("dve", qn_bf[0:nt, :], bank(bq)[0:nt, 0:384], small[0:nt, 2:3], gq[0:nt, :], ALU.mult, ALU.mult, ["ps%d" % bq, "small", "gq"], ["qn_bf"])
            ACT(scr[0:nt, 3, 0:256], bank(bk2)[0:nt, 0:256], AF.Square, ["ps%d" % bk2], ["scr3", "small"], accum=small[0:nt, 3:4])
            ACT(small[0:nt, 4:5], small[0:nt, 3:4], AF.Sqrt, ["small"], ["small"], bias=epscol[0:nt, :], scale=1.0 / 256)
            RCP(small[0:nt, 5:6], small[0:nt, 4:5], ["small"], ["small"])
            STT("dve", kvn_f[0:nt, :], bank(bk2)[0:nt, 0:256], small[0:nt, 5:6], gkv[0:nt, :], ALU.mult, ALU.mult, ["ps%d" % bk2, "small", "gkv"], ["kvn_f"])
            CP("act", kvn_b[0:nt, 0:256], kvn_f[0:nt, :], ["kvn_f"], ["kvn_b"])
            if kind == "samp":
                DMA("sp", cs[0:nt, 0, :], cosS, "cs", [], ["cs"])
                DMA("sp", cs[0:nt, 1, :], sinS, "cs", [], ["cs"])
            else:
                DMA("sp", cs[:, 0, :], cosP[tile_idx * 4 + tbi], "cs", [], ["cs"])
                DMA("sp", cs[:, 1, :], sinP[tile_idx * 4 + tbi], "cs", [], ["cs"])
            x1 = bank(bk2)[0:nt, 256:272]; x2 = bank(bk2)[0:nt, 272:288]
            c16 = cs[0:nt, 0, 0:16]; s16 = cs[0:nt, 1, 0:16]
            TT("dve", rtmp[0:nt, 0, 0:16], x1, c16, ALU.mult, ["ps%d" % bk2, "cs"], ["rtmp0"])
            TT("dve", rtmp[0:nt, 1, 0:16], x2, s16, ALU.mult, ["ps%d" % bk2, "cs"], ["rtmp1"])
            TT("dve", kr_f[0:nt, 0:16], rtmp[0:nt, 0, 0:16], rtmp[0:nt, 1, 0:16], ALU.subtract, ["rtmp0", "rtmp1"], ["kr_f"])
            TT("dve", rtmp[0:nt, 2, 0:16], x1, s16, ALU.mult, ["ps%d" % bk2, "cs"], ["rtmp2"])
            TT("dve", rtmp[0:nt, 3, 0:16], x2, c16, ALU.mult, ["ps%d" % bk2, "cs"], ["rtmp3"])
            TT("dve", kr_f[0:nt, 16:32], rtmp[0:nt, 2, 0:16], rtmp[0:nt, 3, 0:16], ALU.add, ["rtmp2", "rtmp3"], ["kr_f"])
            CP("act", kvn_b[0:nt, 256:288], kr_f[0:nt, :], ["kr_f"], ["kvn_b"])
            if kind == "samp":
                r0 = samp_idx * 32
                DMA("sp", o_ckvs[r0:r0 + 32, :], kvn_f[0:nt, :], "kvn_f", ["kvn_f"], [], final=True)
                DMA("sp", o_ckrs[r0:r0 + 32, :], kr_f[0:nt, :], "kr_f", ["kr_f"], [], final=True)
            else:
                r0 = tile_idx * TOK + t0
                DMA("sp", o_ckv[r0:r0 + nt, :], kvn_f[0:nt, :], "kvn_f", ["kvn_f"], [], final=True)
                DMA("sp", o_ckr[r0:r0 + nt, :], kr_f[0:nt, :], "kr_f", ["kr_f"], [], final=True)
            bt = Gbanks.next()
            btv = bank(bt).bitcast(BF16)
            for c in range(3):
                w_ = 128 if c < 2 else 32
                TR(btv[0:w_, c * 128:c * 128 + nt], kvn_b[0:nt, c * 128:c * 128 + w_], identb[0:nt, 0:nt], ["kvn_b", "identb"], ["ps%d" % bt])
            CP("dve", gT_sb[:, 0:2, t0:t0 + nt], btv[:, 0:256].rearrange("p (c t) -> p c t", t=128)[:, :, 0:nt], ["ps%d" % bt], ["gT_sb"])
            CP("dve", gT_sb[0:32, 2, t0:t0 + nt], btv[0:32, 256:256 + nt], ["ps%d" % bt], ["gT_sb"])
            bt2 = Gbanks.next()
            bt2v = bank(bt2).bitcast(BF16)
            for c in range(3):
                TR(bt2v[:, c * 128:c * 128 + nt], qn_bf[0:nt, c * 128:(c + 1) * 128], identb[0:nt, 0:nt], ["qn_bf", "identb"], ["ps%d" % bt2])
            CP("act", qnT[:, :, t0:t0 + nt], bt2v[:, 0:384].rearrange("p (c t) -> p c t", t=128)[:, :, 0:nt], ["ps%d" % bt2], ["qnT"])
            for cb in range(3):
                wq2, wq2k = wpiece("wqb", cb)
                bqq = Gbanks.next()
                for k in range(3):
                    MM(bank(bqq)[0:nt, 0:512], qnT[:, k, t0:t0 + nt], wq2[:, k, :], k == 0, k == 2, [wq2k, "qnT"], ["ps%d" % bqq])
                CP(cp_rot.next(), qraw[0:nt, cb * 512:(cb + 1) * 512],
                   bank(bqq)[0:nt, 0:512], ["ps%d" % bqq], ["stg0", "stg1"])
            qv = qraw[0:nt, :].rearrange("p (h d) -> p h d", d=96)
            CP("act", q_bf[0:nt, :, 0:64], qv[:, :, 0:64], ["stg0", "stg1"], ["q_bf"])
            cosv = cs[0:nt, 0, :].rearrange("p (h d) -> p h d", d=16); sinv = cs[0:nt, 1, :].rearrange("p (h d) -> p h d", d=16)
            r3 = lambda i: rtmp[0:nt, i, :].rearrange("p (h d) -> p h d", d=16)
            TT("dve", r3(0), qv[:, :, 64:80], cosv, ALU.mult, ["stg0", "stg1", "cs"], ["rtmp0"])
            TT("dve", r3(1), qv[:, :, 80:96], sinv, ALU.mult, ["stg0", "stg1", "cs"], ["rtmp1"])
            TT("dve", q_bf[0:nt, :, 64:80], r3(0), r3(1), ALU.subtract, ["rtmp0", "rtmp1"], ["q_bf"])
            TT("dve", r3(2), qv[:, :, 64:80], sinv, ALU.mult, ["stg0", "stg1", "cs"], ["rtmp2"])
            TT("dve", r3(3), qv[:, :, 80:96], cosv, ALU.mult, ["stg0", "stg1", "cs"], ["rtmp3"])
            TT("dve", q_bf[0:nt, :, 80:96], r3(2), r3(3), ALU.add, ["rtmp2", "rtmp3"], ["q_bf"])
            for hg in range(2):
                bt3 = Gbanks.next()
                bt3v = bank(bt3).bitcast(BF16)
                for hh in range(8):
                    h = hg * 8 + hh
                    TR(bt3v[0:96, hh * 128:hh * 128 + nt], q_bf[0:nt, h, :], identb[0:nt, 0:nt], ["q_bf", "identb"], ["ps%d" % bt3])
                CP(cp_rot.next(), qo[0:96, hg * 8:(hg + 1) * 8, t0:t0 + nt], bt3v[0:96, :].rearrange("p (h t) -> p h t", t=128)[:, :, 0:nt],
                   ["ps%d" % bt3], ["qo%d" % h_ for h_ in range(hg * 8, hg * 8 + 8)])

    qraw = stg[:, :, :].rearrange("p a b -> p (a b)")[:, 0:1536]
    first_own = [False]

    def phase1_tile(kind, ncol, si, slot, seg, a, tile_idx, samp_idx):
        tblocks = [(i * 128, 128) for i in range(4)] if kind != "samp" else [(0, 32)]
        if kind == "samp":
            DMA("sp", xin[0:32, 0, :], xs[samp_idx], "xin", [], ["act%d" % c for c in range(4)])
        else:
            r0 = (a + 1) * TOK if kind == "prompt" else 0
            DMA("sp", xin[:, :, :], xp[seg, r0:r0 + TOK, :].rearrange("(b p) d -> p b d", p=128), "xin", [], ["act%d" % c for c in range(16)])
        for (tbi, (t0, nt)) in enumerate(tblocks):
            for c2 in range(2):
                bk = Gbanks.next()
                for cc in range(4):
                    c = c2 * 4 + cc
                    TR(bank(bk)[:, cc * 128:cc * 128 + nt], xin[0:nt, tbi, c * 128:(c + 1) * 128], identf[0:nt, 0:nt],
                       ["act%d" % c_ for c_ in range(16)] + ["identf"], ["ps%d" % bk])
                CP(cp_rot.next(), xT[:, c2 * 4:(c2 + 1) * 4, t0:t0 + nt], bank(bk).rearrange("p (c t) -> p c t", t=128)[:, :, 0:nt], ["ps%d" % bk], ["xT"])
        if SUB <= 1:
            return
        mod_norm(0, 0, ncol, si)
        if SUB <= 2:
            return
        ffn(0, 0, ncol, si, 2)
        if SUB <= 3:
            return
        mod_norm(0, 1, ncol, si)
        if SUB <= 4:
            return
        l0_mixer(kind, ncol, si, slot, tblocks, seg, kind == "prompt" and seg == 1 and a == ST - 1, samp_idx)
        if kind == "halo" or SUB <= 5:
            return
        out_proj("woab", ncol, 0, si, 5)
        mod_norm(0, 2, ncol, si)
        ffn(0, 1, ncol, si, 8)
        if SUB <= 6:
            return
        mod_norm(1, 0, ncol, si)
        ffn(1, 0, ncol, si, 2)
        mod_norm(1, 1, ncol, si)
        if SUB <= 7:
            return
        l1_prep(kind, ncol, tblocks, tile_idx, samp_idx)
        if SUB <= 8:
            return
        sp_i = tile_idx if kind == "prompt" else NQT + samp_idx
        DMA("sp", x_sp[sp_i][:, :, 0:ncol], xT[:, :, 0:ncol], "xT", ["xT"], ["x_sp%d" % sp_i])
        DMA("sp", q_sp[sp_i][:, :, 0:ncol], qo[0:96, :, 0:ncol], "qo", ["qo%d" % h for h in range(16)], ["q_sp%d" % sp_i])
        if kind == "prompt":
            gi_ = g_in[tile_idx // 2]; c0_ = (tile_idx % 2) * TOK
            DMA("sp", gi_[0:256, c0_:c0_ + TOK].rearrange("(c p) t -> p c t", p=128), gT_sb[:, 0:2, :], "gT_sb", ["gT_sb"], ["g_in%d" % (tile_idx // 2)])
            DMA("sp", gi_[256:288, c0_:c0_ + TOK], gT_sb[0:32, 2, :], "gT_sb", ["gT_sb"], ["g_in%d" % (tile_idx // 2)])
        else:
            DMA("sp", g_s[samp_idx, 0:256, 1024:1056].rearrange("(c p) t -> p c t", p=128), gT_sb[:, 0:2, 0:32], "gT_sb", ["gT_sb"], ["g_s%d" % samp_idx])
            DMA("sp", g_s[samp_idx, 256:288, 1024:1056], gT_sb[0:32, 2, 0:32], "gT_sb", ["gT_sb"], ["g_s%d" % samp_idx])

    def sample_cache_latents(s_):
        for half in range(2):
            DMA("sp", cstage[:, 0:4, :], cckv[s_, half * 512:(half + 1) * 512, :].rearrange("(b p) n -> p b n", p=128), "cstage", [], CS_ALL)
            for c in range(2):
                bk = Gbanks.next()
                for blk in range(4):
                    TR(bank(bk)[:, blk * 128:(blk + 1) * 128], cstage[:, blk, c * 128:(c + 1) * 128], identf[:], CS_ALL + ["identf"], ["ps%d" % bk])
                CP(cp_rot.next(), gT_sb[:, c, :], bank(bk), ["ps%d" % bk], ["gT_sb"])
            DMA("sp", g_s[s_, 0:256, half * 512:(half + 1) * 512].rearrange("(c p) t -> p c t", p=128), gT_sb[:, 0:2, :], "gT_sb", ["gT_sb"], ["g_s%d" % s_])
        for half in range(2):
            crv = cstage[:, 4, :].rearrange("p (b n) -> p b n", n=32)[:, 0:4, :]
            DMA("sp", crv, cckr[s_, half * 512:(half + 1) * 512, :].rearrange("(b p) n -> p b n", p=128), "cstage4", [], ["scr2"])
            bk = Gbanks.next()
            for blk in range(4):
                TR(bank(bk)[0:32, blk * 128:(blk + 1) * 128], crv[:, blk, :], identf[:], ["scr2", "identf"], ["ps%d" % bk])
            CP(cp_rot.next(), gT_sb[0:32, 2, :], bank(bk)[0:32, :], ["ps%d" % bk], ["gT_sb"])
            DMA("sp", g_s[s_, 256:288, half * 512:(half + 1) * 512], gT_sb[0:32, 2, :], "gT_sb", ["gT_sb"], ["g_s%d" % s_])

    KTrot = Rot(range(4)); kvTrot = Rot(range(2))

    def phase2_tile(kind, ncol, si, sp_i, key_tiles, out_ap_of):
        DMA("sp", xT[:, :, 0:ncol], x_sp[sp_i][:, :, 0:ncol], "xT", ["x_sp%d" % sp_i], ["xT"])
        SB = [0, 1, 4, 5]
        nv = len(key_tiles)
        for hp in range(8):
            par = hp % 2
            qkey = "qh%d" % par
            DMA("sp", qh[0:96, 2 * par:2 * par + 2, 0:ncol], q_sp[sp_i][:, hp * 2:hp * 2 + 2, 0:ncol], qkey, ["q_sp%d" % sp_i], [qkey])
            bas = [2, 3]
            units = [(vi, hh) for vi in range(nv) for hh in range(2)]
            blocks = [(ui, blk) for ui in range(len(units)) for blk in range(4)]
            nb = len(blocks)
            kt_of = {}

            def stageA(ui):
                vi, hh = units[ui]
                t_, kaug_ap = key_tiles[vi]
                h = hp * 2 + hh
                kti = KTrot.next()
                DMA("sp", KT[0:96, kti, :], ksc[t_, h], "KTn%d" % kti, ["ksc%d" % t_], ["KTn%d" % kti])
                DMA("sp", KT[96:105, kti, :], kaug_ap, "KTa%d" % kti, [], ["KTa%d" % kti])
                DMA("sp", Vh[:, kti, :, 0:64], vsc[t_, h].rearrange("p (b d) -> p b d", d=64), "Vh%d" % kti, ["vsc%d" % t_], ["Vh%d" % kti])
                kt_of[ui] = kti

            def S_(g):
                ui, blk = blocks[g]
                vi, hh = units[ui]
                kti = kt_of[ui]
                bs = SB[g % 4]
                MM(bank(bs)[:, 0:ncol], KT[0:105, kti, blk * 128:(blk + 1) * 128], qh[0:105, 2 * par + hh, 0:ncol], True, True,
                   ["KTn%d" % kti, "KTa%d" % kti, qkey, "qhaug"], ["ps%d" % bs])

            def E_(g):
                bs = SB[g % 4]
                ACT(PT[:, g % 3, 0:ncol], bank(bs)[:, 0:ncol], AF.Exp, ["ps%d" % bs], ["PT%d" % (g % 3)], scale=MLA_SCALE)

            def P_(g):
                ui, blk = blocks[g]
                vi, hh = units[ui]
                kti = kt_of[ui]
                MM(bank(bas[hh])[0:65, 0:ncol], Vh[:, kti, blk, 0:65], PT[:, g % 3, 0:ncol], vi == 0 and blk == 0, vi == nv - 1 and blk == 3,
                   ["PT%d" % (g % 3), "Vh%d" % kti, "Vhones"], ["ps%d" % bas[hh]])

            stageA(0)
            if len(units) > 1:
                stageA(1)
            S_(0)
            S_(1)
            for g in range(nb):
                ui, blk = blocks[g]
                if blk == 0 and ui + 2 < len(units):
                    stageA(ui + 2)
                E_(g)
                if g + 2 < nb:
                    S_(g + 2)
                P_(g)
            for hh in range(2):
                h = hp * 2 + hh
                attn_finish_p2(bas[hh], ncol, qo[0:64, h, 0:ncol], ["qo%d" % h])
        out_proj("woc", ncol, 1, si, 5)
        mod_norm(1, 2, ncol, si)
        ffn(1, 1, ncol, si, 8)
        norm_mod(ncol, lambda c: gTs[:, 6, c:c + 1], lambda c: None, lambda c: yfin[:, c, 0:ncol], lambda c: ["act%d" % (2 * c), "act%d" % (2 * c + 1)])
        nblk = (ncol + 127) // 128
        for tb in range(nblk):
            nt = min(128, ncol - tb * 128)
            sgi = stgrot.next()
            for c2 in range(2):
                bk = Gbanks.next()
                for cc in range(4):
                    c = c2 * 4 + cc
                    TR(bank(bk)[0:nt, cc * 128:(cc + 1) * 128], yfin[:, c, tb * 128:tb * 128 + nt], identf[:], ["act%d" % (2 * c), "act%d" % (2 * c + 1), "identf"], ["ps%d" % bk])
                CP(cp_rot.next(), stg[0:nt, sgi, c2 * 512:(c2 + 1) * 512], bank(bk)[0:nt, :], ["ps%d" % bk], ["stg%d" % sgi])
            DMA("sp", out_ap_of(tb, nt), stg[0:nt, sgi, :], "stg%d" % sgi, ["stg%d" % sgi], [], final=True)

    NKT = 32 + 6
    ksc = dint("ksc", [NKT, 16, 96, TOK], BF16)
    vsc = dint("vsc", [NKT, 16, 128, 256], BF16)
    KTst = act[0:96, 0:16 * TOK].rearrange("p (h t) -> p h t", t=TOK)
    Vst = act[:, 16 * TOK:24 * TOK].rearrange("p (h b d) -> p h b d", b=4, d=64)
    KST_K = ["act%d" % c for c in range(16)]
    VST_K = ["act%d" % c for c in range(16, 24)]

    def expand_tile(t_, lat2, krs, gkeys):
        kvi = kvTrot.next()
        DMA("sp", kvT[:, kvi, :, :], lat2, "kvT%d" % kvi, gkeys, ["kvT%d" % kvi])
        for h in range(16):
            DMA("sp", KTst[64:96, h, :], krs, "kst", gkeys, KST_K)
        for h in range(16):
            bk = Gbanks.next()
            for c in range(2):
                MM(bank(bk)[0:64, :], wkvb[:, c, h * 128:h * 128 + 64], kvT[:, kvi, c, :], c == 0, c == 1, ["wkvb", "kvT%d" % kvi], ["ps%d" % bk])
            CP(cp_rot.next(), KTst[0:64, h, :], bank(bk)[0:64, :], ["ps%d" % bk], KST_K)
        wv_ = wkvb[:, :, :].rearrange("p c (h e) -> p c h e", e=128)
        for blk in range(4):
            for half in range(2):
                bk = Gbanks.next()
                for c in range(2):
                    MM(bank(bk)[:, :].rearrange("p (h d) -> p h d", d=64), kvT[:, kvi, c, blk * 128:(blk + 1) * 128],
                       wv_[:, c, half * 8:(half + 1) * 8, 64:128], c == 0, c == 1, ["wkvb", "kvT%d" % kvi], ["ps%d" % bk])
                CP(cp_rot.next(), Vst[:, half * 8:(half + 1) * 8, blk, :], bank(bk)[:, :].rearrange("p (h d) -> p h d", d=64), ["ps%d" % bk], VST_K)
        for h0 in range(0, 16, 4):
            DMA("pool", ksc[t_, h0:h0 + 4].rearrange("h p n -> p h n"), KTst[:, h0:h0 + 4, :], "kst_o", KST_K, ["ksc%d" % t_])
            DMA("pool", vsc[t_, h0:h0 + 4].rearrange("h p (b d) -> p h b d", d=64), Vst[:, h0:h0 + 4, :, :], "vst_o", VST_K, ["vsc%d" % t_])

    def attn_finish_p2(ba, ncol, dest_ap, destW):
        CP("dve", den[64:65, 0:ncol], bank(ba)[64:65, 0:ncol], ["ps%d" % ba], ["den"])
        bb = Sbanks.next()
        MM(bank(bb)[0:64, 0:ncol], onesf[64:65, 0:64], den[64:65, 0:ncol], True, True, ["onesf", "den"], ["ps%d" % bb])
        RCP(rb[:, 0:ncol], bank(bb)[0:64, 0:ncol], ["ps%d" % bb], ["rb"])
        TT("dve", dest_ap, bank(ba)[0:64, 0:ncol], rb[:, 0:ncol], ALU.mult, ["ps%d" % ba, "rb"], destW)

    zb = PT[:, 0, 0:480]
    MSET("dve", zb, 0.0, ["PT0"])
    for s_ in range(2):
        DMA("sp", g_s[s_, 0:256, 1056:1536].rearrange("(c p) t -> p c t", p=128)[:, 0, :], zb, "zb%d" % s_, ["PT0"], ["g_s%d" % s_])
        DMA("sp", g_s[s_, 0:256, 1056:1536].rearrange("(c p) t -> p c t", p=128)[:, 1, :], zb, "zb%d" % s_, ["PT0"], ["g_s%d" % s_])
        DMA("sp", g_s[s_, 256:288, 1056:1536], zb[0:32, :], "zb%d" % s_, ["PT0"], ["g_s%d" % s_])
    for s_ in range(2 if 's1' in STAGES else 0):
        phase1_tile("samp", 32, 1 + s_, 0, 0, 0, 0, s_)
        if SUB >= 10:
            sample_cache_latents(s_)
    tile_idx = 0
    for seg in range(NSEG if 'p1' in STAGES else 0):
        for a in range(-1, ST):
            slot = (a + 1) % 2
            if a < 0:
                phase1_tile("halo", TOK, 0, slot, seg, a, -1, 0)
            else:
                first_own[0] = (a == 0)
                phase1_tile("prompt", TOK, 0, slot, seg, a, tile_idx, 0)
                tile_idx += 1
    if 'ag' in STAGES:
        for gc in range(NGC):
            S.op("pool", lambda e, gc=gc: e.collective_compute("AllGather", ALU.bypass, replica_groups=[[0, 1, 2, 3], [4, 5, 6, 7]],
                                                               ins=[g_in[gc].opt()], outs=[g_all[gc].opt()]), ["g_in%d" % gc], ["g_all%d" % gc])
    MSET("dve", Vh[:, :, :, 64:65], 1.0, ["VA0", "VA1", "VAones", "Vhones"])
    for s_ in range(2 if 's2' in STAGES else 0):
        kts = []
        for v in range(3):
            lat2 = g_s[s_, 0:256, v * TOK:(v + 1) * TOK].rearrange("(c p) t -> p c t", p=128)
            t_ = 32 + 3 * s_ + v
            expand_tile(t_, lat2, g_s[s_, 256:288, v * TOK:(v + 1) * TOK], ["g_s%d" % s_])
            kts.append((t_, kaugs_d[v]))
        phase2_tile("samp", 32, 1 + s_, NQT + s_, kts, lambda tb, nt, s_=s_: y_s[s_ * 32:s_ * 32 + 32, :])
    if 'p2' in STAGES:
        for v in range(32):
            jj, off = gloc(v)
            lt_ = off // TOK
            ga_ = g_all[lt_ // 2]; c0_ = (lt_ % 2) * TOK
            lat2 = ga_[jj * 288:jj * 288 + 256, c0_:c0_ + TOK].rearrange("(c p) t -> p c t", p=128)
            expand_tile(v, lat2, ga_[jj * 288 + 256:jj * 288 + 288, c0_:c0_ + TOK], ["g_all%d" % (lt_ // 2)])
    for qi in range(NQT if 'p2' in STAGES else 0):
        kts = [(v, kaug_d[qi, v]) for v in range(nv_of(qi))]
        phase2_tile("prompt", TOK, 0, qi, kts, lambda tb, nt, qi=qi: y_p[qi * TOK + tb * 128:qi * TOK + tb * 128 + nt, :])

    S.emit(st)
    st.close()
    return nc


_NC_CACHE = {}
STOP = 9
STAGES = {'s1', 'p1', 'ag', 's2', 'p2'}
OUTQ = 'sp'
SUB = 99


def _cpu():
    return jax.default_device(jax.devices("cpu")[0])


def _t5_bucket(rel):
    nb = 16
    rel = jnp.asarray(rel)
    ret = jnp.where(rel > 0, nb, 0)
    n = jnp.abs(rel)
    max_exact = nb // 2
    large = max_exact + (jnp.log(jnp.maximum(n, 1).astype(jnp.float32) / max_exact)
                         / np.log(128 / max_exact) * (nb - max_exact)).astype(jnp.int32)
    large = jnp.minimum(large, nb - 1)
    return np.asarray(ret + jnp.where(n < max_exact, n, large))


def _tables(rel_bias_a, t5_bias):
    tab = np.asarray(rel_bias_a[0], np.float32)
    t5 = np.asarray(t5_bias, np.float32)
    kk = np.arange(128)[:, None]
    col = np.arange(640)[None, :]
    u = col - 512
    rel = kk - col
    d = (kk >= 64).astype(np.int64) - np.floor_divide(u, 64)
    idx = np.clip(rel, -128, 128) + 128
    FA = np.where(((d >= 0) & (d <= 8))[None], tab[idx].transpose(2, 0, 1), np.float32(NEG)).astype(np.float32)
    col = np.arange(256)[None, :]
    u = col - 512
    rel = kk - col
    d = (kk >= 64).astype(np.int64) - np.floor_divide(u, 64)
    FB = np.where(((d >= 6) & (d <= 8))[None], t5[_t5_bucket(rel)].transpose(2, 0, 1), np.float32(NEG)).astype(np.float32)
    q = np.arange(32)[None, None, :]
    blk = np.arange(5)[None, :, None]
    kk3 = np.arange(128)[:, None, None]
    rel = 128 * blk + kk3 - 512 - q
    FAs = tab[np.clip(rel, -128, 128) + 128].transpose(3, 0, 1, 2).astype(np.float32)
    blk = np.arange(2)[None, :, None]
    rel = 128 * blk + kk3 - 128 - q
    FBs = t5[_t5_bucket(rel)].transpose(3, 0, 1, 2).astype(np.float32)
    return FA, FB, FAs, FBs


def _rope_tab(pos):
    half = 16
    inv = 10000.0 ** (-jnp.arange(half, dtype=jnp.float32) / half)
    ang = jnp.asarray(pos).astype(jnp.float32)[:, None] * inv[None, :]
    return np.asarray(jnp.cos(ang)), np.asarray(jnp.sin(ang))


def kernel(x_prompt, x_sample, c_prompt, c_sample, cache_a_k, cache_a_v, cache_b_k, cache_b_v,
           cache_c_kv, cache_c_kr, w_ada, b_ada, norm_g, final_norm_g, ffn_w_gate, ffn_w_up, ffn_w_down,
           w_in_ab, w_out_ab, rel_bias_a, t5_bias, sinks_b, w_in_c, c_q_norm_g, c_kv_norm_g, w_qb, w_kvb,
           w_out_c):
    if "nc" not in _NC_CACHE:
        _NC_CACHE["nc"] = build_program()
    nc = _NC_CACHE["nc"]
    in_maps = make_in_maps(x_prompt, x_sample, c_prompt, c_sample, cache_a_k, cache_a_v, cache_b_k, cache_b_v,
                           cache_c_kv, cache_c_kr, w_ada, b_ada, norm_g, final_norm_g, ffn_w_gate, ffn_w_up, ffn_w_down,
                           w_in_ab, w_out_ab, rel_bias_a, t5_bias, sinks_b, w_in_c, c_q_norm_g, c_kv_norm_g, w_qb, w_kvb,
                           w_out_c)
    res = run_bass_kernel_spmd(nc, in_maps, core_ids=list(range(8)))
    return assemble(res.results)


def make_in_maps(x_prompt, x_sample, c_prompt, c_sample, cache_a_k, cache_a_v, cache_b_k, cache_b_v,
                 cache_c_kv, cache_c_kr, w_ada, b_ada, norm_g, final_norm_g, ffn_w_gate, ffn_w_up, ffn_w_down,
                 w_in_ab, w_out_ab, rel_bias_a, t5_bias, sinks_b, w_in_c, c_q_norm_g, c_kv_norm_g, w_qb, w_kvb,
                 w_out_c):
    f = lambda a: np.ascontiguousarray(np.asarray(a, np.float32))
    x_prompt = f(x_prompt); x_sample = f(x_sample)
    with _cpu():
        FA, FB, FAs, FBs = _tables(f(rel_bias_a), f(t5_bias))
    b_adaT = f(b_ada).reshape(2, 72, 128).transpose(2, 0, 1)
    gT = np.concatenate([f(norm_g).reshape(6, 8, 128), f(final_norm_g).reshape(1, 8, 128)], 0).transpose(2, 0, 1)
    qaug = np.zeros((9, TOK), np.float32)
    qc = np.arange(TOK) // 64
    for r in range(8):
        qaug[r] = -(qc < r).astype(np.float32)
    qaug[8] = -1.0
    kaugs = np.zeros((3, 9, TOK), np.float32)
    kidx = np.arange(3 * TOK).reshape(3, TOK)
    kaugs[:, 8, :] = np.where(kidx >= 1056, BIG, 0.0)
    with _cpu():
        cS, sS = _rope_tab(1024 + np.arange(32))
    shared = dict(
        w_ada=f(w_ada), b_adaT=f(b_adaT), gT=f(gT), ffn_w_gate=f(ffn_w_gate), ffn_w_up=f(ffn_w_up), ffn_w_down=f(ffn_w_down),
        w_in_ab=f(w_in_ab)[0], w_out_ab=f(w_out_ab)[0], w_in_c=f(w_in_c)[0], w_qb=f(w_qb)[0], w_kvb=f(w_kvb)[0], w_out_c=f(w_out_c)[0],
        gq=f(np.broadcast_to(f(c_q_norm_g)[0][None], (128, 384))), gkv=f(np.broadcast_to(f(c_kv_norm_g)[0][None], (128, 256))),
        FA=FA, FB=FB, FAs=FAs, FBs=FBs, sinks=f(np.broadcast_to(f(sinks_b)[0][None], (128, 8))),
        cosS=f(np.tile(cS, (1, 16))), sinS=f(np.tile(sS, (1, 16))),
        kaugs=kaugs.astype(ml_dtypes.bfloat16), qaug=qaug.astype(ml_dtypes.bfloat16),
        identf=np.eye(128, dtype=np.float32), identb=np.eye(128, dtype=np.float32).astype(ml_dtypes.bfloat16),
    )
    in_maps = []
    for c in range(8):
        b, j = divmod(c, 4)
        xp = np.zeros((NSEG, (ST + 1) * TOK, D), np.float32)
        hvv = np.zeros((128, 4), np.float32)
        pos = np.zeros(NQT * TOK, np.int64)
        for k in range(NSEG):
            s0 = seg_of(j, k) * ST * TOK
            if s0 == 0:
                xp[k, TOK:] = x_prompt[b, 0:ST * TOK]
                hvv[:, k] = NEG
            else:
                xp[k] = x_prompt[b, s0 - TOK:s0 + ST * TOK]
            pos[k * ST * TOK:(k + 1) * ST * TOK] = s0 + np.arange(ST * TOK)
        with _cpu():
            cP, sP = _rope_tab(pos)
        cosP = np.tile(cP.reshape(NQT * 4, 128, 1, 16), (1, 1, 16, 1)).reshape(NQT * 4, 128, 256)
        sinP = np.tile(sP.reshape(NQT * 4, 128, 1, 16), (1, 1, 16, 1)).reshape(NQT * 4, 128, 256)
        cvec = np.stack([f(c_prompt)[b], f(c_sample)[2 * c], f(c_sample)[2 * c + 1]], -1)
        kaug = np.zeros((NQT, 32, 9, TOK), np.float32)
        kc = np.arange(TOK) // 64
        for qi in range(NQT):
            k, a = divmod(qi, ST)
            T = seg_of(j, k) * ST + a
            for v in range(nv_of(qi)):
                if v == T:
                    for r in range(8):
                        kaug[qi, v, r] = (kc == r) * BIG
                elif v > T:
                    kaug[qi, v, 8] = BIG
        m = dict(shared)
        m.update(
            xp=xp, xs=f(x_sample[2 * c:2 * c + 2]), cT=f(cvec.reshape(8, 128, 3).transpose(1, 0, 2)),
            cak=f(cache_a_k)[0, 2 * c:2 * c + 2].reshape(2, 512, 512), cav=f(cache_a_v)[0, 2 * c:2 * c + 2].reshape(2, 512, 512),
            cbk=f(cache_b_k)[0, 2 * c:2 * c + 2].reshape(2, 128, 128), cbv=f(cache_b_v)[0, 2 * c:2 * c + 2].reshape(2, 128, 128),
            cckv=f(cache_c_kv)[0, 2 * c:2 * c + 2], cckr=f(cache_c_kr)[0, 2 * c:2 * c + 2],
            hv=hvv, cosP=f(cosP), sinP=f(sinP), kaug=kaug.astype(ml_dtypes.bfloat16),
        )
        in_maps.append({k_: np.ascontiguousarray(v_) for k_, v_ in m.items()})
    return in_maps


def assemble(R):
    y_prompt = np.zeros((2, 16384, D), np.float32)
    ckv_p = np.zeros((1, 2, 16384, 256), np.float32)
    ckr_p = np.zeros((1, 2, 16384, 32), np.float32)
    for c in range(8):
        b, j = divmod(c, 4)
        for k in range(NSEG):
            s0 = seg_of(j, k) * ST * TOK
            sl = slice(k * ST * TOK, (k + 1) * ST * TOK)
            y_prompt[b, s0:s0 + ST * TOK] = R[c]["y_p"][sl]
            ckv_p[0, b, s0:s0 + ST * TOK] = R[c]["o_ckv"][sl]
            ckr_p[0, b, s0:s0 + ST * TOK] = R[c]["o_ckr"][sl]
    cat = lambda name: np.concatenate([R[c][name] for c in range(8)], 0)
    y_sample = cat("y_s").reshape(16, 32, D)
    last = [0, 4]
    a_k_p = np.stack([R[c]["o_ak"] for c in last], 0).reshape(1, 2, 512, 8, 64)
    a_v_p = np.stack([R[c]["o_av"] for c in last], 0).reshape(1, 2, 512, 8, 64)
    b_k_p = np.stack([R[c]["o_bk"] for c in last], 0).reshape(1, 2, 128, 2, 64)
    b_v_p = np.stack([R[c]["o_bv"] for c in last], 0).reshape(1, 2, 128, 2, 64)
    return (y_prompt, y_sample, a_k_p, a_v_p, b_k_p, b_v_p, ckv_p, ckr_p,
            cat("o_aks").reshape(1, 16, 32, 8, 64), cat("o_avs").reshape(1, 16, 32, 8, 64),
            cat("o_bks").reshape(1, 16, 32, 2, 64), cat("o_bvs").reshape(1, 16, 32, 2, 64),
            cat("o_ckvs").reshape(1, 16, 32, 256), cat("o_ckrs").reshape(1, 16, 32, 32))
```

```python
import numpy as np
import concourse.bass as bass
import concourse.mybir as mybir

F32 = mybir.dt.float32
BF16 = mybir.dt.bfloat16
AF = mybir.ActivationFunctionType
ALU = mybir.AluOpType

EPOCH = 16000
DMA_EPOCH = 1000


class Sched:
    ENGS = ("pe", "act", "dve", "pool", "sp")

    def __init__(self, nc, same_engine_sync=("act", "dve", "pool")):
        self.nc = nc
        self.ops = []
        self.lastw = {}
        self.lastr = {}
        self.same = set(same_engine_sync)
        self.dma_count = {}

    def _deps(self, reads, writes):
        deps = set()
        for k in reads:
            for d in self.lastw.get(k, {}).values():
                deps.add(d)
        for k in writes:
            for d in self.lastw.get(k, {}).values():
                deps.add(d)
            for d in self.lastr.get(k, {}).values():
                deps.add(d)
        return deps

    def _record(self, idx, agent, reads, writes):
        for k in reads:
            self.lastr.setdefault(k, {})[agent] = idx
        for k in writes:
            self.lastw.setdefault(k, {})[agent] = idx

    def op(self, eng, fn, reads=(), writes=()):
        idx = len(self.ops)
        deps = self._deps(reads, writes)
        self.ops.append(dict(kind="c", eng=eng, fn=fn, deps=deps))
        self._record(idx, eng, reads, writes)
        return idx

    def dma(self, queue, fn, key, reads=(), writes=(), final=False):
        idx = len(self.ops)
        key = key + "_" + queue
        deps = self._deps(reads, writes)
        n = self.dma_count.get(key, 0) + 1
        self.dma_count[key] = n
        self.ops.append(dict(kind="d", eng=queue, fn=fn, deps=deps, key=key, n=n, final=final))
        self._record(idx, "dma:" + key, reads, writes)
        return idx

    def emit(self, stack):
        nc = self.nc
        ops = self.ops
        needed = set()
        for i, o in enumerate(ops):
            for d in o["deps"]:
                od = ops[d]
                if od["kind"] == "c":
                    if od["eng"] == o["eng"] and od["eng"] not in self.same:
                        continue
                    needed.add(d)
        cnt = {e: 0 for e in self.ENGS}
        for i, o in enumerate(ops):
            if o["kind"] == "c" and i in needed:
                cnt[o["eng"]] += 1
                o["ms"] = cnt[o["eng"]]
        sems = {}

        def sem(name):
            if name not in sems:
                sems[name] = stack.enter_context(nc.semaphore(name))
            return sems[name]

        def target(d):
            od = ops[d]
            if od["kind"] == "c":
                m = od["ms"]
                return ("c_%s_%d" % (od["eng"], (m - 1) // EPOCH), (m - 1) % EPOCH + 1)
            n = od["n"]
            return ("d_%s_%d" % (od["key"], (n - 1) // DMA_EPOCH), ((n - 1) % DMA_EPOCH + 1) * 16)

        per_eng = {e: [] for e in self.ENGS}
        for i, o in enumerate(ops):
            per_eng[o["eng"]].append(i)
        for i, o in enumerate(ops):
            if o["kind"] == "c":
                if "ms" in o:
                    sem(target(i)[0])
            else:
                sem(target(i)[0])
        self.n_sems = len(sems)

        def run(engname, eng):
            waited = {}
            for i in per_eng[engname]:
                o = ops[i]
                wl = {}
                for d in o["deps"]:
                    od = ops[d]
                    if od["kind"] == "c" and od["eng"] == engname and engname not in self.same:
                        continue
                    s, v = target(d)
                    if waited.get(s, 0) >= v:
                        continue
                    wl[s] = max(wl.get(s, 0), v)
                for s, v in wl.items():
                    eng.wait_ge(sem(s), v)
                    waited[s] = v
                ins = o["fn"](eng)
                if o["kind"] == "c":
                    if "ms" in o:
                        ins.then_inc(sem(target(i)[0]), 1)
                else:
                    ins.then_inc(sem(target(i)[0]), 16)
            for i in per_eng[engname]:
                o = ops[i]
                if o["kind"] == "d" and o.get("final"):
                    s, v = target(i)
                    if waited.get(s, 0) < v:
                        eng.wait_ge(sem(s), v)
                        waited[s] = v

        with nc.Block() as block:
            @block.sync
            def _(e):
                run("sp", e)

            @block.scalar
            def _(e):
                run("act", e)

            @block.vector
            def _(e):
                run("dve", e)

            @block.gpsimd
            def _(e):
                run("pool", e)

            @block.tensor
            def _(e):
                run("pe", e)

from contextlib import ExitStack
import ml_dtypes
import jax
import jax.numpy as jnp
from concourse.bass_utils import run_bass_kernel_spmd

D = 1024
FF = 2816
NFC = 22
TOK = 512
ST = 4
NSEG = 2
EPS = 1e-6
BIG = 16384.0
MAXDESC = 512
NEG = -1e30
MLA_SCALE = 96 ** -0.5
NQT = NSEG * ST


def seg_of(j, k):
    return j if k == 0 else 7 - j


def nv_of(qi):
    k, a = divmod(qi, ST)
    return (ST * (3 if k == 0 else 7)) + a + 1


def gloc(v):
    s, a = divmod(v, ST)
    jj = s if s < 4 else 7 - s
    kk = 0 if s < 4 else 1
    return jj, (kk * ST + a) * TOK


class Rot:
    def __init__(self, items):
        self.items = list(items)
        self.i = 0

    def next(self):
        r = self.items[self.i % len(self.items)]
        self.i += 1
        return r


def build_program():
    nc = bass.Bass("TRN2", target_bir_lowering=False)
    st = ExitStack()
    S = Sched(nc)

    def din(name, shape, dt=F32):
        return nc.dram_tensor(name, list(shape), dt, kind="ExternalInput").ap()

    def dout(name, shape, dt=F32):
        return nc.dram_tensor(name, list(shape), dt, kind="ExternalOutput").ap()

    def dint(name, shape, dt):
        return nc.dram_tensor(name, list(shape), dt).ap()

    def sb(name, shape, dt):
        return st.enter_context(nc.sbuf_tensor(name, list(shape), dt))

    xp = din("xp", [NSEG, (ST + 1) * TOK, D])
    xs = din("xs", [2, 32, D])
    cT = din("cT", [128, 8, 3])
    cak = din("cak", [2, 512, 512]); cav = din("cav", [2, 512, 512])
    cbk = din("cbk", [2, 128, 128]); cbv = din("cbv", [2, 128, 128])
    cckv = din("cckv", [2, 1024, 256]); cckr = din("cckr", [2, 1024, 32])
    w_ada = din("w_ada", [2, D, 9 * D]); b_adaT = din("b_adaT", [128, 2, 72])
    gT = din("gT", [128, 7, 8])
    wg_d = din("ffn_w_gate", [2, 2, D, FF]); wu_d = din("ffn_w_up", [2, 2, D, FF]); wd_d = din("ffn_w_down", [2, 2, FF, D])
    w_in_ab = din("w_in_ab", [D, 2304]); w_out_ab = din("w_out_ab", [D, D])
    w_in_c = din("w_in_c", [D, 672]); w_qb = din("w_qb", [384, 1536]); w_kvb = din("w_kvb", [256, 2048]); w_out_c = din("w_out_c", [D, D])
    gq_d = din("gq", [128, 384]); gkv_d = din("gkv", [128, 256])
    FA_d = din("FA", [8, 128, 640]); FB_d = din("FB", [8, 128, 256])
    FAs_d = din("FAs", [8, 128, 5, 32]); FBs_d = din("FBs", [8, 128, 2, 32])
    sinks_d = din("sinks", [128, 8]); hv_d = din("hv", [128, 4])
    cosP = din("cosP", [NQT * 4, 128, 256]); sinP = din("sinP", [NQT * 4, 128, 256])
    cosS = din("cosS", [32, 256]); sinS = din("sinS", [32, 256])
    kaug_d = din("kaug", [NQT, 32, 9, TOK], BF16); kaugs_d = din("kaugs", [3, 9, TOK], BF16)
    qaug_d = din("qaug", [9, TOK], BF16)
    identf_d = din("identf", [128, 128]); identb_d = din("identb", [128, 128], BF16)

    y_p = dout("y_p", [NQT * TOK, D]); y_s = dout("y_s", [64, D])
    o_ak = dout("o_ak", [512, 512]); o_av = dout("o_av", [512, 512]); o_bk = dout("o_bk", [128, 128]); o_bv = dout("o_bv", [128, 128])
    o_ckv = dout("o_ckv", [NQT * TOK, 256]); o_ckr = dout("o_ckr", [NQT * TOK, 32])
    o_aks = dout("o_aks", [64, 512]); o_avs = dout("o_avs", [64, 512]); o_bks = dout("o_bks", [64, 128]); o_bvs = dout("o_bvs", [64, 128])
    o_ckvs = dout("o_ckvs", [64, 256]); o_ckrs = dout("o_ckrs", [64, 32])

    x_sp = dint("x_sp", [NQT + 2, 128, 8, TOK], F32)
    q_sp = dint("q_sp", [NQT + 2, 96, 16, TOK], BF16)
    NGC = NQT // 2
    g_in = [dint("g_in%d" % i, [288, 2 * TOK], BF16) for i in range(NGC)]
    g_all = [dint("g_all%d" % i, [4 * 288, 2 * TOK], BF16) for i in range(NGC)]
    g_s = dint("g_s", [2, 288, 3 * TOK], BF16)

    xT = sb("xT", [128, 8, TOK], F32)
    hT = sb("hT", [128, 8, TOK], BF16)
    act = sb("act", [128, 24 * TOK], BF16)
    actc = lambda c: act[:, c * TOK:(c + 1) * TOK]
    xin = act[:, 0:16 * TOK].bitcast(F32).rearrange("p (b d) -> p b d", d=D)
    sq = act[:, 16 * TOK:24 * TOK].rearrange("p (c t) -> p c t", t=TOK)
    yfin = act[:, 0:16 * TOK].bitcast(F32).rearrange("p (c t) -> p c t", t=TOK)
    scr = sb("scr", [128, 6, TOK], F32)
    qo = sb("qo", [128, 16, TOK], BF16)
    kTA = sb("kTA", [64, 2, 8, TOK], BF16)
    kTB = sb("kTB", [64, 2, 2, TOK], BF16)
    VA = sb("VA", [128, 2, 4, 8, 80], BF16)
    VB = sb("VB", [128, 2, 4, 2, 80], BF16)
    Sb = sb("Sb", [128, 2, TOK], F32)
    PT = sb("PT", [128, 3, TOK], BF16)
    fa = sb("fa", [128, 2, 640], F32)
    fb = sb("fb", [128, 2, 256], F32)
    NWP = 5
    wp = sb("wp", [128, NWP, 3072], BF16)
    M = sb("M", [128, 2, 9, 8, 3], F32)
    gTs = sb("gTs", [128, 7, 8], F32)
    scT = sb("scT", [128, 8, 3], F32)
    bT = sb("bT", [128, 2, 72], F32)
    identf = sb("identf_s", [128, 128], F32)
    identb = sb("identb_s", [128, 128], BF16)
    onesb = sb("onesb", [128, 128], BF16)
    onesf = sb("onesf", [128, 64], F32)
    zcol = sb("zcol", [128, 1], F32)
    sinkexp = sb("sinkexp", [128, 8], F32)
    hv = sb("hv_s", [128, 4], F32)
    gq = sb("gq_s", [128, 384], F32)
    gkv = sb("gkv_s", [128, 256], F32)
    den = sb("den", [128, TOK], F32)
    rb = sb("rb", [64, TOK], F32)
    small = sb("small", [128, 8], F32)
    qn_bf = sb("qn_bf", [128, 384], BF16)
    kvn_f = sb("kvn_f", [128, 256], F32)
    kvn_b = sb("kvn_b", [128, 288], BF16)
    kr_f = sb("kr_f", [128, 32], F32)
    q_bf = sb("q_bf", [128, 16, 96], BF16)
    qnT = sb("qnT", [128, 3, TOK], BF16)
    cs = sb("cs", [128, 2, 256], F32)
    rtmp = sb("rtmp", [128, 4, 256], F32)
    stg = sb("stg", [128, 2, 1024], F32)
    gT_sb = sb("gT_sb", [128, 3, TOK], BF16)
    VAflat = VA[:, :, :, :, :].rearrange("p a b c d -> p (a b c d)")
    KT = VAflat[0:105, 0:4 * TOK].rearrange("p (i t) -> p i t", t=TOK)
    Vh = VAflat[:, 4 * TOK:4 * TOK + 1280].rearrange("p (i b d) -> p i b d", b=4, d=80)
    qh = sb("qh", [105, 4, TOK], BF16)
    kvT = sb("kvT", [128, 2, 2, TOK], BF16)
    wkvb = sb("wkvb", [128, 2, 2048], BF16)
    cstage = scr[:, 0:4, :].rearrange("p a b -> p (a b)").rearrange("p (a b) -> p a b", b=256)

    ps = st.enter_context(nc.psum_tensor("ps", [128, 8 * 512], F32))
    bank = lambda i: ps[:, i * 512:(i + 1) * 512]
    Sbanks = Rot([0, 1]); Abanks = Rot([2, 3]); Gbanks = Rot([4, 5, 6, 7])
    wrot = Rot(range(NWP))

    def MM(out, lhsT, rhs, start, stop, R, W):
        S.op("pe", lambda e: e.matmul(out, lhsT, rhs, start=start, stop=stop), R, W)

    def TR(out, in_, ident, R, W):
        S.op("pe", lambda e: e.transpose(out, in_, ident), R, W)

    def ACT(out, in_, func, R, W, bias=None, scale=None, accum=None):
        kw = {}
        if bias is not None:
            kw["bias"] = bias
        if scale is not None:
            kw["scale"] = scale
        if accum is not None:
            kw["accum_out"] = accum
        S.op("act", lambda e: e.activation(out, in_, func, **kw), R, W)

    def TT(eng, out, in0, in1, op, R, W):
        S.op(eng, lambda e: e.tensor_tensor(out, in0, in1, op), R, W)

    def STT(eng, out, in0, scalar, in1, op0, op1, R, W):
        S.op(eng, lambda e: e.scalar_tensor_tensor(out, in0, scalar, in1, op0, op1), R, W)

    def TS(eng, out, in0, s1, s2, op0, op1, R, W):
        S.op(eng, lambda e: e.tensor_scalar(out, in0, s1, 0.0, op0, ALU.add), R, W)

    def CP(eng, out, in_, R, W):
        if eng == "act":
            S.op("act", lambda e: e.copy(out, in_), R, W)
        else:
            S.op(eng, lambda e: e.tensor_copy(out, in_), R, W)

    def RCP(out, in_, R, W):
        S.op("dve", lambda e: e.reciprocal(out, in_), R, W)

    def MSET(eng, ap, val, W):
        S.op(eng, lambda e: e.memset(ap, val), (), W)

    def DMA(q, out, in_, key, R, W, final=False):
        if type(out.tensor).__name__ == "DRamTensorHandle":
            q = "pool"
        S.dma(q, lambda e: e.dma_start(out=out, in_=in_), key, R, W, final=final)

    cp_rot = Rot(["act", "dve"])

    WREG = {}
    wscr_off = [0]
    wscr = dint("wscr", [48 * 1024 * 1024], BF16)

    def wreg(gkey, idx, src_ap):
        shp = list(src_ap.shape)
        n = 1
        for d in shp:
            n *= d
        off = wscr_off[0]
        wscr_off[0] += n
        flat = wscr[off:off + n]
        if len(shp) == 3:
            dst = flat.rearrange("(p a b) -> p a b", a=shp[1], b=shp[2])
            step = max(1, MAXDESC // shp[0])
            for a0 in range(0, shp[1], step):
                a1 = min(shp[1], a0 + step)
                DMA("pool", dst[:, a0:a1, :], src_ap[:, a0:a1, :], "cv_" + gkey, [], ["W_" + gkey])
        else:
            dst = flat.rearrange("(p a) -> p a", a=shp[1])
            DMA("pool", dst, src_ap, "cv_" + gkey, [], ["W_" + gkey])
        WREG[(gkey, idx)] = (flat.rearrange("(p a) -> p a", p=shp[0]), shp)

    def wpiece(gkey, idx):
        src2, shp = WREG[(gkey, idx)]
        i = wrot.next()
        n = src2.shape[1]
        assert n <= 3072, n
        dst = wp[0:shp[0], i, 0:n]
        DMA("sp", dst, src2, "wp%d" % i, ["W_" + gkey], ["wp%d" % i])
        if len(shp) == 3:
            dst = dst.rearrange("p (a b) -> p a b", b=shp[2])
        return dst, "wp%d" % i

    w_ab = w_in_ab.rearrange("(k p) n -> p k n", p=128)
    w_c = w_in_c.rearrange("(k p) n -> p k n", p=128)
    w_qbv = w_qb.rearrange("(k p) n -> p k n", p=128)

    def reg_ffn(l, i):
        wg = wg_d[l, i].rearrange("(k p) f -> p k f", p=128)
        wu = wu_d[l, i].rearrange("(k p) f -> p k f", p=128)
        wd = wd_d[l, i].rearrange("(f p) d -> p f d", p=128)
        for pc in range(11):
            wreg("wg%d%d" % (l, i), pc, wg[:, :, pc * 256:(pc + 1) * 256])
        for pc in range(11):
            wreg("wu%d%d" % (l, i), pc, wu[:, :, pc * 256:(pc + 1) * 256])
        for dc in range(8):
            wreg("wd%d%d" % (l, i), dc, wd[:, :, dc * 128:(dc + 1) * 128])

    def reg_all():
        reg_ffn(0, 0)
        for p0 in range(0, 2304, 256):
            wreg("wab", p0 // 256, w_ab[:, :, p0:p0 + 256])
        wv = w_out_ab.rearrange("(h p) d -> p h d", p=64)
        for dc in range(8):
            wreg("woab", dc, wv[:, :, dc * 128:(dc + 1) * 128])
        reg_ffn(0, 1)
        reg_ffn(1, 0)
        wreg("wc", 0, w_c[:, :, 0:384])
        wreg("wc", 1, w_c[:, :, 384:672])
        for cb in range(3):
            wreg("wqb", cb, w_qbv[:, :, cb * 512:(cb + 1) * 512])
        wv = w_out_c.rearrange("(h p) d -> p h d", p=64)
        for dc in range(8):
            wreg("woc", dc, wv[:, :, dc * 128:(dc + 1) * 128])
        reg_ffn(1, 1)

    reg_all()

    DMA("sp", identf[:], identf_d, "identf", [], ["identf"])
    DMA("sp", identb[:], identb_d, "identb", [], ["identb"])
    DMA("sp", gTs[:], gT, "gTs", [], ["gTs"])
    DMA("sp", bT[:], b_adaT, "bT", [], ["bT"])
    DMA("sp", scT[:], cT, "scT", [], ["scT"])
    DMA("sp", hv[:], hv_d, "hv", [], ["hv"])
    DMA("sp", gq[:], gq_d, "gq", [], ["gq"])
    DMA("sp", gkv[:], gkv_d, "gkv", [], ["gkv"])
    DMA("sp", sinkexp[:], sinks_d, "sinkexp", [], ["sinkexp"])
    MSET("dve", onesb[:], 1.0, ["onesb"])
    MSET("dve", onesf[:], 1.0, ["onesf"])
    MSET("dve", zcol[:], 0.0, ["zcol"])
    MSET("dve", VA[:, :, :, :, 64:65], 1.0, ["VAones"])
    MSET("dve", VB[:, :, :, :, 64:65], 1.0, ["VBones"])
    ACT(sinkexp[:], sinkexp[:], AF.Exp, ["sinkexp"], ["sinkexp"])
    ACT(scT[:], scT[:], AF.Silu, ["scT"], ["scT"])
    for hh in range(4):
        DMA("sp", qh[96:105, hh, :], qaug_d, "qhaug", [], ["qhaug"])
    DMA("pool", wkvb[:], w_kvb.rearrange("(c p) n -> p c n", p=128), "wkvb", [], ["wkvb"])

    for l in range(2):
        for m in range(9):
            for half in range(2):
                if (m * 2 + half) % 2 == 0:
                    wst = act[:, 0:16 * TOK].bitcast(F32).rearrange("p (k n) -> p k n", n=512)
                    wkeys = ["act%d" % c for c in range(16)]; wsk = "wstA"
                else:
                    wst = qo[:, :, :].rearrange("p a b -> p (a b)").bitcast(F32).rearrange("p (k n) -> p k n", n=512)
                    wkeys = ["qo%d" % c for c in range(16)]; wsk = "wstB"
                src = w_ada[l].rearrange("(k p) n -> p k n", p=128)[:, :, m * 1024 + half * 512: m * 1024 + half * 512 + 512]
                DMA("sp", wst, src, wsk, [], wkeys)
                bk = Gbanks.next()
                for oc in range(4):
                    for k in range(8):
                        MM(bank(bk)[:, oc * 4:oc * 4 + 3], wst[:, k, oc * 128:(oc + 1) * 128], scT[:, k, :], k == 0, k == 7,
                           wkeys + ["scT"], ["ps%d" % bk])
                for oc in range(4):
                    ch = half * 4 + oc
                    TS("dve", M[:, l, m, ch, :], bank(bk)[:, oc * 4:oc * 4 + 3], bT[:, l, m * 8 + ch:m * 8 + ch + 1], None, ALU.add, None,
                       ["ps%d" % bk, "bT"], ["M"])
    for l in range(2):
        for n in range(3):
            for si in range(3):
                STT("dve", M[:, l, 3 * n + 1, :, si], M[:, l, 3 * n + 1, :, si], 1.0, gTs[:, 3 * l + n, :], ALU.add, ALU.mult, ["M", "gTs"], ["M"])
        for m in (2, 8):
            TS("dve", M[:, l, m, :, :], M[:, l, m, :, :], 0.5, None, ALU.mult, None, ["M"], ["M"])

    def norm_mod(ncol, scale_of, shift_of, out_of, out_keys):
        ACT(sq[:, :, 0:ncol], xT[:, :, 0:ncol], AF.Square, ["xT"], ["act%d" % (16 + c) for c in range(8)])
        bk = Gbanks.next()
        for c in range(8):
            MM(bank(bk)[:, 0:ncol], onesb[:], sq[:, c, 0:ncol], c == 0, c == 7, ["onesb", "act%d" % (16 + c)], ["ps%d" % bk])
        ACT(scr[:, 0, 0:ncol], bank(bk)[:, 0:ncol], AF.Sqrt, ["ps%d" % bk], ["scr0"], bias=epscol[:], scale=1.0 / D)
        RCP(scr[:, 1, 0:ncol], scr[:, 0, 0:ncol], ["scr0"], ["scr1"])
        for c in range(8):
            tb_ = 2 + (c % 2)
            TT("dve", scr[:, tb_, 0:ncol], xT[:, c, 0:ncol], scr[:, 1, 0:ncol], ALU.mult, ["xT", "scr1"], ["scr%d" % tb_])
            kw = dict(scale=scale_of(c))
            sh = shift_of(c)
            ACT(out_of(c), scr[:, tb_, 0:ncol], AF.Identity, ["scr%d" % tb_, "M", "gTs"], out_keys(c), bias=(sh if sh is not None else zcol[:]), **kw)

    def ffn(l, i, ncol, si, gate_m):
        for pc in range(11):
            g_ap, gk = wpiece("wg%d%d" % (l, i), pc)
            u_ap, uk = wpiece("wu%d%d" % (l, i), pc)
            for sub in range(2):
                fc = pc * 2 + sub
                bg = Gbanks.next(); bu = Gbanks.next()
                for k in range(8):
                    MM(bank(bg)[:, 0:ncol], g_ap[:, k, sub * 128:(sub + 1) * 128], hT[:, k, 0:ncol], k == 0, k == 7, [gk, "hT"], ["ps%d" % bg])
                for k in range(8):
                    MM(bank(bu)[:, 0:ncol], u_ap[:, k, sub * 128:(sub + 1) * 128], hT[:, k, 0:ncol], k == 0, k == 7, [uk, "hT"], ["ps%d" % bu])
                sgi = 4 + (fc % 2)
                ACT(scr[:, sgi, 0:ncol], bank(bg)[:, 0:ncol], AF.Silu, ["ps%d" % bg], ["scr%d" % sgi])
                TT("dve", actc(fc)[:, 0:ncol], bank(bu)[:, 0:ncol], scr[:, sgi, 0:ncol], ALU.mult, ["ps%d" % bu, "scr%d" % sgi], ["act%d" % fc])
        for dc in range(8):
            d_ap, dk = wpiece("wd%d%d" % (l, i), dc)
            by = Gbanks.next()
            for fc in range(NFC):
                MM(bank(by)[:, 0:ncol], d_ap[:, fc, :], actc(fc)[:, 0:ncol], fc == 0, fc == NFC - 1, [dk, "act%d" % fc], ["ps%d" % by])
            STT("dve", xT[:, dc, 0:ncol], bank(by)[:, 0:ncol], M[:, l, gate_m, dc, si:si + 1], xT[:, dc, 0:ncol], ALU.mult, ALU.add,
                ["ps%d" % by, "M", "xT"], ["xT"])

    def mod_norm(l, n, ncol, si):
        norm_mod(ncol,
                 lambda c: M[:, l, 3 * n + 1, c, si:si + 1],
                 lambda c: M[:, l, 3 * n + 0, c, si:si + 1],
                 lambda c: hT[:, c, 0:ncol],
                 lambda c: ["hT"])

    def attn_block(kT_ap, q_ap, nk, ncols, scale, bias_ap, pbias_ap, v_ap, acc_ap, first, R, accW, last=False):
        bs = Sbanks.next()
        MM(bank(bs)[0:nk, 0:ncols], kT_ap, q_ap, True, True, R, ["ps%d" % bs])
        pi = PTrot.next()
        if bias_ap is not None:
            si_ = Sbrot.next()
            STT("dve", Sb[0:nk, si_, 0:ncols], bank(bs)[0:nk, 0:ncols], scale, bias_ap, ALU.mult, ALU.add, ["ps%d" % bs] + R, ["Sb%d" % si_])
            ACT(PT[0:nk, pi, 0:ncols], Sb[0:nk, si_, 0:ncols], AF.Exp, ["Sb%d" % si_, "hv"], ["PT%d" % pi], bias=pbias_ap)
        else:
            ACT(PT[0:nk, pi, 0:ncols], bank(bs)[0:nk, 0:ncols], AF.Exp, ["ps%d" % bs], ["PT%d" % pi], scale=scale)
        MM(acc_ap, v_ap, PT[0:nk, pi, 0:ncols], first, last, ["PT%d" % pi] + R, accW)

    def attn_finish(ba, ncol, dest_ap, destW, sink_ap=None):
        if sink_ap is not None:
            TS("dve", den[64:65, 0:ncol], bank(ba)[64:65, 0:ncol], sink_ap, None, ALU.add, None, ["ps%d" % ba, "sinkexp"], ["den"])
        else:
            CP("dve", den[64:65, 0:ncol], bank(ba)[64:65, 0:ncol], ["ps%d" % ba], ["den"])
        bb = Gbanks.next()
        MM(bank(bb)[0:64, 0:ncol], onesf[64:65, 0:64], den[64:65, 0:ncol], True, True, ["onesf", "den"], ["ps%d" % bb])
        RCP(rb[:, 0:ncol], bank(bb)[0:64, 0:ncol], ["ps%d" % bb], ["rb"])
        TT("dve", dest_ap, bank(ba)[0:64, 0:ncol], rb[:, 0:ncol], ALU.mult, ["ps%d" % ba, "rb"], destW)

    CS_ALL = ["scr0", "scr1", "scr2", "scr3"]
    PTrot = Rot(range(3)); Sbrot = Rot(range(2)); farot = Rot(range(2)); fbrot = Rot(range(2)); stgrot = Rot(range(2))
    epscol = sb("epscol", [128, 1], F32)
    MSET("dve", epscol[:], EPS, ["epscol"])

    def l0_mixer(kind, ncol, si, slot, tblocks, seg, is_last, samp_idx):
        do_q = kind != "halo"
        plan = []
        if do_q:
            plan += [("q", h, 0 + 64 * h) for h in range(8)]
            plan += [("q", 8 + h, 1536 + 64 * h) for h in range(8)]
        plan += [("ka", h, 512 + 64 * h) for h in range(8)]
        plan += [("kb", h, 2048 + 64 * h) for h in range(2)]
        cur_piece = None
        for (typ, h, col) in plan:
            p0 = (col // 256) * 256
            if cur_piece is None or cur_piece[0] != p0:
                ap_, k_ = wpiece("wab", p0 // 256)
                cur_piece = (p0, ap_, k_)
            _, w_ap, wk = cur_piece
            bk = Gbanks.next()
            for k in range(8):
                MM(bank(bk)[0:64, 0:ncol], w_ap[:, k, col - p0:col - p0 + 64], hT[:, k, 0:ncol], k == 0, k == 7, [wk, "hT"], ["ps%d" % bk])
            if typ == "q":
                dst, W = qo[0:64, h, 0:ncol], ["qo%d" % h]
            elif typ == "ka":
                dst, W = kTA[:, slot, h, 0:ncol], ["kTA%d" % slot]
            else:
                dst, W = kTB[:, slot, h, 0:ncol], ["kTB%d" % slot]
            CP(cp_rot.next(), dst, bank(bk)[0:64, 0:ncol], ["ps%d" % bk], W)
        if SUB <= 4.2:
            return
        want_out = (is_last or kind == "samp") and SUB != 4.41
        for (tbi, (t0, nt)) in enumerate(tblocks):
            jobs = [("va", 1024, 512)]
            jobs += [("vb", 2176, 128)]
            if want_out:
                jobs += [("ka_o", 512, 512), ("kb_o", 2048, 128)]
            for (typ, col, wdt) in jobs:
                bk = Gbanks.next()
                for half in range(wdt // 256 if wdt >= 256 else 1):
                    cw = min(256, wdt)
                    p0 = ((col + half * 256) // 256) * 256
                    w_ap, wk = wpiece("wab", p0 // 256)
                    off = col + half * 256 - p0
                    for k in range(8):
                        MM(bank(bk)[0:nt, half * 256:half * 256 + cw], hT[:, k, t0:t0 + nt], w_ap[:, k, off:off + cw], k == 0, k == 7,
                           [wk, "hT"], ["ps%d" % bk])
                if typ == "va":
                    CP("act", VA[0:nt, slot, tbi, :, 0:64], bank(bk)[0:nt, 0:512].rearrange("p (h d) -> p h d", d=64), ["ps%d" % bk], ["VA%d" % slot])
                elif typ == "vb":
                    CP("dve", VB[0:nt, slot, tbi, :, 0:64], bank(bk)[0:nt, 0:128].rearrange("p (h d) -> p h d", d=64), ["ps%d" % bk], ["VB%d" % slot])
                if want_out:
                    sgi = stgrot.next()
                    CP("act" if typ in ("va", "ka_o") else "dve", stg[0:nt, sgi, 0:wdt], bank(bk)[0:nt, 0:wdt], ["ps%d" % bk], ["stg%d" % sgi])
                    if SUB == 4.42:
                        continue
                    if kind == "samp":
                        r0 = samp_idx * 32
                        dst = {"va": o_avs, "vb": o_bvs, "ka_o": o_aks, "kb_o": o_bks}[typ][r0:r0 + 32, :]
                        DMA(OUTQ, dst, stg[0:nt, sgi, 0:wdt], "stg%d" % sgi, ["stg%d" % sgi], [], final=True)
                    else:
                        if typ in ("va", "ka_o"):
                            dst = (o_av if typ == "va" else o_ak)[t0:t0 + nt, :]
                            DMA(OUTQ, dst, stg[0:nt, sgi, 0:wdt], "stg%d" % sgi, ["stg%d" % sgi], [], final=True)
                        elif tbi == 3:
                            dst = (o_bv if typ == "vb" else o_bk)[:, :]
                            DMA(OUTQ, dst, stg[0:nt, sgi, 0:wdt], "stg%d" % sgi, ["stg%d" % sgi], [], final=True)
        if not do_q or SUB <= 4.42:
            return
        if kind == "prompt":
            prev = 1 - slot
            for h in range(8):
                fi = farot.next()
                DMA("sp", fa[:, fi, :], FA_d[h], "fa%d" % fi, [], ["fa%d" % fi])
                ba = Abanks.next()
                first = True
                for m in (3, 4, 0, 1, 2, 5, 6, 7):
                    q0 = max(0, 128 * m - 512); q1 = min(512, 128 * m + 128)
                    sl = prev if m < 4 else slot
                    blk = m % 4
                    pb = hv[:, seg:seg + 1] if (m < 4) else zcol[:]
                    if m < 4 and not first_own[0]:
                        pb = zcol[:]
                    attn_block(kTA[:, sl, h, blk * 128:(blk + 1) * 128], qo[0:64, h, q0:q1], 128, q1 - q0, 0.125,
                               fa[:, fi, q0 - 128 * m + 512:q1 - 128 * m + 512], pb, VA[:, sl, blk, h, 0:65], bank(ba)[0:65, q0:q1], first,
                               ["kTA%d" % sl, "qo%d" % h, "fa%d" % fi, "VA%d" % sl, "VAones"], ["ps%d" % ba], last=(m == 7))
                    first = False
                attn_finish(ba, 512, qo[0:64, h, :], ["qo%d" % h])
            for h in range(8):
                fi = fbrot.next()
                DMA("sp", fb[:, fi, :], FB_d[h], "fb%d" % fi, [], ["fb%d" % fi])
                ba = Abanks.next()
                g = h // 4
                first = True
                for m in (4, 6, 3, 5, 7):
                    q0 = max(0, 128 * m - 512); q1 = min(512, 128 * m - 256)
                    sl = prev if m < 4 else slot
                    blk = m % 4
                    pb = hv[:, seg:seg + 1] if (m < 4 and first_own[0]) else zcol[:]
                    attn_block(kTB[:, sl, g, blk * 128:(blk + 1) * 128], qo[0:64, 8 + h, q0:q1], 128, q1 - q0, 0.125,
                               fb[:, fi, q0 - 128 * m + 512:q1 - 128 * m + 512], pb, VB[:, sl, blk, g, 0:65], bank(ba)[0:65, q0:q1], first,
                               ["kTB%d" % sl, "qo%d" % (8 + h), "fb%d" % fi, "VB%d" % sl, "VBones"], ["ps%d" % ba], last=(m == 7))
                    first = False
                attn_finish(ba, 512, qo[0:64, 8 + h, :], ["qo%d" % (8 + h)], sink_ap=sinkexp[64:65, h:h + 1])
        else:
            s_ = samp_idx
            csl = 1 - slot
            ck = cstage[:, 0:8, :].rearrange("p a b -> p (a b)")[:, 0:2048].rearrange("p (b n) -> p b n", n=512)
            DMA("sp", ck, cak[s_].rearrange("(b p) n -> p b n", p=128), "cstage", [], CS_ALL)
            for blk in range(4):
                for h in range(8):
                    bk = Gbanks.next()
                    TR(bank(bk)[0:64, 0:128], ck[:, blk, h * 64:(h + 1) * 64], identf[:], CS_ALL + ["identf"], ["ps%d" % bk])
                    CP(cp_rot.next(), kTA[:, csl, h, blk * 128:(blk + 1) * 128], bank(bk)[0:64, 0:128], ["ps%d" % bk], ["kTA%d" % csl])
            for blk in range(4):
                for hh in range(2):
                    DMA("pool", VA[:, csl, blk, hh * 4:(hh + 1) * 4, 0:64],
                        cav[s_, blk * 128:(blk + 1) * 128, hh * 256:(hh + 1) * 256].rearrange("p (h d) -> p h d", d=64), "VA%d" % csl, [], ["VA%d" % csl])
            for h in range(8):
                fi = farot.next()
                DMA("sp", fa[:, fi, 0:160].rearrange("p (b q) -> p b q", q=32), FAs_d[h], "fa%d" % fi, [], ["fa%d" % fi])
                fav = fa[:, fi, 0:160].rearrange("p (b q) -> p b q", q=32)
                ba = Abanks.next()
                for blk in range(5):
                    if blk < 4:
                        kT_ap = kTA[:, csl, h, blk * 128:(blk + 1) * 128]; v_ap = VA[:, csl, blk, h, 0:65]; nk = 128
                        R = ["kTA%d" % csl, "VA%d" % csl]
                    else:
                        kT_ap = kTA[:, slot, h, 0:32]; v_ap = VA[0:32, slot, 0, h, 0:65]; nk = 32
                        R = ["kTA%d" % slot, "VA%d" % slot]
                    attn_block(kT_ap, qo[0:64, h, 0:32], nk, 32, 0.125, fav[0:nk, blk, :], zcol[0:nk, :], v_ap, bank(ba)[0:65, 0:32], blk == 0,
                               R + ["qo%d" % h, "fa%d" % fi, "VAones"], ["ps%d" % ba], last=(blk == 4))
                attn_finish(ba, 32, qo[0:64, h, 0:32], ["qo%d" % h])
            if SUB <= 4.6:
                return
            ckb = cstage[:, 0, 0:128]
            DMA("sp", ckb, cbk[s_], "cstage", [], CS_ALL)
            for g in range(2):
                bk = Gbanks.next()
                TR(bank(bk)[0:64, 0:128], ckb[:, g * 64:(g + 1) * 64], identf[:], CS_ALL + ["identf"], ["ps%d" % bk])
                CP(cp_rot.next(), kTB[:, csl, g, 0:128], bank(bk)[0:64, 0:128], ["ps%d" % bk], ["kTB%d" % csl])
            DMA("pool", VB[:, csl, 0, :, 0:64], cbv[s_].rearrange("p (h d) -> p h d", d=64), "VB%d" % csl, [], ["VB%d" % csl])
            for h in range(8):
                fi = fbrot.next()
                DMA("sp", fb[:, fi, 0:64].rearrange("p (b q) -> p b q", q=32), FBs_d[h], "fb%d" % fi, [], ["fb%d" % fi])
                fbv = fb[:, fi, 0:64].rearrange("p (b q) -> p b q", q=32)
                ba = Abanks.next()
                g = h // 4
                for blk in range(2):
                    if blk == 0:
                        kT_ap = kTB[:, csl, g, 0:128]; v_ap = VB[:, csl, 0, g, 0:65]; nk = 128; R = ["kTB%d" % csl, "VB%d" % csl]
                    else:
                        kT_ap = kTB[:, slot, g, 0:32]; v_ap = VB[0:32, slot, 0, g, 0:65]; nk = 32; R = ["kTB%d" % slot, "VB%d" % slot]
                    attn_block(kT_ap, qo[0:64, 8 + h, 0:32], nk, 32, 0.125, fbv[0:nk, blk, :], zcol[0:nk, :], v_ap, bank(ba)[0:65, 0:32], blk == 0,
                               R + ["qo%d" % (8 + h), "fb%d" % fi, "VBones"], ["ps%d" % ba], last=(blk == 1))
                attn_finish(ba, 32, qo[0:64, 8 + h, 0:32], ["qo%d" % (8 + h)], sink_ap=sinkexp[64:65, h:h + 1])

    def out_proj(w_dram, ncol, l, si, gate_m):
        for dc in range(8):
            w_ap, wk = wpiece(w_dram, dc)
            by = Gbanks.next()
            for h in range(16):
                MM(bank(by)[:, 0:ncol], w_ap[:, h, :], qo[0:64, h, 0:ncol], h == 0, h == 15, [wk, "qo%d" % h], ["ps%d" % by])
            STT("dve", xT[:, dc, 0:ncol], bank(by)[:, 0:ncol], M[:, l, gate_m, dc, si:si + 1], xT[:, dc, 0:ncol], ALU.mult, ALU.add,
                ["ps%d" % by, "M", "xT"], ["xT"])

    def l1_prep(kind, ncol, tblocks, tile_idx, samp_idx):
        for (tbi, (t0, nt)) in enumerate(tblocks):
            wq_ap, wqk = wpiece("wc", 0)
            bq = Gbanks.next()
            for k in range(8):
                MM(bank(bq)[0:nt, 0:384], hT[:, k, t0:t0 + nt], wq_ap[:, k, :], k == 0, k == 7, [wqk, "hT"], ["ps%d" % bq])
            wk_ap, wkk = wpiece("wc", 1)
            bk2 = Gbanks.next()
            for k in range(8):
                MM(bank(bk2)[0:nt, 0:288], hT[:, k, t0:t0 + nt], wk_ap[:, k, :], k == 0, k == 7, [wkk, "hT"], ["ps%d" % bk2])
            ACT(scr[0:nt, 2, 0:384], bank(bq)[0:nt, 0:384], AF.Square,
                ["ps%d" % bq], ["scr2", "small"], accum=small[0:nt, 0:1])
            ACT(small[0:nt, 1:2], small[0:nt, 0:1], AF.Sqrt, ["small"], ["small"], bias=epscol[0:nt, :], scale=1.0 / 384)
            RCP(small[0:nt, 2:3], small[0:nt, 1:2], ["small"], ["small"])
            STT("dve", qn_bf[0:nt, :], bank(bq)[0:nt, 0:384], small[0:nt, 2:3], gq[0:nt, :], ALU.mult, ALU.mult, ["ps%d" % bq, "small", "gq"], ["qn_bf"])
            ACT(scr[0:nt, 3, 0:256], bank(bk2)[0:nt, 0:256], AF.Square, ["ps%d" % bk2], ["scr3", "small"], accum=small[0:nt, 3:4])
            ACT(small[0:nt, 4:5], small[0:nt, 3:4], AF.Sqrt, ["small"], ["small"], bias=epscol[0:nt, :], scale=1.0 / 256)
            RCP(small[0:nt, 5:6], small[0:nt, 4:5], ["small"], ["small"])
            STT("dve", kvn_f[0:nt, :], bank(bk2)[0:nt, 0:256], small[0:nt, 5:6], gkv[0:nt, :], ALU.mult, ALU.mult, ["ps%d" % bk2, "small", "gkv"], ["kvn_f"])
            CP("act", kvn_b[0:nt, 0:256], kvn_f[0:nt, :], ["kvn_f"], ["kvn_b"])
            if kind == "samp":
                DMA("sp", cs[0:nt, 0, :], cosS, "cs", [], ["cs"])
                DMA("sp", cs[0:nt, 1, :], sinS, "cs", [], ["cs"])
            else:
                DMA("sp", cs[:, 0, :], cosP[tile_idx * 4 + tbi], "cs", [], ["cs"])
                DMA("sp", cs[:, 1, :], sinP[tile_idx * 4 + tbi], "cs", [], ["cs"])
            x1 = bank(bk2)[0:nt, 256:272]; x2 = bank(bk2)[0:nt, 272:288]
            c16 = cs[0:nt, 0, 0:16]; s16 = cs[0:nt, 1, 0:16]
            TT("dve", rtmp[0:nt, 0, 0:16], x1, c16, ALU.mult, ["ps%d" % bk2, "cs"], ["rtmp0"])
            TT("dve", rtmp[0:nt, 1, 0:16], x2, s16, ALU.mult, ["ps%d" % bk2, "cs"], ["rtmp1"])
            TT("dve", kr_f[0:nt, 0:16], rtmp[0:nt, 0, 0:16], rtmp[0:nt, 1, 0:16], ALU.subtract, ["rtmp0", "rtmp1"], ["kr_f"])
            TT("dve", rtmp[0:nt, 2, 0:16], x1, s16, ALU.mult, ["ps%d" % bk2, "cs"], ["rtmp2"])
            TT("dve", rtmp[0:nt, 3, 0:16], x2, c16, ALU.mult, ["ps%d" % bk2, "cs"], ["rtmp3"])
            TT("dve", kr_f[0:nt, 16:32], rtmp[0:nt, 2, 0:16], rtmp[0:nt, 3, 0:16], ALU.add, ["rtmp2", "rtmp3"], ["kr_f"])
            CP("act", kvn_b[0:nt, 256:288], kr_f[0:nt, :], ["kr_f"], ["kvn_b"])
            if kind == "samp":
                r0 = samp_idx * 32
                DMA("sp", o_ckvs[r0:r0 + 32, :], kvn_f[0:nt, :], "kvn_f", ["kvn_f"], [], final=True)
                DMA("sp", o_ckrs[r0:r0 + 32, :], kr_f[0:nt, :], "kr_f", ["kr_f"], [], final=True)
            else:
                r0 = tile_idx * TOK + t0
                DMA("sp", o_ckv[r0:r0 + nt, :], kvn_f[0:nt, :], "kvn_f", ["kvn_f"], [], final=True)
                DMA("sp", o_ckr[r0:r0 + nt, :], kr_f[0:nt, :], "kr_f", ["kr_f"], [], final=True)
            bt = Gbanks.next()
            btv = bank(bt).bitcast(BF16)
            for c in range(3):
                w_ = 128 if c < 2 else 32
                TR(btv[0:w_, c * 128:c * 128 + nt], kvn_b[0:nt, c * 128:c * 128 + w_], identb[0:nt, 0:nt], ["kvn_b", "identb"], ["ps%d" % bt])
            CP("dve", gT_sb[:, 0:2, t0:t0 + nt], btv[:, 0:256].rearrange("p (c t) -> p c t", t=128)[:, :, 0:nt], ["ps%d" % bt], ["gT_sb"])
            CP("dve", gT_sb[0:32, 2, t0:t0 + nt], btv[0:32, 256:256 + nt], ["ps%d" % bt], ["gT_sb"])
            bt2 = Gbanks.next()
            bt2v = bank(bt2).bitcast(BF16)
            for c in range(3):
                TR(bt2v[:, c * 128:c * 128 + nt], qn_bf[0:nt, c * 128:(c + 1) * 128], identb[0:nt, 0:nt], ["qn_bf", "identb"], ["ps%d" % bt2])
            CP("act", qnT[:, :, t0:t0 + nt], bt2v[:, 0:384].rearrange("p (c t) -> p c t", t=128)[:, :, 0:nt], ["ps%d" % bt2], ["qnT"])
            for cb in range(3):
                wq2, wq2k = wpiece("wqb", cb)
                bqq = Gbanks.next()
                for k in range(3):
                    MM(bank(bqq)[0:nt, 0:512], qnT[:, k, t0:t0 + nt], wq2[:, k, :], k == 0, k == 2, [wq2k, "qnT"], ["ps%d" % bqq])
                CP(cp_rot.next(), qraw[0:nt, cb * 512:(cb + 1) * 512],
                   bank(bqq)[0:nt, 0:512], ["ps%d" % bqq], ["stg0", "stg1"])
            qv = qraw[0:nt, :].rearrange("p (h d) -> p h d", d=96)
            CP("act", q_bf[0:nt, :, 0:64], qv[:, :, 0:64], ["stg0", "stg1"], ["q_bf"])
            cosv = cs[0:nt, 0, :].rearrange("p (h d) -> p h d", d=16); sinv = cs[0:nt, 1, :].rearrange("p (h d) -> p h d", d=16)
            r3 = lambda i: rtmp[0:nt, i, :].rearrange("p (h d) -> p h d", d=16)
            TT("dve", r3(0), qv[:, :, 64:80], cosv, ALU.mult, ["stg0", "stg1", "cs"], ["rtmp0"])
            TT("dve", r3(1), qv[:, :, 80:96], sinv, ALU.mult, ["stg0", "stg1", "cs"], ["rtmp1"])
            TT("dve", q_bf[0:nt, :, 64:80], r3(0), r3(1), ALU.subtract, ["rtmp0", "rtmp1"], ["q_bf"])
            TT("dve", r3(2), qv[:, :, 64:80], sinv, ALU.mult, ["stg0", "stg1", "cs"], ["rtmp2"])
            TT("dve", r3(3), qv[:, :, 80:96], cosv, ALU.mult, ["stg0", "stg1", "cs"], ["rtmp3"])
            TT("dve", q_bf[0:nt, :, 80:96], r3(2), r3(3), ALU.add, ["rtmp2", "rtmp3"], ["q_bf"])
            for hg in range(2):
                bt3 = Gbanks.next()
                bt3v = bank(bt3).bitcast(BF16)
                for hh in range(8):
                    h = hg * 8 + hh
                    TR(bt3v[0:96, hh * 128:hh * 128 + nt], q_bf[0:nt, h, :], identb[0:nt, 0:nt], ["q_bf", "identb"], ["ps%d" % bt3])
                CP(cp_rot.next(), qo[0:96, hg * 8:(hg + 1) * 8, t0:t0 + nt], bt3v[0:96, :].rearrange("p (h t) -> p h t", t=128)[:, :, 0:nt],
                   ["ps%d" % bt3], ["qo%d" % h_ for h_ in range(hg * 8, hg * 8 + 8)])

    qraw = stg[:, :, :].rearrange("p a b -> p (a b)")[:, 0:1536]
    first_own = [False]

    def phase1_tile(kind, ncol, si, slot, seg, a, tile_idx, samp_idx):
        tblocks = [(i * 128, 128) for i in range(4)] if kind != "samp" else [(0, 32)]
        if kind == "samp":
            DMA("sp", xin[0:32, 0, :], xs[samp_idx], "xin", [], ["act%d" % c for c in range(4)])
        else:
            r0 = (a + 1) * TOK if kind == "prompt" else 0
            DMA("sp", xin[:, :, :], xp[seg, r0:r0 + TOK, :].rearrange("(b p) d -> p b d", p=128), "xin", [], ["act%d" % c for c in range(16)])
        for (tbi, (t0, nt)) in enumerate(tblocks):
            for c2 in range(2):
                bk = Gbanks.next()
                for cc in range(4):
                    c = c2 * 4 + cc
                    TR(bank(bk)[:, cc * 128:cc * 128 + nt], xin[0:nt, tbi, c * 128:(c + 1) * 128], identf[0:nt, 0:nt],
                       ["act%d" % c_ for c_ in range(16)] + ["identf"], ["ps%d" % bk])
                CP(cp_rot.next(), xT[:, c2 * 4:(c2 + 1) * 4, t0:t0 + nt], bank(bk).rearrange("p (c t) -> p c t", t=128)[:, :, 0:nt], ["ps%d" % bk], ["xT"])
        if SUB <= 1:
            return
        mod_norm(0, 0, ncol, si)
        if SUB <= 2:
            return
        ffn(0, 0, ncol, si, 2)
        if SUB <= 3:
            return
        mod_norm(0, 1, ncol, si)
        if SUB <= 4:
            return
        l0_mixer(kind, ncol, si, slot, tblocks, seg, kind == "prompt" and seg == 1 and a == ST - 1, samp_idx)
        if kind == "halo" or SUB <= 5:
            return
        out_proj("woab", ncol, 0, si, 5)
        mod_norm(0, 2, ncol, si)
        ffn(0, 1, ncol, si, 8)
        if SUB <= 6:
            return
        mod_norm(1, 0, ncol, si)
        ffn(1, 0, ncol, si, 2)
        mod_norm(1, 1, ncol, si)
        if SUB <= 7:
            return
        l1_prep(kind, ncol, tblocks, tile_idx, samp_idx)
        if SUB <= 8:
            return
        sp_i = tile_idx if kind == "prompt" else NQT + samp_idx
        DMA("sp", x_sp[sp_i][:, :, 0:ncol], xT[:, :, 0:ncol], "xT", ["xT"], ["x_sp%d" % sp_i])
        DMA("sp", q_sp[sp_i][:, :, 0:ncol], qo[0:96, :, 0:ncol], "qo", ["qo%d" % h for h in range(16)], ["q_sp%d" % sp_i])
        if kind == "prompt":
            gi_ = g_in[tile_idx // 2]; c0_ = (tile_idx % 2) * TOK
            DMA("sp", gi_[0:256, c0_:c0_ + TOK].rearrange("(c p) t -> p c t", p=128), gT_sb[:, 0:2, :], "gT_sb", ["gT_sb"], ["g_in%d" % (tile_idx // 2)])
            DMA("sp", gi_[256:288, c0_:c0_ + TOK], gT_sb[0:32, 2, :], "gT_sb", ["gT_sb"], ["g_in%d" % (tile_idx // 2)])
        else:
            DMA("sp", g_s[samp_idx, 0:256, 1024:1056].rearrange("(c p) t -> p c t", p=128), gT_sb[:, 0:2, 0:32], "gT_sb", ["gT_sb"], ["g_s%d" % samp_idx])
            DMA("sp", g_s[samp_idx, 256:288, 1024:1056], gT_sb[0:32, 2, 0:32], "gT_sb", ["gT_sb"], ["g_s%d" % samp_idx])

    def sample_cache_latents(s_):
        for half in range(2):
            DMA("sp", cstage[:, 0:4, :], cckv[s_, half * 512:(half + 1) * 512, :].rearrange("(b p) n -> p b n", p=128), "cstage", [], CS_ALL)
            for c in range(2):
                bk = Gbanks.next()
                for blk in range(4):
                    TR(bank(bk)[:, blk * 128:(blk + 1) * 128], cstage[:, blk, c * 128:(c + 1) * 128], identf[:], CS_ALL + ["identf"], ["ps%d" % bk])
                CP(cp_rot.next(), gT_sb[:, c, :], bank(bk), ["ps%d" % bk], ["gT_sb"])
            DMA("sp", g_s[s_, 0:256, half * 512:(half + 1) * 512].rearrange("(c p) t -> p c t", p=128), gT_sb[:, 0:2, :], "gT_sb", ["gT_sb"], ["g_s%d" % s_])
        for half in range(2):
            crv = cstage[:, 4, :].rearrange("p (b n) -> p b n", n=32)[:, 0:4, :]
            DMA("sp", crv, cckr[s_, half * 512:(half + 1) * 512, :].rearrange("(b p) n -> p b n", p=128), "cstage4", [], ["scr2"])
            bk = Gbanks.next()
            for blk in range(4):
                TR(bank(bk)[0:32, blk * 128:(blk + 1) * 128], crv[:, blk, :], identf[:], ["scr2", "identf"], ["ps%d" % bk])
            CP(cp_rot.next(), gT_sb[0:32, 2, :], bank(bk)[0:32, :], ["ps%d" % bk], ["gT_sb"])
            DMA("sp", g_s[s_, 256:288, half * 512:(half + 1) * 512], gT_sb[0:32, 2, :], "gT_sb", ["gT_sb"], ["g_s%d" % s_])

    KTrot = Rot(range(4)); kvTrot = Rot(range(2))

    def phase2_tile(kind, ncol, si, sp_i, key_tiles, out_ap_of):
        DMA("sp", xT[:, :, 0:ncol], x_sp[sp_i][:, :, 0:ncol], "xT", ["x_sp%d" % sp_i], ["xT"])
        SB = [0, 1, 4, 5]
        nv = len(key_tiles)
        for hp in range(8):
            par = hp % 2
            qkey = "qh%d" % par
            DMA("sp", qh[0:96, 2 * par:2 * par + 2, 0:ncol], q_sp[sp_i][:, hp * 2:hp * 2 + 2, 0:ncol], qkey, ["q_sp%d" % sp_i], [qkey])
            bas = [2, 3]
            units = [(vi, hh) for vi in range(nv) for hh in range(2)]
            blocks = [(ui, blk) for ui in range(len(units)) for blk in range(4)]
            nb = len(blocks)
            kt_of = {}

            def stageA(ui):
                vi, hh = units[ui]
                t_, kaug_ap = key_tiles[vi]
                h = hp * 2 + hh
                kti = KTrot.next()
                DMA("sp", KT[0:96, kti, :], ksc[t_, h], "KTn%d" % kti, ["ksc%d" % t_], ["KTn%d" % kti])
                DMA("sp", KT[96:105, kti, :], kaug_ap, "KTa%d" % kti, [], ["KTa%d" % kti])
                DMA("sp", Vh[:, kti, :, 0:64], vsc[t_, h].rearrange("p (b d) -> p b d", d=64), "Vh%d" % kti, ["vsc%d" % t_], ["Vh%d" % kti])
                kt_of[ui] = kti

            def S_(g):
                ui, blk = blocks[g]
                vi, hh = units[ui]
                kti = kt_of[ui]
                bs = SB[g % 4]
                MM(bank(bs)[:, 0:ncol], KT[0:105, kti, blk * 128:(blk + 1) * 128], qh[0:105, 2 * par + hh, 0:ncol], True, True,
                   ["KTn%d" % kti, "KTa%d" % kti, qkey, "qhaug"], ["ps%d" % bs])

            def E_(g):
                bs = SB[g % 4]
                ACT(PT[:, g % 3, 0:ncol], bank(bs)[:, 0:ncol], AF.Exp, ["ps%d" % bs], ["PT%d" % (g % 3)], scale=MLA_SCALE)

            def P_(g):
                ui, blk = blocks[g]
                vi, hh = units[ui]
                kti = kt_of[ui]
                MM(bank(bas[hh])[0:65, 0:ncol], Vh[:, kti, blk, 0:65], PT[:, g % 3, 0:ncol], vi == 0 and blk == 0, vi == nv - 1 and blk == 3,
                   ["PT%d" % (g % 3), "Vh%d" % kti, "Vhones"], ["ps%d" % bas[hh]])

            stageA(0)
            if len(units) > 1:
                stageA(1)
            S_(0)
            S_(1)
            for g in range(nb):
                ui, blk = blocks[g]
                if blk == 0 and ui + 2 < len(units):
                    stageA(ui + 2)
                E_(g)
                if g + 2 < nb:
                    S_(g + 2)
                P_(g)
            for hh in range(2):
                h = hp * 2 + hh
                attn_finish_p2(bas[hh], ncol, qo[0:64, h, 0:ncol], ["qo%d" % h])
        out_proj("woc", ncol, 1, si, 5)
        mod_norm(1, 2, ncol, si)
        ffn(1, 1, ncol, si, 8)
        norm_mod(ncol, lambda c: gTs[:, 6, c:c + 1], lambda c: None, lambda c: yfin[:, c, 0:ncol], lambda c: ["act%d" % (2 * c), "act%d" % (2 * c + 1)])
        nblk = (ncol + 127) // 128
        for tb in range(nblk):
            nt = min(128, ncol - tb * 128)
            sgi = stgrot.next()
            for c2 in range(2):
                bk = Gbanks.next()
                for cc in range(4):
                    c = c2 * 4 + cc
                    TR(bank(bk)[0:nt, cc * 128:(cc + 1) * 128], yfin[:, c, tb * 128:tb * 128 + nt], identf[:], ["act%d" % (2 * c), "act%d" % (2 * c + 1), "identf"], ["ps%d" % bk])
                CP(cp_rot.next(), stg[0:nt, sgi, c2 * 512:(c2 + 1) * 512], bank(bk)[0:nt, :], ["ps%d" % bk], ["stg%d" % sgi])
            DMA("sp", out_ap_of(tb, nt), stg[0:nt, sgi, :], "stg%d" % sgi, ["stg%d" % sgi], [], final=True)

    NKT = 32 + 6
    ksc = dint("ksc", [NKT, 16, 96, TOK], BF16)
    vsc = dint("vsc", [NKT, 16, 128, 256], BF16)
    KTst = act[0:96, 0:16 * TOK].rearrange("p (h t) -> p h t", t=TOK)
    Vst = act[:, 16 * TOK:24 * TOK].rearrange("p (h b d) -> p h b d", b=4, d=64)
    KST_K = ["act%d" % c for c in range(16)]
    VST_K = ["act%d" % c for c in range(16, 24)]

    def expand_tile(t_, lat2, krs, gkeys):
        kvi = kvTrot.next()
        DMA("sp", kvT[:, kvi, :, :], lat2, "kvT%d" % kvi, gkeys, ["kvT%d" % kvi])
        wv_ = wkvb[:, :, :].rearrange("p c (h e) -> p c h e", e=128)
        for h0 in range(0, 16, 4):
            kk_ = ["act%d" % h for h in range(h0, h0 + 4)]
            for h in range(h0, h0 + 4):
                DMA("sp", KTst[64:96, h, :], krs, "kst%d" % h, gkeys, ["act%d" % h])
            for h in range(h0, h0 + 4):
                bk = Gbanks.next()
                for c in range(2):
                    MM(bank(bk)[0:64, :], wkvb[:, c, h * 128:h * 128 + 64], kvT[:, kvi, c, :], c == 0, c == 1, ["wkvb", "kvT%d" % kvi], ["ps%d" % bk])
                CP(cp_rot.next(), KTst[0:64, h, :], bank(bk)[0:64, :], ["ps%d" % bk], ["act%d" % h])
            DMA("pool", ksc[t_, h0:h0 + 4].rearrange("h p n -> p h n"), KTst[:, h0:h0 + 4, :], "kst_o%d" % (h0 // 4), kk_, ["ksc%d" % t_])
        for half in range(2):
            vk_ = ["act%d" % c for c in range(16 + 4 * half, 20 + 4 * half)]
            for blk in range(4):
                bk = Gbanks.next()
                for c in range(2):
                    MM(bank(bk)[:, :].rearrange("p (h d) -> p h d", d=64), kvT[:, kvi, c, blk * 128:(blk + 1) * 128],
                       wv_[:, c, half * 8:(half + 1) * 8, 64:128], c == 0, c == 1, ["wkvb", "kvT%d" % kvi], ["ps%d" % bk])
                CP(cp_rot.next(), Vst[:, half * 8:(half + 1) * 8, blk, :], bank(bk)[:, :].rearrange("p (h d) -> p h d", d=64), ["ps%d" % bk], vk_)
            for q4 in range(2):
                h0 = half * 8 + q4 * 4
                DMA("pool", vsc[t_, h0:h0 + 4].rearrange("h p (b d) -> p h b d", d=64), Vst[:, h0:h0 + 4, :, :], "vst_o%d" % (half * 2 + q4),
                    ["act%d" % c for c in range(16 + h0 // 2, 16 + h0 // 2 + 2)], ["vsc%d" % t_])

    def attn_finish_p2(ba, ncol, dest_ap, destW):
        CP("dve", den[64:65, 0:ncol], bank(ba)[64:65, 0:ncol], ["ps%d" % ba], ["den"])
        bb = Sbanks.next()
        MM(bank(bb)[0:64, 0:ncol], onesf[64:65, 0:64], den[64:65, 0:ncol], True, True, ["onesf", "den"], ["ps%d" % bb])
        RCP(rb[:, 0:ncol], bank(bb)[0:64, 0:ncol], ["ps%d" % bb], ["rb"])
        TT("dve", dest_ap, bank(ba)[0:64, 0:ncol], rb[:, 0:ncol], ALU.mult, ["ps%d" % ba, "rb"], destW)

    zb = PT[:, 0, 0:480]
    MSET("dve", zb, 0.0, ["PT0"])
    for s_ in range(2):
        DMA("sp", g_s[s_, 0:256, 1056:1536].rearrange("(c p) t -> p c t", p=128)[:, 0, :], zb, "zb%d" % s_, ["PT0"], ["g_s%d" % s_])
        DMA("sp", g_s[s_, 0:256, 1056:1536].rearrange("(c p) t -> p c t", p=128)[:, 1, :], zb, "zb%d" % s_, ["PT0"], ["g_s%d" % s_])
        DMA("sp", g_s[s_, 256:288, 1056:1536], zb[0:32, :], "zb%d" % s_, ["PT0"], ["g_s%d" % s_])
    for s_ in range(2 if 's1' in STAGES else 0):
        phase1_tile("samp", 32, 1 + s_, 0, 0, 0, 0, s_)
        if SUB >= 10:
            sample_cache_latents(s_)
    tile_idx = 0
    for seg in range(NSEG if 'p1' in STAGES else 0):
        for a in range(-1, ST):
            slot = (a + 1) % 2
            if a < 0:
                phase1_tile("halo", TOK, 0, slot, seg, a, -1, 0)
            else:
                first_own[0] = (a == 0)
                phase1_tile("prompt", TOK, 0, slot, seg, a, tile_idx, 0)
                tile_idx += 1
    if 'ag' in STAGES:
        for gc in range(NGC):
            S.op("pool", lambda e, gc=gc: e.collective_compute("AllGather", ALU.bypass, replica_groups=[[0, 1, 2, 3], [4, 5, 6, 7]],
                                                               ins=[g_in[gc].opt()], outs=[g_all[gc].opt()]), ["g_in%d" % gc], ["g_all%d" % gc])
    MSET("dve", Vh[:, :, :, 64:65], 1.0, ["VA0", "VA1", "VAones", "Vhones"])
    for s_ in range(2 if 's2' in STAGES else 0):
        kts = []
        for v in range(3):
            lat2 = g_s[s_, 0:256, v * TOK:(v + 1) * TOK].rearrange("(c p) t -> p c t", p=128)
            t_ = 32 + 3 * s_ + v
            expand_tile(t_, lat2, g_s[s_, 256:288, v * TOK:(v + 1) * TOK], ["g_s%d" % s_])
            kts.append((t_, kaugs_d[v]))
        phase2_tile("samp", 32, 1 + s_, NQT + s_, kts, lambda tb, nt, s_=s_: y_s[s_ * 32:s_ * 32 + 32, :])
    if 'p2' in STAGES:
        for v in range(32):
            jj, off = gloc(v)
            lt_ = off // TOK
            ga_ = g_all[lt_ // 2]; c0_ = (lt_ % 2) * TOK
            lat2 = ga_[jj * 288:jj * 288 + 256, c0_:c0_ + TOK].rearrange("(c p) t -> p c t", p=128)
            expand_tile(v, lat2, ga_[jj * 288 + 256:jj * 288 + 288, c0_:c0_ + TOK], ["g_all%d" % (lt_ // 2)])
    for qi in range(NQT if 'p2' in STAGES else 0):
        kts = [(v, kaug_d[qi, v]) for v in range(nv_of(qi))]
        phase2_tile("prompt", TOK, 0, qi, kts, lambda tb, nt, qi=qi: y_p[qi * TOK + tb * 128:qi * TOK + tb * 128 + nt, :])

    S.emit(st)
    st.close()
    return nc


_NC_CACHE = {}
STOP = 9
STAGES = {'s1', 'p1', 'ag', 's2', 'p2'}
OUTQ = 'sp'
SUB = 99


def _cpu():
    return jax.default_device(jax.devices("cpu")[0])


def _t5_bucket(rel):
    nb = 16
    rel = jnp.asarray(rel)
    ret = jnp.where(rel > 0, nb, 0)
    n = jnp.abs(rel)
    max_exact = nb // 2
    large = max_exact + (jnp.log(jnp.maximum(n, 1).astype(jnp.float32) / max_exact)
                         / np.log(128 / max_exact) * (nb - max_exact)).astype(jnp.int32)
    large = jnp.minimum(large, nb - 1)
    return np.asarray(ret + jnp.where(n < max_exact, n, large))


def _tables(rel_bias_a, t5_bias):
    tab = np.asarray(rel_bias_a[0], np.float32)
    t5 = np.asarray(t5_bias, np.float32)
    kk = np.arange(128)[:, None]
    col = np.arange(640)[None, :]
    u = col - 512
    rel = kk - col
    d = (kk >= 64).astype(np.int64) - np.floor_divide(u, 64)
    idx = np.clip(rel, -128, 128) + 128
    FA = np.where(((d >= 0) & (d <= 8))[None], tab[idx].transpose(2, 0, 1), np.float32(NEG)).astype(np.float32)
    col = np.arange(256)[None, :]
    u = col - 512
    rel = kk - col
    d = (kk >= 64).astype(np.int64) - np.floor_divide(u, 64)
    FB = np.where(((d >= 6) & (d <= 8))[None], t5[_t5_bucket(rel)].transpose(2, 0, 1), np.float32(NEG)).astype(np.float32)
    q = np.arange(32)[None, None, :]
    blk = np.arange(5)[None, :, None]
    kk3 = np.arange(128)[:, None, None]
    rel = 128 * blk + kk3 - 512 - q
    FAs = tab[np.clip(rel, -128, 128) + 128].transpose(3, 0, 1, 2).astype(np.float32)
    blk = np.arange(2)[None, :, None]
    rel = 128 * blk + kk3 - 128 - q
    FBs = t5[_t5_bucket(rel)].transpose(3, 0, 1, 2).astype(np.float32)
    return FA, FB, FAs, FBs


def _rope_tab(pos):
    half = 16
    inv = 10000.0 ** (-jnp.arange(half, dtype=jnp.float32) / half)
    ang = jnp.asarray(pos).astype(jnp.float32)[:, None] * inv[None, :]
    return np.asarray(jnp.cos(ang)), np.asarray(jnp.sin(ang))


def kernel(x_prompt, x_sample, c_prompt, c_sample, cache_a_k, cache_a_v, cache_b_k, cache_b_v,
           cache_c_kv, cache_c_kr, w_ada, b_ada, norm_g, final_norm_g, ffn_w_gate, ffn_w_up, ffn_w_down,
           w_in_ab, w_out_ab, rel_bias_a, t5_bias, sinks_b, w_in_c, c_q_norm_g, c_kv_norm_g, w_qb, w_kvb,
           w_out_c):
    if "nc" not in _NC_CACHE:
        _NC_CACHE["nc"] = build_program()
    nc = _NC_CACHE["nc"]
    in_maps = make_in_maps(x_prompt, x_sample, c_prompt, c_sample, cache_a_k, cache_a_v, cache_b_k, cache_b_v,
                           cache_c_kv, cache_c_kr, w_ada, b_ada, norm_g, final_norm_g, ffn_w_gate, ffn_w_up, ffn_w_down,
                           w_in_ab, w_out_ab, rel_bias_a, t5_bias, sinks_b, w_in_c, c_q_norm_g, c_kv_norm_g, w_qb, w_kvb,
                           w_out_c)
    res = run_bass_kernel_spmd(nc, in_maps, core_ids=list(range(8)))
    return assemble(res.results)


def make_in_maps(x_prompt, x_sample, c_prompt, c_sample, cache_a_k, cache_a_v, cache_b_k, cache_b_v,
                 cache_c_kv, cache_c_kr, w_ada, b_ada, norm_g, final_norm_g, ffn_w_gate, ffn_w_up, ffn_w_down,
                 w_in_ab, w_out_ab, rel_bias_a, t5_bias, sinks_b, w_in_c, c_q_norm_g, c_kv_norm_g, w_qb, w_kvb,
                 w_out_c):
    f = lambda a: np.ascontiguousarray(np.asarray(a, np.float32))
    x_prompt = f(x_prompt); x_sample = f(x_sample)
    with _cpu():
        FA, FB, FAs, FBs = _tables(f(rel_bias_a), f(t5_bias))
    b_adaT = f(b_ada).reshape(2, 72, 128).transpose(2, 0, 1)
    gT = np.concatenate([f(norm_g).reshape(6, 8, 128), f(final_norm_g).reshape(1, 8, 128)], 0).transpose(2, 0, 1)
    qaug = np.zeros((9, TOK), np.float32)
    qc = np.arange(TOK) // 64
    for r in range(8):
        qaug[r] = -(qc < r).astype(np.float32)
    qaug[8] = -1.0
    kaugs = np.zeros((3, 9, TOK), np.float32)
    kidx = np.arange(3 * TOK).reshape(3, TOK)
    kaugs[:, 8, :] = np.where(kidx >= 1056, BIG, 0.0)
    with _cpu():
        cS, sS = _rope_tab(1024 + np.arange(32))
    shared = dict(
        w_ada=f(w_ada), b_adaT=f(b_adaT), gT=f(gT), ffn_w_gate=f(ffn_w_gate), ffn_w_up=f(ffn_w_up), ffn_w_down=f(ffn_w_down),
        w_in_ab=f(w_in_ab)[0], w_out_ab=f(w_out_ab)[0], w_in_c=f(w_in_c)[0], w_qb=f(w_qb)[0], w_kvb=f(w_kvb)[0], w_out_c=f(w_out_c)[0],
        gq=f(np.broadcast_to(f(c_q_norm_g)[0][None], (128, 384))), gkv=f(np.broadcast_to(f(c_kv_norm_g)[0][None], (128, 256))),
        FA=FA, FB=FB, FAs=FAs, FBs=FBs, sinks=f(np.broadcast_to(f(sinks_b)[0][None], (128, 8))),
        cosS=f(np.tile(cS, (1, 16))), sinS=f(np.tile(sS, (1, 16))),
        kaugs=kaugs.astype(ml_dtypes.bfloat16), qaug=qaug.astype(ml_dtypes.bfloat16),
        identf=np.eye(128, dtype=np.float32), identb=np.eye(128, dtype=np.float32).astype(ml_dtypes.bfloat16),
    )
    in_maps = []
    for c in range(8):
        b, j = divmod(c, 4)
        xp = np.zeros((NSEG, (ST + 1) * TOK, D), np.float32)
        hvv = np.zeros((128, 4), np.float32)
        pos = np.zeros(NQT * TOK, np.int64)
        for k in range(NSEG):
            s0 = seg_of(j, k) * ST * TOK
            if s0 == 0:
                xp[k, TOK:] = x_prompt[b, 0:ST * TOK]
                hvv[:, k] = NEG
            else:
                xp[k] = x_prompt[b, s0 - TOK:s0 + ST * TOK]
            pos[k * ST * TOK:(k + 1) * ST * TOK] = s0 + np.arange(ST * TOK)
        with _cpu():
            cP, sP = _rope_tab(pos)
        cosP = np.tile(cP.reshape(NQT * 4, 128, 1, 16), (1, 1, 16, 1)).reshape(NQT * 4, 128, 256)
        sinP = np.tile(sP.reshape(NQT * 4, 128, 1, 16), (1, 1, 16, 1)).reshape(NQT * 4, 128, 256)
        cvec = np.stack([f(c_prompt)[b], f(c_sample)[2 * c], f(c_sample)[2 * c + 1]], -1)
        kaug = np.zeros((NQT, 32, 9, TOK), np.float32)
        kc = np.arange(TOK) // 64
        for qi in range(NQT):
            k, a = divmod(qi, ST)
            T = seg_of(j, k) * ST + a
            for v in range(nv_of(qi)):
                if v == T:
                    for r in range(8):
                        kaug[qi, v, r] = (kc == r) * BIG
                elif v > T:
                    kaug[qi, v, 8] = BIG
        m = dict(shared)
        m.update(
            xp=xp, xs=f(x_sample[2 * c:2 * c + 2]), cT=f(cvec.reshape(8, 128, 3).transpose(1, 0, 2)),
            cak=f(cache_a_k)[0, 2 * c:2 * c + 2].reshape(2, 512, 512), cav=f(cache_a_v)[0, 2 * c:2 * c + 2].reshape(2, 512, 512),
            cbk=f(cache_b_k)[0, 2 * c:2 * c + 2].reshape(2, 128, 128), cbv=f(cache_b_v)[0, 2 * c:2 * c + 2].reshape(2, 128, 128),
            cckv=f(cache_c_kv)[0, 2 * c:2 * c + 2], cckr=f(cache_c_kr)[0, 2 * c:2 * c + 2],
            hv=hvv, cosP=f(cosP), sinP=f(sinP), kaug=kaug.astype(ml_dtypes.bfloat16),
        )
        in_maps.append({k_: np.ascontiguousarray(v_) for k_, v_ in m.items()})
    return in_maps


def assemble(R):
    y_prompt = np.zeros((2, 16384, D), np.float32)
    ckv_p = np.zeros((1, 2, 16384, 256), np.float32)
    ckr_p = np.zeros((1, 2, 16384, 32), np.float32)
    for c in range(8):
        b, j = divmod(c, 4)
        for k in range(NSEG):
            s0 = seg_of(j, k) * ST * TOK
            sl = slice(k * ST * TOK, (k + 1) * ST * TOK)
            y_prompt[b, s0:s0 + ST * TOK] = R[c]["y_p"][sl]
            ckv_p[0, b, s0:s0 + ST * TOK] = R[c]["o_ckv"][sl]
            ckr_p[0, b, s0:s0 + ST * TOK] = R[c]["o_ckr"][sl]
    cat = lambda name: np.concatenate([R[c][name] for c in range(8)], 0)
    y_sample = cat("y_s").reshape(16, 32, D)
    last = [0, 4]
    a_k_p = np.stack([R[c]["o_ak"] for c in last], 0).reshape(1, 2, 512, 8, 64)
    a_v_p = np.stack([R[c]["o_av"] for c in last], 0).reshape(1, 2, 512, 8, 64)
    b_k_p = np.stack([R[c]["o_bk"] for c in last], 0).reshape(1, 2, 128, 2, 64)
    b_v_p = np.stack([R[c]["o_bv"] for c in last], 0).reshape(1, 2, 128, 2, 64)
    return (y_prompt, y_sample, a_k_p, a_v_p, b_k_p, b_v_p, ckv_p, ckr_p,
            cat("o_aks").reshape(1, 16, 32, 8, 64), cat("o_avs").reshape(1, 16, 32, 8, 64),
            cat("o_bks").reshape(1, 16, 32, 2, 64), cat("o_bvs").reshape(1, 16, 32, 2, 64),
            cat("o_ckvs").reshape(1, 16, 32, 256), cat("o_ckrs").reshape(1, 16, 32, 32))
```

```python
import numpy as np
import concourse.bass as bass
import concourse.mybir as mybir

F32 = mybir.dt.float32
BF16 = mybir.dt.bfloat16
AF = mybir.ActivationFunctionType
ALU = mybir.AluOpType

EPOCH = 16000
DMA_EPOCH = 1000


class Sched:
    ENGS = ("pe", "act", "dve", "pool", "sp")

    def __init__(self, nc, same_engine_sync=("act", "dve", "pool")):
        self.nc = nc
        self.ops = []
        self.lastw = {}
        self.lastr = {}
        self.same = set(same_engine_sync)
        self.dma_count = {}

    def _deps(self, reads, writes):
        deps = set()
        for k in reads:
            for d in self.lastw.get(k, {}).values():
                deps.add(d)
        for k in writes:
            for d in self.lastw.get(k, {}).values():
                deps.add(d)
            for d in self.lastr.get(k, {}).values():
                deps.add(d)
        return deps

    def _record(self, idx, agent, reads, writes):
        for k in reads:
            self.lastr.setdefault(k, {})[agent] = idx
        for k in writes:
            self.lastw.setdefault(k, {})[agent] = idx

    def op(self, eng, fn, reads=(), writes=()):
        idx = len(self.ops)
        deps = self._deps(reads, writes)
        self.ops.append(dict(kind="c", eng=eng, fn=fn, deps=deps))
        self._record(idx, eng, reads, writes)
        return idx

    def dma(self, queue, fn, key, reads=(), writes=(), final=False):
        idx = len(self.ops)
        key = key + "_" + queue
        deps = self._deps(reads, writes)
        n = self.dma_count.get(key, 0) + 1
        self.dma_count[key] = n
        self.ops.append(dict(kind="d", eng=queue, fn=fn, deps=deps, key=key, n=n, final=final))
        self._record(idx, "dma:" + key, reads, writes)
        return idx

    def emit(self, stack):
        nc = self.nc
        ops = self.ops
        needed = set()
        for i, o in enumerate(ops):
            for d in o["deps"]:
                od = ops[d]
                if od["kind"] == "c":
                    if od["eng"] == o["eng"] and od["eng"] not in self.same:
                        continue
                    needed.add(d)
        cnt = {e: 0 for e in self.ENGS}
        for i, o in enumerate(ops):
            if o["kind"] == "c" and i in needed:
                cnt[o["eng"]] += 1
                o["ms"] = cnt[o["eng"]]
        sems = {}

        def sem(name):
            if name not in sems:
                sems[name] = stack.enter_context(nc.semaphore(name))
            return sems[name]

        def target(d):
            od = ops[d]
            if od["kind"] == "c":
                m = od["ms"]
                return ("c_%s_%d" % (od["eng"], (m - 1) // EPOCH), (m - 1) % EPOCH + 1)
            n = od["n"]
            return ("d_%s_%d" % (od["key"], (n - 1) // DMA_EPOCH), ((n - 1) % DMA_EPOCH + 1) * 16)

        per_eng = {e: [] for e in self.ENGS}
        for i, o in enumerate(ops):
            per_eng[o["eng"]].append(i)
        for i, o in enumerate(ops):
            if o["kind"] == "c":
                if "ms" in o:
                    sem(target(i)[0])
            else:
                sem(target(i)[0])
        self.n_sems = len(sems)

        def run(engname, eng):
            waited = {}
            for i in per_eng[engname]:
                o = ops[i]
                wl = {}
                for d in o["deps"]:
                    od = ops[d]
                    if od["kind"] == "c" and od["eng"] == engname and engname not in self.same:
                        continue
                    s, v = target(d)
                    if waited.get(s, 0) >= v:
                        continue
                    wl[s] = max(wl.get(s, 0), v)
                for s, v in wl.items():
                    eng.wait_ge(sem(s), v)
                    waited[s] = v
                ins = o["fn"](eng)
                if o["kind"] == "c":
                    if "ms" in o:
                        ins.then_inc(sem(target(i)[0]), 1)
                else:
                    ins.then_inc(sem(target(i)[0]), 16)
            for i in per_eng[engname]:
                o = ops[i]
                if o["kind"] == "d" and o.get("final"):
                    s, v = target(i)
                    if waited.get(s, 0) < v:
                        eng.wait_ge(sem(s), v)
                        waited[s] = v

        with nc.Block() as block:
            @block.sync
            def _(e):
                run("sp", e)

            @block.scalar
            def _(e):
                run("act", e)

            @block.vector
            def _(e):
                run("dve", e)

            @block.gpsimd
            def _(e):
                run("pool", e)

            @block.tensor
            def _(e):
                run("pe", e)

from contextlib import ExitStack
import ml_dtypes
import jax
import jax.numpy as jnp
from concourse.bass_utils import run_bass_kernel_spmd

D = 1024
FF = 2816
NFC = 22
TOK = 512
ST = 4
NSEG = 2
EPS = 1e-6
BIG = 16384.0
MAXDESC = 512
NEG = -1e30
MLA_SCALE = 96 ** -0.5
NQT = NSEG * ST


def seg_of(j, k):
    return j if k == 0 else 7 - j


def nv_of(qi):
    k, a = divmod(qi, ST)
    return (ST * (3 if k == 0 else 7)) + a + 1


def gloc(v):
    s, a = divmod(v, ST)
    jj = s if s < 4 else 7 - s
    kk = 0 if s < 4 else 1
    return jj, (kk * ST + a) * TOK


class Rot:
    def __init__(self, items):
        self.items = list(items)
        self.i = 0

    def next(self):
        r = self.items[self.i % len(self.items)]
        self.i += 1
        return r


def build_program():
    nc = bass.Bass("TRN2", target_bir_lowering=False)
    st = ExitStack()
    S = Sched(nc)

    def din(name, shape, dt=F32):
        return nc.dram_tensor(name, list(shape), dt, kind="ExternalInput").ap()

    def dout(name, shape, dt=F32):
        return nc.dram_tensor(name, list(shape), dt, kind="ExternalOutput").ap()

    def dint(name, shape, dt):
        return nc.dram_tensor(name, list(shape), dt).ap()

    def sb(name, shape, dt):
        return st.enter_context(nc.sbuf_tensor(name, list(shape), dt))

    xp = din("xp", [NSEG, (ST + 1) * TOK, D])
    xs = din("xs", [2, 32, D])
    cT = din("cT", [128, 8, 3])
    cak = din("cak", [2, 512, 512]); cav = din("cav", [2, 512, 512])
    cbk = din("cbk", [2, 128, 128]); cbv = din("cbv", [2, 128, 128])
    cckv = din("cckv", [2, 1024, 256]); cckr = din("cckr", [2, 1024, 32])
    w_ada = din("w_ada", [2, D, 9 * D]); b_adaT = din("b_adaT", [128, 2, 72])
    gT = din("gT", [128, 7, 8])
    wg_d = din("ffn_w_gate", [2, 2, D, FF]); wu_d = din("ffn_w_up", [2, 2, D, FF]); wd_d = din("ffn_w_down", [2, 2, FF, D])
    w_in_ab = din("w_in_ab", [D, 2304]); w_out_ab = din("w_out_ab", [D, D])
    w_in_c = din("w_in_c", [D, 672]); w_qb = din("w_qb", [384, 1536]); w_kvb = din("w_kvb", [256, 2048]); w_out_c = din("w_out_c", [D, D])
    gq_d = din("gq", [128, 384]); gkv_d = din("gkv", [128, 256])
    FA_d = din("FA", [8, 128, 640]); FB_d = din("FB", [8, 128, 256])
    FAs_d = din("FAs", [8, 128, 5, 32]); FBs_d = din("FBs", [8, 128, 2, 32])
    sinks_d = din("sinks", [128, 8]); hv_d = din("hv", [128, 4])
    cosP = din("cosP", [NQT * 4, 128, 256]); sinP = din("sinP", [NQT * 4, 128, 256])
    cosS = din("cosS", [32, 256]); sinS = din("sinS", [32, 256])
    kaug_d = din("kaug", [NQT, 32, 9, TOK], BF16); kaugs_d = din("kaugs", [3, 9, TOK], BF16)
    qaug_d = din("qaug", [9, TOK], BF16)
    identf_d = din("identf", [128, 128]); identb_d = din("identb", [128, 128], BF16)

    y_p = dout("y_p", [NQT * TOK, D]); y_s = dout("y_s", [64, D])
    o_ak = dout("o_ak", [512, 512]); o_av = dout("o_av", [512, 512]); o_bk = dout("o_bk", [128, 128]); o_bv = dout("o_bv", [128, 128])
    o_ckv = dout("o_ckv", [NQT * TOK, 256]); o_ckr = dout("o_ckr", [NQT * TOK, 32])
    o_aks = dout("o_aks", [64, 512]); o_avs = dout("o_avs", [64, 512]); o_bks = dout("o_bks", [64, 128]); o_bvs = dout("o_bvs", [64, 128])
    o_ckvs = dout("o_ckvs", [64, 256]); o_ckrs = dout("o_ckrs", [64, 32])

    x_sp = dint("x_sp", [NQT + 2, 128, 8, TOK], F32)
    q_sp = dint("q_sp", [NQT + 2, 96, 16, TOK], BF16)
    NGC = NQT // 2
    g_in = [dint("g_in%d" % i, [288, 2 * TOK], BF16) for i in range(NGC)]
    g_all = [dint("g_all%d" % i, [4 * 288, 2 * TOK], BF16) for i in range(NGC)]
    g_s = dint("g_s", [2, 288, 3 * TOK], BF16)

    xT = sb("xT", [128, 8, TOK], F32)
    hT = sb("hT", [128, 8, TOK], BF16)
    act = sb("act", [128, 24 * TOK], BF16)
    actc = lambda c: act[:, c * TOK:(c + 1) * TOK]
    xin = act[:, 0:16 * TOK].bitcast(F32).rearrange("p (b d) -> p b d", d=D)
    sq = act[:, 16 * TOK:24 * TOK].rearrange("p (c t) -> p c t", t=TOK)
    yfin = act[:, 0:16 * TOK].bitcast(F32).rearrange("p (c t) -> p c t", t=TOK)
    scr = sb("scr", [128, 6, TOK], F32)
    qo = sb("qo", [128, 16, TOK], BF16)
    kTA = sb("kTA", [64, 2, 8, TOK], BF16)
    kTB = sb("kTB", [64, 2, 2, TOK], BF16)
    VA = sb("VA", [128, 2, 4, 8, 80], BF16)
    VB = sb("VB", [128, 2, 4, 2, 80], BF16)
    Sb = sb("Sb", [128, 2, TOK], F32)
    PT = sb("PT", [128, 3, TOK], BF16)
    fa = sb("fa", [128, 2, 640], F32)
    fb = sb("fb", [128, 2, 256], F32)
    NWP = 5
    wp = sb("wp", [128, NWP, 3072], BF16)
    M = sb("M", [128, 2, 9, 8, 3], F32)
    gTs = sb("gTs", [128, 7, 8], F32)
    scT = sb("scT", [128, 8, 3], F32)
    bT = sb("bT", [128, 2, 72], F32)
    identf = sb("identf_s", [128, 128], F32)
    identb = sb("identb_s", [128, 128], BF16)
    onesb = sb("onesb", [128, 128], BF16)
    onesf = sb("onesf", [128, 64], F32)
    zcol = sb("zcol", [128, 1], F32)
    sinkexp = sb("sinkexp", [128, 8], F32)
    hv = sb("hv_s", [128, 4], F32)
    gq = sb("gq_s", [128, 384], F32)
    gkv = sb("gkv_s", [128, 256], F32)
    den = sb("den", [128, TOK], F32)
    rb = sb("rb", [64, TOK], F32)
    small = sb("small", [128, 8], F32)
    qn_bf = sb("qn_bf", [128, 384], BF16)
    kvn_f = sb("kvn_f", [128, 256], F32)
    kvn_b = sb("kvn_b", [128, 288], BF16)
    kr_f = sb("kr_f", [128, 32], F32)
    q_bf = sb("q_bf", [128, 16, 96], BF16)
    qnT = sb("qnT", [128, 3, TOK], BF16)
    cs = sb("cs", [128, 2, 256], F32)
    rtmp = sb("rtmp", [128, 4, 256], F32)
    stg = sb("stg", [128, 2, 1024], F32)
    gT_sb = sb("gT_sb", [128, 3, TOK], BF16)
    VAflat = VA[:, :, :, :, :].rearrange("p a b c d -> p (a b c d)")
    KT = VAflat[0:105, 0:4 * TOK].rearrange("p (i t) -> p i t", t=TOK)
    Vh = VAflat[:, 4 * TOK:4 * TOK + 1280].rearrange("p (i b d) -> p i b d", b=4, d=80)
    qh = sb("qh", [105, 4, TOK], BF16)
    kvT = sb("kvT", [128, 2, 2, TOK], BF16)
    wkvb = sb("wkvb", [128, 2, 2048], BF16)
    cstage = scr[:, 0:4, :].rearrange("p a b -> p (a b)").rearrange("p (a b) -> p a b", b=256)

    ps = st.enter_context(nc.psum_tensor("ps", [128, 8 * 512], F32))
    bank = lambda i: ps[:, i * 512:(i + 1) * 512]
    Sbanks = Rot([0, 1]); Abanks = Rot([2, 3]); Gbanks = Rot([4, 5, 6, 7])
    wrot = Rot(range(NWP))

    def MM(out, lhsT, rhs, start, stop, R, W):
        S.op("pe", lambda e: e.matmul(out, lhsT, rhs, start=start, stop=stop), R, W)

    def TR(out, in_, ident, R, W):
        S.op("pe", lambda e: e.transpose(out, in_, ident), R, W)

    def ACT(out, in_, func, R, W, bias=None, scale=None, accum=None):
        kw = {}
        if bias is not None:
            kw["bias"] = bias
        if scale is not None:
            kw["scale"] = scale
        if accum is not None:
            kw["accum_out"] = accum
        S.op("act", lambda e: e.activation(out, in_, func, **kw), R, W)

    def TT(eng, out, in0, in1, op, R, W):
        S.op(eng, lambda e: e.tensor_tensor(out, in0, in1, op), R, W)

    def STT(eng, out, in0, scalar, in1, op0, op1, R, W):
        S.op(eng, lambda e: e.scalar_tensor_tensor(out, in0, scalar, in1, op0, op1), R, W)

    def TS(eng, out, in0, s1, s2, op0, op1, R, W):
        S.op(eng, lambda e: e.tensor_scalar(out, in0, s1, 0.0, op0, ALU.add), R, W)

    def CP(eng, out, in_, R, W):
        if eng == "act":
            S.op("act", lambda e: e.copy(out, in_), R, W)
        else:
            S.op(eng, lambda e: e.tensor_copy(out, in_), R, W)

    def RCP(out, in_, R, W):
        S.op("dve", lambda e: e.reciprocal(out, in_), R, W)

    def MSET(eng, ap, val, W):
        S.op(eng, lambda e: e.memset(ap, val), (), W)

    def DMA(q, out, in_, key, R, W, final=False):
        if type(out.tensor).__name__ == "DRamTensorHandle":
            q = "pool"
        S.dma(q, lambda e: e.dma_start(out=out, in_=in_), key, R, W, final=final)

    cp_rot = Rot(["act", "dve"])

    WREG = {}
    wscr_off = [0]
    wscr = dint("wscr", [48 * 1024 * 1024], BF16)

    def wreg(gkey, idx, src_ap):
        shp = list(src_ap.shape)
        n = 1
        for d in shp:
            n *= d
        off = wscr_off[0]
        wscr_off[0] += n
        flat = wscr[off:off + n]
        if len(shp) == 3:
            dst = flat.rearrange("(p a b) -> p a b", a=shp[1], b=shp[2])
            step = max(1, MAXDESC // shp[0])
            for a0 in range(0, shp[1], step):
                a1 = min(shp[1], a0 + step)
                DMA("pool", dst[:, a0:a1, :], src_ap[:, a0:a1, :], "cv_" + gkey, [], ["W_" + gkey])
        else:
            dst = flat.rearrange("(p a) -> p a", a=shp[1])
            DMA("pool", dst, src_ap, "cv_" + gkey, [], ["W_" + gkey])
        WREG[(gkey, idx)] = (flat.rearrange("(p a) -> p a", p=shp[0]), shp)

    def wpiece(gkey, idx):
        src2, shp = WREG[(gkey, idx)]
        i = wrot.next()
        n = src2.shape[1]
        assert n <= 3072, n
        dst = wp[0:shp[0], i, 0:n]
        DMA("sp", dst, src2, "wp%d" % i, ["W_" + gkey], ["wp%d" % i])
        if len(shp) == 3:
            dst = dst.rearrange("p (a b) -> p a b", b=shp[2])
        return dst, "wp%d" % i

    w_ab = w_in_ab.rearrange("(k p) n -> p k n", p=128)
    w_c = w_in_c.rearrange("(k p) n -> p k n", p=128)
    w_qbv = w_qb.rearrange("(k p) n -> p k n", p=128)

    def reg_ffn(l, i):
        wg = wg_d[l, i].rearrange("(k p) f -> p k f", p=128)
        wu = wu_d[l, i].rearrange("(k p) f -> p k f", p=128)
        wd = wd_d[l, i].rearrange("(f p) d -> p f d", p=128)
        for pc in range(11):
            wreg("wg%d%d" % (l, i), pc, wg[:, :, pc * 256:(pc + 1) * 256])
        for pc in range(11):
            wreg("wu%d%d" % (l, i), pc, wu[:, :, pc * 256:(pc + 1) * 256])
        for dc in range(8):
            wreg("wd%d%d" % (l, i), dc, wd[:, :, dc * 128:(dc + 1) * 128])

    def reg_all():
        reg_ffn(0, 0)
        for p0 in range(0, 2304, 256):
            wreg("wab", p0 // 256, w_ab[:, :, p0:p0 + 256])
        wv = w_out_ab.rearrange("(h p) d -> p h d", p=64)
        for dc in range(8):
            wreg("woab", dc, wv[:, :, dc * 128:(dc + 1) * 128])
        reg_ffn(0, 1)
        reg_ffn(1, 0)
        wreg("wc", 0, w_c[:, :, 0:384])
        wreg("wc", 1, w_c[:, :, 384:672])
        for cb in range(3):
            wreg("wqb", cb, w_qbv[:, :, cb * 512:(cb + 1) * 512])
        wv = w_out_c.rearrange("(h p) d -> p h d", p=64)
        for dc in range(8):
            wreg("woc", dc, wv[:, :, dc * 128:(dc + 1) * 128])
        reg_ffn(1, 1)

    reg_all()

    DMA("sp", identf[:], identf_d, "identf", [], ["identf"])
    DMA("sp", identb[:], identb_d, "identb", [], ["identb"])
    DMA("sp", gTs[:], gT, "gTs", [], ["gTs"])
    DMA("sp", bT[:], b_adaT, "bT", [], ["bT"])
    DMA("sp", scT[:], cT, "scT", [], ["scT"])
    DMA("sp", hv[:], hv_d, "hv", [], ["hv"])
    DMA("sp", gq[:], gq_d, "gq", [], ["gq"])
    DMA("sp", gkv[:], gkv_d, "gkv", [], ["gkv"])
    DMA("sp", sinkexp[:], sinks_d, "sinkexp", [], ["sinkexp"])
    MSET("dve", onesb[:], 1.0, ["onesb"])
    MSET("dve", onesf[:], 1.0, ["onesf"])
    MSET("dve", zcol[:], 0.0, ["zcol"])
    MSET("dve", VA[:, :, :, :, 64:65], 1.0, ["VAones"])
    MSET("dve", VB[:, :, :, :, 64:65], 1.0, ["VBones"])
    ACT(sinkexp[:], sinkexp[:], AF.Exp, ["sinkexp"], ["sinkexp"])
    ACT(scT[:], scT[:], AF.Silu, ["scT"], ["scT"])
    for hh in range(4):
        DMA("sp", qh[96:105, hh, :], qaug_d, "qhaug", [], ["qhaug"])
    DMA("pool", wkvb[:], w_kvb.rearrange("(c p) n -> p c n", p=128), "wkvb", [], ["wkvb"])

    for l in range(2):
        for m in range(9):
            for half in range(2):
                if (m * 2 + half) % 2 == 0:
                    wst = act[:, 0:16 * TOK].bitcast(F32).rearrange("p (k n) -> p k n", n=512)
                    wkeys = ["act%d" % c for c in range(16)]; wsk = "wstA"
                else:
                    wst = qo[:, :, :].rearrange("p a b -> p (a b)").bitcast(F32).rearrange("p (k n) -> p k n", n=512)
                    wkeys = ["qo%d" % c for c in range(16)]; wsk = "wstB"
                src = w_ada[l].rearrange("(k p) n -> p k n", p=128)[:, :, m * 1024 + half * 512: m * 1024 + half * 512 + 512]
                DMA("sp", wst, src, wsk, [], wkeys)
                bk = Gbanks.next()
                for oc in range(4):
                    for k in range(8):
                        MM(bank(bk)[:, oc * 4:oc * 4 + 3], wst[:, k, oc * 128:(oc + 1) * 128], scT[:, k, :], k == 0, k == 7,
                           wkeys + ["scT"], ["ps%d" % bk])
                for oc in range(4):
                    ch = half * 4 + oc
                    TS("dve", M[:, l, m, ch, :], bank(bk)[:, oc * 4:oc * 4 + 3], bT[:, l, m * 8 + ch:m * 8 + ch + 1], None, ALU.add, None,
                       ["ps%d" % bk, "bT"], ["M"])
    for l in range(2):
        for n in range(3):
            for si in range(3):
                STT("dve", M[:, l, 3 * n + 1, :, si], M[:, l, 3 * n + 1, :, si], 1.0, gTs[:, 3 * l + n, :], ALU.add, ALU.mult, ["M", "gTs"], ["M"])
        for m in (2, 8):
            TS("dve", M[:, l, m, :, :], M[:, l, m, :, :], 0.5, None, ALU.mult, None, ["M"], ["M"])

    def norm_mod(ncol, scale_of, shift_of, out_of, out_keys):
        ACT(sq[:, :, 0:ncol], xT[:, :, 0:ncol], AF.Square, ["xT"], ["act%d" % (16 + c) for c in range(8)])
        bk = Gbanks.next()
        for c in range(8):
            MM(bank(bk)[:, 0:ncol], onesb[:], sq[:, c, 0:ncol], c == 0, c == 7, ["onesb", "act%d" % (16 + c)], ["ps%d" % bk])
        ACT(scr[:, 0, 0:ncol], bank(bk)[:, 0:ncol], AF.Sqrt, ["ps%d" % bk], ["scr0"], bias=epscol[:], scale=1.0 / D)
        RCP(scr[:, 1, 0:ncol], scr[:, 0, 0:ncol], ["scr0"], ["scr1"])
        for c in range(8):
            tb_ = 2 + (c % 2)
            TT("dve", scr[:, tb_, 0:ncol], xT[:, c, 0:ncol], scr[:, 1, 0:ncol], ALU.mult, ["xT", "scr1"], ["scr%d" % tb_])
            kw = dict(scale=scale_of(c))
            sh = shift_of(c)
            ACT(out_of(c), scr[:, tb_, 0:ncol], AF.Identity, ["scr%d" % tb_, "M", "gTs"], out_keys(c), bias=(sh if sh is not None else zcol[:]), **kw)

    def ffn(l, i, ncol, si, gate_m):
        for pc in range(11):
            g_ap, gk = wpiece("wg%d%d" % (l, i), pc)
            u_ap, uk = wpiece("wu%d%d" % (l, i), pc)
            for sub in range(2):
                fc = pc * 2 + sub
                bg = Gbanks.next(); bu = Gbanks.next()
                for k in range(8):
                    MM(bank(bg)[:, 0:ncol], g_ap[:, k, sub * 128:(sub + 1) * 128], hT[:, k, 0:ncol], k == 0, k == 7, [gk, "hT"], ["ps%d" % bg])
                for k in range(8):
                    MM(bank(bu)[:, 0:ncol], u_ap[:, k, sub * 128:(sub + 1) * 128], hT[:, k, 0:ncol], k == 0, k == 7, [uk, "hT"], ["ps%d" % bu])
                sgi = 4 + (fc % 2)
                ACT(scr[:, sgi, 0:ncol], bank(bg)[:, 0:ncol], AF.Silu, ["ps%d" % bg], ["scr%d" % sgi])
                TT("dve", actc(fc)[:, 0:ncol], bank(bu)[:, 0:ncol], scr[:, sgi, 0:ncol], ALU.mult, ["ps%d" % bu, "scr%d" % sgi], ["act%d" % fc])
        for dc in range(8):
            d_ap, dk = wpiece("wd%d%d" % (l, i), dc)
            by = Gbanks.next()
            for fc in range(NFC):
                MM(bank(by)[:, 0:ncol], d_ap[:, fc, :], actc(fc)[:, 0:ncol], fc == 0, fc == NFC - 1, [dk, "act%d" % fc], ["ps%d" % by])
            STT("dve", xT[:, dc, 0:ncol], bank(by)[:, 0:ncol], M[:, l, gate_m, dc, si:si + 1], xT[:, dc, 0:ncol], ALU.mult, ALU.add,
                ["ps%d" % by, "M", "xT"], ["xT"])

    def mod_norm(l, n, ncol, si):
        norm_mod(ncol,
                 lambda c: M[:, l, 3 * n + 1, c, si:si + 1],
                 lambda c: M[:, l, 3 * n + 0, c, si:si + 1],
                 lambda c: hT[:, c, 0:ncol],
                 lambda c: ["hT"])

    def attn_S(kT_ap, q_ap, nk, ncols, scale, bias_ap, pbias_ap, R):
        bs = Sbanks.next()
        MM(bank(bs)[0:nk, 0:ncols], kT_ap, q_ap, True, True, R, ["ps%d" % bs])
        pi = PTrot.next()
        if bias_ap is not None:
            si_ = Sbrot.next()
            STT("dve", Sb[0:nk, si_, 0:ncols], bank(bs)[0:nk, 0:ncols], scale, bias_ap, ALU.mult, ALU.add, ["ps%d" % bs] + R, ["Sb%d" % si_])
            ACT(PT[0:nk, pi, 0:ncols], Sb[0:nk, si_, 0:ncols], AF.Exp, ["Sb%d" % si_, "hv"], ["PT%d" % pi], bias=pbias_ap)
        else:
            ACT(PT[0:nk, pi, 0:ncols], bank(bs)[0:nk, 0:ncols], AF.Exp, ["ps%d" % bs], ["PT%d" % pi], scale=scale)
        return pi

    def attn_PV(pi, nk, ncols, v_ap, acc_ap, first, R, accW, last):
        MM(acc_ap, v_ap, PT[0:nk, pi, 0:ncols], first, last, ["PT%d" % pi] + R, accW)

    def attn_block(kT_ap, q_ap, nk, ncols, scale, bias_ap, pbias_ap, v_ap, acc_ap, first, R, accW, last=False):
        pi = attn_S(kT_ap, q_ap, nk, ncols, scale, bias_ap, pbias_ap, R)
        attn_PV(pi, nk, ncols, v_ap, acc_ap, first, R, accW, last)

    def attn_finish(ba, ncol, dest_ap, destW, sink_ap=None):
        if sink_ap is not None:
            TS("dve", den[64:65, 0:ncol], bank(ba)[64:65, 0:ncol], sink_ap, None, ALU.add, None, ["ps%d" % ba, "sinkexp"], ["den"])
        else:
            CP("dve", den[64:65, 0:ncol], bank(ba)[64:65, 0:ncol], ["ps%d" % ba], ["den"])
        bb = Gbanks.next()
        MM(bank(bb)[0:64, 0:ncol], onesf[64:65, 0:64], den[64:65, 0:ncol], True, True, ["onesf", "den"], ["ps%d" % bb])
        RCP(rb[:, 0:ncol], bank(bb)[0:64, 0:ncol], ["ps%d" % bb], ["rb"])
        TT("dve", dest_ap, bank(ba)[0:64, 0:ncol], rb[:, 0:ncol], ALU.mult, ["ps%d" % ba, "rb"], destW)

    CS_ALL = ["scr0", "scr1", "scr2", "scr3"]
    PTrot = Rot(range(3)); Sbrot = Rot(range(2)); farot = Rot(range(2)); fbrot = Rot(range(2)); stgrot = Rot(range(2))
    epscol = sb("epscol", [128, 1], F32)
    MSET("dve", epscol[:], EPS, ["epscol"])

    def l0_mixer(kind, ncol, si, slot, tblocks, seg, is_last, samp_idx):
        do_q = kind != "halo"
        plan = []
        if do_q:
            plan += [("q", h, 0 + 64 * h) for h in range(8)]
            plan += [("q", 8 + h, 1536 + 64 * h) for h in range(8)]
        plan += [("ka", h, 512 + 64 * h) for h in range(8)]
        plan += [("kb", h, 2048 + 64 * h) for h in range(2)]
        cur_piece = None
        for (typ, h, col) in plan:
            p0 = (col // 256) * 256
            if cur_piece is None or cur_piece[0] != p0:
                ap_, k_ = wpiece("wab", p0 // 256)
                cur_piece = (p0, ap_, k_)
            _, w_ap, wk = cur_piece
            bk = Gbanks.next()
            for k in range(8):
                MM(bank(bk)[0:64, 0:ncol], w_ap[:, k, col - p0:col - p0 + 64], hT[:, k, 0:ncol], k == 0, k == 7, [wk, "hT"], ["ps%d" % bk])
            if typ == "q":
                dst, W = qo[0:64, h, 0:ncol], ["qo%d" % h]
            elif typ == "ka":
                dst, W = kTA[:, slot, h, 0:ncol], ["kTA%d" % slot]
            else:
                dst, W = kTB[:, slot, h, 0:ncol], ["kTB%d" % slot]
            CP(cp_rot.next(), dst, bank(bk)[0:64, 0:ncol], ["ps%d" % bk], W)
        if SUB <= 4.2:
            return
        want_out = (is_last or kind == "samp") and SUB != 4.41
        for (tbi, (t0, nt)) in enumerate(tblocks):
            jobs = [("va", 1024, 512)]
            jobs += [("vb", 2176, 128)]
            if want_out:
                jobs += [("ka_o", 512, 512), ("kb_o", 2048, 128)]
            for (typ, col, wdt) in jobs:
                bk = Gbanks.next()
                for half in range(wdt // 256 if wdt >= 256 else 1):
                    cw = min(256, wdt)
                    p0 = ((col + half * 256) // 256) * 256
                    w_ap, wk = wpiece("wab", p0 // 256)
                    off = col + half * 256 - p0
                    for k in range(8):
                        MM(bank(bk)[0:nt, half * 256:half * 256 + cw], hT[:, k, t0:t0 + nt], w_ap[:, k, off:off + cw], k == 0, k == 7,
                           [wk, "hT"], ["ps%d" % bk])
                if typ == "va":
                    CP("act", VA[0:nt, slot, tbi, :, 0:64], bank(bk)[0:nt, 0:512].rearrange("p (h d) -> p h d", d=64), ["ps%d" % bk], ["VA%d" % slot])
                elif typ == "vb":
                    CP("dve", VB[0:nt, slot, tbi, :, 0:64], bank(bk)[0:nt, 0:128].rearrange("p (h d) -> p h d", d=64), ["ps%d" % bk], ["VB%d" % slot])
                if want_out:
                    sgi = stgrot.next()
                    CP("act" if typ in ("va", "ka_o") else "dve", stg[0:nt, sgi, 0:wdt], bank(bk)[0:nt, 0:wdt], ["ps%d" % bk], ["stg%d" % sgi])
                    if SUB == 4.42:
                        continue
                    if kind == "samp":
                        r0 = samp_idx * 32
                        dst = {"va": o_avs, "vb": o_bvs, "ka_o": o_aks, "kb_o": o_bks}[typ][r0:r0 + 32, :]
                        DMA(OUTQ, dst, stg[0:nt, sgi, 0:wdt], "stg%d" % sgi, ["stg%d" % sgi], [], final=True)
                    else:
                        if typ in ("va", "ka_o"):
                            dst = (o_av if typ == "va" else o_ak)[t0:t0 + nt, :]
                            DMA(OUTQ, dst, stg[0:nt, sgi, 0:wdt], "stg%d" % sgi, ["stg%d" % sgi], [], final=True)
                        elif tbi == 3:
                            dst = (o_bv if typ == "vb" else o_bk)[:, :]
                            DMA(OUTQ, dst, stg[0:nt, sgi, 0:wdt], "stg%d" % sgi, ["stg%d" % sgi], [], final=True)
        if not do_q or SUB <= 4.42:
            return
        if kind == "prompt":
            prev = 1 - slot
            for h in range(8):
                fi = farot.next()
                DMA("sp", fa[:, fi, :], FA_d[h], "fa%d" % fi, [], ["fa%d" % fi])
                ba = Abanks.next()
                first = True
                pend = []
                for m in (3, 4, 0, 1, 2, 5, 6, 7):
                    q0 = max(0, 128 * m - 512); q1 = min(512, 128 * m + 128)
                    sl = prev if m < 4 else slot
                    blk = m % 4
                    pb = hv[:, seg:seg + 1] if (m < 4) else zcol[:]
                    if m < 4 and not first_own[0]:
                        pb = zcol[:]
                    R_ = ["kTA%d" % sl, "qo%d" % h, "fa%d" % fi, "VA%d" % sl, "VAones"]
                    pi_ = attn_S(kTA[:, sl, h, blk * 128:(blk + 1) * 128], qo[0:64, h, q0:q1], 128, q1 - q0, 0.125,
                                 fa[:, fi, q0 - 128 * m + 512:q1 - 128 * m + 512], pb, R_)
                    if len(pend) >= 2:
                        attn_PV(*pend.pop(0))
                    pend.append((pi_, 128, q1 - q0, VA[:, sl, blk, h, 0:65], bank(ba)[0:65, q0:q1], first, R_, ["ps%d" % ba], m == 7))
                    first = False
                while pend:
                    attn_PV(*pend.pop(0))
                attn_finish(ba, 512, qo[0:64, h, :], ["qo%d" % h])
            for h in range(8):
                fi = fbrot.next()
                DMA("sp", fb[:, fi, :], FB_d[h], "fb%d" % fi, [], ["fb%d" % fi])
                ba = Abanks.next()
                g = h // 4
                first = True
                pend = []
                for m in (4, 6, 3, 5, 7):
                    q0 = max(0, 128 * m - 512); q1 = min(512, 128 * m - 256)
                    sl = prev if m < 4 else slot
                    blk = m % 4
                    pb = hv[:, seg:seg + 1] if (m < 4 and first_own[0]) else zcol[:]
                    R_ = ["kTB%d" % sl, "qo%d" % (8 + h), "fb%d" % fi, "VB%d" % sl, "VBones"]
                    pi_ = attn_S(kTB[:, sl, g, blk * 128:(blk + 1) * 128], qo[0:64, 8 + h, q0:q1], 128, q1 - q0, 0.125,
                                 fb[:, fi, q0 - 128 * m + 512:q1 - 128 * m + 512], pb, R_)
                    if len(pend) >= 2:
                        attn_PV(*pend.pop(0))
                    pend.append((pi_, 128, q1 - q0, VB[:, sl, blk, g, 0:65], bank(ba)[0:65, q0:q1], first, R_, ["ps%d" % ba], m == 7))
                    first = False
                while pend:
                    attn_PV(*pend.pop(0))
                attn_finish(ba, 512, qo[0:64, 8 + h, :], ["qo%d" % (8 + h)], sink_ap=sinkexp[64:65, h:h + 1])
        else:
            s_ = samp_idx
            csl = 1 - slot
            ck = cstage[:, 0:8, :].rearrange("p a b -> p (a b)")[:, 0:2048].rearrange("p (b n) -> p b n", n=512)
            DMA("sp", ck, cak[s_].rearrange("(b p) n -> p b n", p=128), "cstage", [], CS_ALL)
            for blk in range(4):
                for h in range(8):
                    bk = Gbanks.next()
                    TR(bank(bk)[0:64, 0:128], ck[:, blk, h * 64:(h + 1) * 64], identf[:], CS_ALL + ["identf"], ["ps%d" % bk])
                    CP(cp_rot.next(), kTA[:, csl, h, blk * 128:(blk + 1) * 128], bank(bk)[0:64, 0:128], ["ps%d" % bk], ["kTA%d" % csl])
            for blk in range(4):
                for hh in range(2):
                    DMA("pool", VA[:, csl, blk, hh * 4:(hh + 1) * 4, 0:64],
                        cav[s_, blk * 128:(blk + 1) * 128, hh * 256:(hh + 1) * 256].rearrange("p (h d) -> p h d", d=64), "VA%d" % csl, [], ["VA%d" % csl])
            for h in range(8):
                fi = farot.next()
                DMA("sp", fa[:, fi, 0:160].rearrange("p (b q) -> p b q", q=32), FAs_d[h], "fa%d" % fi, [], ["fa%d" % fi])
                fav = fa[:, fi, 0:160].rearrange("p (b q) -> p b q", q=32)
                ba = Abanks.next()
                for blk in range(5):
                    if blk < 4:
                        kT_ap = kTA[:, csl, h, blk * 128:(blk + 1) * 128]; v_ap = VA[:, csl, blk, h, 0:65]; nk = 128
                        R = ["kTA%d" % csl, "VA%d" % csl]
                    else:
                        kT_ap = kTA[:, slot, h, 0:32]; v_ap = VA[0:32, slot, 0, h, 0:65]; nk = 32
                        R = ["kTA%d" % slot, "VA%d" % slot]
                    attn_block(kT_ap, qo[0:64, h, 0:32], nk, 32, 0.125, fav[0:nk, blk, :], zcol[0:nk, :], v_ap, bank(ba)[0:65, 0:32], blk == 0,
                               R + ["qo%d" % h, "fa%d" % fi, "VAones"], ["ps%d" % ba], last=(blk == 4))
                attn_finish(ba, 32, qo[0:64, h, 0:32], ["qo%d" % h])
            if SUB <= 4.6:
                return
            ckb = cstage[:, 0, 0:128]
            DMA("sp", ckb, cbk[s_], "cstage", [], CS_ALL)
            for g in range(2):
                bk = Gbanks.next()
                TR(bank(bk)[0:64, 0:128], ckb[:, g * 64:(g + 1) * 64], identf[:], CS_ALL + ["identf"], ["ps%d" % bk])
                CP(cp_rot.next(), kTB[:, csl, g, 0:128], bank(bk)[0:64, 0:128], ["ps%d" % bk], ["kTB%d" % csl])
            DMA("pool", VB[:, csl, 0, :, 0:64], cbv[s_].rearrange("p (h d) -> p h d", d=64), "VB%d" % csl, [], ["VB%d" % csl])
            for h in range(8):
                fi = fbrot.next()
                DMA("sp", fb[:, fi, 0:64].rearrange("p (b q) -> p b q", q=32), FBs_d[h], "fb%d" % fi, [], ["fb%d" % fi])
                fbv = fb[:, fi, 0:64].rearrange("p (b q) -> p b q", q=32)
                ba = Abanks.next()
                g = h // 4
                for blk in range(2):
                    if blk == 0:
                        kT_ap = kTB[:, csl, g, 0:128]; v_ap = VB[:, csl, 0, g, 0:65]; nk = 128; R = ["kTB%d" % csl, "VB%d" % csl]
                    else:
                        kT_ap = kTB[:, slot, g, 0:32]; v_ap = VB[0:32, slot, 0, g, 0:65]; nk = 32; R = ["kTB%d" % slot, "VB%d" % slot]
                    attn_block(kT_ap, qo[0:64, 8 + h, 0:32], nk, 32, 0.125, fbv[0:nk, blk, :], zcol[0:nk, :], v_ap, bank(ba)[0:65, 0:32], blk == 0,
                               R + ["qo%d" % (8 + h), "fb%d" % fi, "VBones"], ["ps%d" % ba], last=(blk == 1))
                attn_finish(ba, 32, qo[0:64, 8 + h, 0:32], ["qo%d" % (8 + h)], sink_ap=sinkexp[64:65, h:h + 1])

    def out_proj(w_dram, ncol, l, si, gate_m):
        for dc in range(8):
            w_ap, wk = wpiece(w_dram, dc)
            by = Gbanks.next()
            for h in range(16):
                MM(bank(by)[:, 0:ncol], w_ap[:, h, :], qo[0:64, h, 0:ncol], h == 0, h == 15, [wk, "qo%d" % h], ["ps%d" % by])
            STT("dve", xT[:, dc, 0:ncol], bank(by)[:, 0:ncol], M[:, l, gate_m, dc, si:si + 1], xT[:, dc, 0:ncol], ALU.mult, ALU.add,
                ["ps%d" % by, "M", "xT"], ["xT"])

    def l1_prep(kind, ncol, tblocks, tile_idx, samp_idx):
        for (tbi, (t0, nt)) in enumerate(tblocks):
            wq_ap, wqk = wpiece("wc", 0)
            bq = Gbanks.next()
            for k in range(8):
                MM(bank(bq)[0:nt, 0:384], hT[:, k, t0:t0 + nt], wq_ap[:, k, :], k == 0, k == 7, [wqk, "hT"], ["ps%d" % bq])
            wk_ap, wkk = wpiece("wc", 1)
            bk2 = Gbanks.next()
            for k in range(8):
                MM(bank(bk2)[0:nt, 0:288], hT[:, k, t0:t0 + nt], wk_ap[:, k, :], k == 0, k == 7, [wkk, "hT"], ["ps%d" % bk2])
            ACT(scr[0:nt, 2, 0:384], bank(bq)[0:nt, 0:384], AF.Square,
                ["ps%d" % bq], ["scr2", "small"], accum=small[0:nt, 0:1])
            ACT(small[0:nt, 1:2], small[0:nt, 0:1], AF.Sqrt, ["small"], ["small"], bias=epscol[0:nt, :], scale=1.0 / 384)
            RCP(small[0:nt, 2:3], small[0:nt, 1:2], ["small"], ["small"])
            STT("dve", qn_bf[0:nt, :], bank(bq)[0:nt, 0:384], small[0:nt, 2:3], gq[0:nt, :], ALU.mult, ALU.mult, ["ps%d" % bq, "small", "gq"], ["qn_bf"])
            ACT(scr[0:nt, 3, 0:256], bank(bk2)[0:nt, 0:256], AF.Square, ["ps%d" % bk2], ["scr3", "small"], accum=small[0:nt, 3:4])
            ACT(small[0:nt, 4:5], small[0:nt, 3:4], AF.Sqrt, ["small"], ["small"], bias=epscol[0:nt, :], scale=1.0 / 256)
            RCP(small[0:nt, 5:6], small[0:nt, 4:5], ["small"], ["small"])
            STT("dve", kvn_f[0:nt, :], bank(bk2)[0:nt, 0:256], small[0:nt, 5:6], gkv[0:nt, :], ALU.mult, ALU.mult, ["ps%d" % bk2, "small", "gkv"], ["kvn_f"])
            CP("act", kvn_b[0:nt, 0:256], kvn_f[0:nt, :], ["kvn_f"], ["kvn_b"])
            if kind == "samp":
                DMA("sp", cs[0:nt, 0, :], cosS, "cs", [], ["cs"])
                DMA("sp", cs[0:nt, 1, :], sinS, "cs", [], ["cs"])
            else:
                DMA("sp", cs[:, 0, :], cosP[tile_idx * 4 + tbi], "cs", [], ["cs"])
                DMA("sp", cs[:, 1, :], sinP[tile_idx * 4 + tbi], "cs", [], ["cs"])
            x1 = bank(bk2)[0:nt, 256:272]; x2 = bank(bk2)[0:nt, 272:288]
            c16 = cs[0:nt, 0, 0:16]; s16 = cs[0:nt, 1, 0:16]
            TT("dve", rtmp[0:nt, 0, 0:16], x1, c16, ALU.mult, ["ps%d" % bk2, "cs"], ["rtmp0"])
            TT("dve", rtmp[0:nt, 1, 0:16], x2, s16, ALU.mult, ["ps%d" % bk2, "cs"], ["rtmp1"])
            TT("dve", kr_f[0:nt, 0:16], rtmp[0:nt, 0, 0:16], rtmp[0:nt, 1, 0:16], ALU.subtract, ["rtmp0", "rtmp1"], ["kr_f"])
            TT("dve", rtmp[0:nt, 2, 0:16], x1, s16, ALU.mult, ["ps%d" % bk2, "cs"], ["rtmp2"])
            TT("dve", rtmp[0:nt, 3, 0:16], x2, c16, ALU.mult, ["ps%d" % bk2, "cs"], ["rtmp3"])
            TT("dve", kr_f[0:nt, 16:32], rtmp[0:nt, 2, 0:16], rtmp[0:nt, 3, 0:16], ALU.add, ["rtmp2", "rtmp3"], ["kr_f"])
            CP("act", kvn_b[0:nt, 256:288], kr_f[0:nt, :], ["kr_f"], ["kvn_b"])
            if kind == "samp":
                r0 = samp_idx * 32
                DMA("sp", o_ckvs[r0:r0 + 32, :], kvn_f[0:nt, :], "kvn_f", ["kvn_f"], [], final=True)
                DMA("sp", o_ckrs[r0:r0 + 32, :], kr_f[0:nt, :], "kr_f", ["kr_f"], [], final=True)
            else:
                r0 = tile_idx * TOK + t0
                DMA("sp", o_ckv[r0:r0 + nt, :], kvn_f[0:nt, :], "kvn_f", ["kvn_f"], [], final=True)
                DMA("sp", o_ckr[r0:r0 + nt, :], kr_f[0:nt, :], "kr_f", ["kr_f"], [], final=True)
            bt = Gbanks.next()
            btv = bank(bt).bitcast(BF16)
            for c in range(3):
                w_ = 128 if c < 2 else 32
                TR(btv[0:w_, c * 128:c * 128 + nt], kvn_b[0:nt, c * 128:c * 128 + w_], identb[0:nt, 0:nt], ["kvn_b", "identb"], ["ps%d" % bt])
            CP("dve", gT_sb[:, 0:2, t0:t0 + nt], btv[:, 0:256].rearrange("p (c t) -> p c t", t=128)[:, :, 0:nt], ["ps%d" % bt], ["gT_sb"])
            CP("dve", gT_sb[0:32, 2, t0:t0 + nt], btv[0:32, 256:256 + nt], ["ps%d" % bt], ["gT_sb"])
            bt2 = Gbanks.next()
            bt2v = bank(bt2).bitcast(BF16)
            for c in range(3):
                TR(bt2v[:, c * 128:c * 128 + nt], qn_bf[0:nt, c * 128:(c + 1) * 128], identb[0:nt, 0:nt], ["qn_bf", "identb"], ["ps%d" % bt2])
            CP("act", qnT[:, :, t0:t0 + nt], bt2v[:, 0:384].rearrange("p (c t) -> p c t", t=128)[:, :, 0:nt], ["ps%d" % bt2], ["qnT"])
            for cb in range(3):
                wq2, wq2k = wpiece("wqb", cb)
                bqq = Gbanks.next()
                for k in range(3):
                    MM(bank(bqq)[0:nt, 0:512], qnT[:, k, t0:t0 + nt], wq2[:, k, :], k == 0, k == 2, [wq2k, "qnT"], ["ps%d" % bqq])
                CP(cp_rot.next(), qraw[0:nt, cb * 512:(cb + 1) * 512],
                   bank(bqq)[0:nt, 0:512], ["ps%d" % bqq], ["stg0", "stg1"])
            qv = qraw[0:nt, :].rearrange("p (h d) -> p h d", d=96)
            CP("act", q_bf[0:nt, :, 0:64], qv[:, :, 0:64], ["stg0", "stg1"], ["q_bf"])
            cosv = cs[0:nt, 0, :].rearrange("p (h d) -> p h d", d=16); sinv = cs[0:nt, 1, :].rearrange("p (h d) -> p h d", d=16)
            r3 = lambda i: rtmp[0:nt, i, :].rearrange("p (h d) -> p h d", d=16)
            TT("dve", r3(0), qv[:, :, 64:80], cosv, ALU.mult, ["stg0", "stg1", "cs"], ["rtmp0"])
            TT("dve", r3(1), qv[:, :, 80:96], sinv, ALU.mult, ["stg0", "stg1", "cs"], ["rtmp1"])
            TT("dve", q_bf[0:nt, :, 64:80], r3(0), r3(1), ALU.subtract, ["rtmp0", "rtmp1"], ["q_bf"])
            TT("dve", r3(2), qv[:, :, 64:80], sinv, ALU.mult, ["stg0", "stg1", "cs"], ["rtmp2"])
            TT("dve", r3(3), qv[:, :, 80:96], cosv, ALU.mult, ["stg0", "stg1", "cs"], ["rtmp3"])
            TT("dve", q_bf[0:nt, :, 80:96], r3(2), r3(3), ALU.add, ["rtmp2", "rtmp3"], ["q_bf"])
            for hg in range(2):
                bt3 = Gbanks.next()
                bt3v = bank(bt3).bitcast(BF16)
                for hh in range(8):
                    h = hg * 8 + hh
                    TR(bt3v[0:96, hh * 128:hh * 128 + nt], q_bf[0:nt, h, :], identb[0:nt, 0:nt], ["q_bf", "identb"], ["ps%d" % bt3])
                CP(cp_rot.next(), qo[0:96, hg * 8:(hg + 1) * 8, t0:t0 + nt], bt3v[0:96, :].rearrange("p (h t) -> p h t", t=128)[:, :, 0:nt],
                   ["ps%d" % bt3], ["qo%d" % h_ for h_ in range(hg * 8, hg * 8 + 8)])

    qraw = stg[:, :, :].rearrange("p a b -> p (a b)")[:, 0:1536]
    first_own = [False]

    def phase1_tile(kind, ncol, si, slot, seg, a, tile_idx, samp_idx):
        tblocks = [(i * 128, 128) for i in range(4)] if kind != "samp" else [(0, 32)]
        if kind == "samp":
            DMA("sp", xin[0:32, 0, :], xs[samp_idx], "xin", [], ["act%d" % c for c in range(4)])
        else:
            r0 = (a + 1) * TOK if kind == "prompt" else 0
            DMA("sp", xin[:, :, :], xp[seg, r0:r0 + TOK, :].rearrange("(b p) d -> p b d", p=128), "xin", [], ["act%d" % c for c in range(16)])
        for (tbi, (t0, nt)) in enumerate(tblocks):
            for c2 in range(2):
                bk = Gbanks.next()
                for cc in range(4):
                    c = c2 * 4 + cc
                    TR(bank(bk)[:, cc * 128:cc * 128 + nt], xin[0:nt, tbi, c * 128:(c + 1) * 128], identf[0:nt, 0:nt],
                       ["act%d" % c_ for c_ in range(16)] + ["identf"], ["ps%d" % bk])
                CP(cp_rot.next(), xT[:, c2 * 4:(c2 + 1) * 4, t0:t0 + nt], bank(bk).rearrange("p (c t) -> p c t", t=128)[:, :, 0:nt], ["ps%d" % bk], ["xT"])
        if SUB <= 1:
            return
        mod_norm(0, 0, ncol, si)
        if SUB <= 2:
            return
        ffn(0, 0, ncol, si, 2)
        if SUB <= 3:
            return
        mod_norm(0, 1, ncol, si)
        if SUB <= 4:
            return
        l0_mixer(kind, ncol, si, slot, tblocks, seg, kind == "prompt" and seg == 1 and a == ST - 1, samp_idx)
        if kind == "halo" or SUB <= 5:
            return
        out_proj("woab", ncol, 0, si, 5)
        mod_norm(0, 2, ncol, si)
        ffn(0, 1, ncol, si, 8)
        if SUB <= 6:
            return
        mod_norm(1, 0, ncol, si)
        ffn(1, 0, ncol, si, 2)
        mod_norm(1, 1, ncol, si)
        if SUB <= 7:
            return
        l1_prep(kind, ncol, tblocks, tile_idx, samp_idx)
        if SUB <= 8:
            return
        sp_i = tile_idx if kind == "prompt" else NQT + samp_idx
        DMA("sp", x_sp[sp_i][:, :, 0:ncol], xT[:, :, 0:ncol], "xT", ["xT"], ["x_sp%d" % sp_i])
        DMA("sp", q_sp[sp_i][:, :, 0:ncol], qo[0:96, :, 0:ncol], "qo", ["qo%d" % h for h in range(16)], ["q_sp%d" % sp_i])
        if kind == "prompt":
            gi_ = g_in[tile_idx // 2]; c0_ = (tile_idx % 2) * TOK
            DMA("sp", gi_[0:256, c0_:c0_ + TOK].rearrange("(c p) t -> p c t", p=128), gT_sb[:, 0:2, :], "gT_sb", ["gT_sb"], ["g_in%d" % (tile_idx // 2)])
            DMA("sp", gi_[256:288, c0_:c0_ + TOK], gT_sb[0:32, 2, :], "gT_sb", ["gT_sb"], ["g_in%d" % (tile_idx // 2)])
        else:
            DMA("sp", g_s[samp_idx, 0:256, 1024:1056].rearrange("(c p) t -> p c t", p=128), gT_sb[:, 0:2, 0:32], "gT_sb", ["gT_sb"], ["g_s%d" % samp_idx])
            DMA("sp", g_s[samp_idx, 256:288, 1024:1056], gT_sb[0:32, 2, 0:32], "gT_sb", ["gT_sb"], ["g_s%d" % samp_idx])

    def sample_cache_latents(s_):
        for half in range(2):
            DMA("sp", cstage[:, 0:4, :], cckv[s_, half * 512:(half + 1) * 512, :].rearrange("(b p) n -> p b n", p=128), "cstage", [], CS_ALL)
            for c in range(2):
                bk = Gbanks.next()
                for blk in range(4):
                    TR(bank(bk)[:, blk * 128:(blk + 1) * 128], cstage[:, blk, c * 128:(c + 1) * 128], identf[:], CS_ALL + ["identf"], ["ps%d" % bk])
                CP(cp_rot.next(), gT_sb[:, c, :], bank(bk), ["ps%d" % bk], ["gT_sb"])
            DMA("sp", g_s[s_, 0:256, half * 512:(half + 1) * 512].rearrange("(c p) t -> p c t", p=128), gT_sb[:, 0:2, :], "gT_sb", ["gT_sb"], ["g_s%d" % s_])
        for half in range(2):
            crv = cstage[:, 4, :].rearrange("p (b n) -> p b n", n=32)[:, 0:4, :]
            DMA("sp", crv, cckr[s_, half * 512:(half + 1) * 512, :].rearrange("(b p) n -> p b n", p=128), "cstage4", [], ["scr2"])
            bk = Gbanks.next()
            for blk in range(4):
                TR(bank(bk)[0:32, blk * 128:(blk + 1) * 128], crv[:, blk, :], identf[:], ["scr2", "identf"], ["ps%d" % bk])
            CP(cp_rot.next(), gT_sb[0:32, 2, :], bank(bk)[0:32, :], ["ps%d" % bk], ["gT_sb"])
            DMA("sp", g_s[s_, 256:288, half * 512:(half + 1) * 512], gT_sb[0:32, 2, :], "gT_sb", ["gT_sb"], ["g_s%d" % s_])

    KTrot = Rot(range(4)); kvTrot = Rot(range(2))

    def phase2_tile(kind, ncol, si, sp_i, key_tiles, out_ap_of):
        DMA("sp", xT[:, :, 0:ncol], x_sp[sp_i][:, :, 0:ncol], "xT", ["x_sp%d" % sp_i], ["xT"])
        SB = [0, 1, 4, 5]
        nv = len(key_tiles)
        for hp in range(8):
            par = hp % 2
            qkey = "qh%d" % par
            DMA("sp", qh[0:96, 2 * par:2 * par + 2, 0:ncol], q_sp[sp_i][:, hp * 2:hp * 2 + 2, 0:ncol], qkey, ["q_sp%d" % sp_i], [qkey])
            bas = [2, 3]
            units = [(vi, hh) for vi in range(nv) for hh in range(2)]
            blocks = [(ui, blk) for ui in range(len(units)) for blk in range(4)]
            nb = len(blocks)
            kt_of = {}

            def stageA(ui):
                vi, hh = units[ui]
                t_, kaug_ap = key_tiles[vi]
                h = hp * 2 + hh
                kti = KTrot.next()
                DMA("sp", KT[0:96, kti, :], ksc[t_, h], "KTn%d" % kti, ["ksc%d" % t_], ["KTn%d" % kti])
                DMA("sp", KT[96:105, kti, :], kaug_ap, "KTa%d" % kti, [], ["KTa%d" % kti])
                DMA("sp", Vh[:, kti, :, 0:64], vsc[t_, h].rearrange("p (b d) -> p b d", d=64), "Vh%d" % kti, ["vsc%d" % t_], ["Vh%d" % kti])
                kt_of[ui] = kti

            def S_(g):
                ui, blk = blocks[g]
                vi, hh = units[ui]
                kti = kt_of[ui]
                bs = SB[g % 4]
                MM(bank(bs)[:, 0:ncol], KT[0:105, kti, blk * 128:(blk + 1) * 128], qh[0:105, 2 * par + hh, 0:ncol], True, True,
                   ["KTn%d" % kti, "KTa%d" % kti, qkey, "qhaug"], ["ps%d" % bs])

            def E_(g):
                bs = SB[g % 4]
                ACT(PT[:, g % 3, 0:ncol], bank(bs)[:, 0:ncol], AF.Exp, ["ps%d" % bs], ["PT%d" % (g % 3)], scale=MLA_SCALE)

            def P_(g):
                ui, blk = blocks[g]
                vi, hh = units[ui]
                kti = kt_of[ui]
                MM(bank(bas[hh])[0:65, 0:ncol], Vh[:, kti, blk, 0:65], PT[:, g % 3, 0:ncol], vi == 0 and blk == 0, vi == nv - 1 and blk == 3,
                   ["PT%d" % (g % 3), "Vh%d" % kti, "Vhones"], ["ps%d" % bas[hh]])

            stageA(0)
            if len(units) > 1:
                stageA(1)
            S_(0)
            S_(1)
            for g in range(nb):
                ui, blk = blocks[g]
                if blk == 0 and ui + 2 < len(units):
                    stageA(ui + 2)
                E_(g)
                if g + 2 < nb:
                    S_(g + 2)
                P_(g)
            for hh in range(2):
                h = hp * 2 + hh
                attn_finish_p2(bas[hh], ncol, qo[0:64, h, 0:ncol], ["qo%d" % h])
        out_proj("woc", ncol, 1, si, 5)
        mod_norm(1, 2, ncol, si)
        ffn(1, 1, ncol, si, 8)
        norm_mod(ncol, lambda c: gTs[:, 6, c:c + 1], lambda c: None, lambda c: yfin[:, c, 0:ncol], lambda c: ["act%d" % (2 * c), "act%d" % (2 * c + 1)])
        nblk = (ncol + 127) // 128
        for tb in range(nblk):
            nt = min(128, ncol - tb * 128)
            sgi = stgrot.next()
            for c2 in range(2):
                bk = Gbanks.next()
                for cc in range(4):
                    c = c2 * 4 + cc
                    TR(bank(bk)[0:nt, cc * 128:(cc + 1) * 128], yfin[:, c, tb * 128:tb * 128 + nt], identf[:], ["act%d" % (2 * c), "act%d" % (2 * c + 1), "identf"], ["ps%d" % bk])
                CP(cp_rot.next(), stg[0:nt, sgi, c2 * 512:(c2 + 1) * 512], bank(bk)[0:nt, :], ["ps%d" % bk], ["stg%d" % sgi])
            DMA("sp", out_ap_of(tb, nt), stg[0:nt, sgi, :], "stg%d" % sgi, ["stg%d" % sgi], [], final=True)

    NKT = 32 + 6
    ksc = dint("ksc", [NKT, 16, 96, TOK], BF16)
    vsc = dint("vsc", [NKT, 16, 128, 256], BF16)
    KTst = act[0:96, 0:16 * TOK].rearrange("p (h t) -> p h t", t=TOK)
    Vst = act[:, 16 * TOK:24 * TOK].rearrange("p (h b d) -> p h b d", b=4, d=64)
    KST_K = ["act%d" % c for c in range(16)]
    VST_K = ["act%d" % c for c in range(16, 24)]

    def expand_tile(t_, lat2, krs, gkeys):
        kvi = kvTrot.next()
        DMA("sp", kvT[:, kvi, :, :], lat2, "kvT%d" % kvi, gkeys, ["kvT%d" % kvi])
        wv_ = wkvb[:, :, :].rearrange("p c (h e) -> p c h e", e=128)
        for h0 in range(0, 16, 4):
            kk_ = ["act%d" % h for h in range(h0, h0 + 4)]
            for h in range(h0, h0 + 4):
                DMA("sp", KTst[64:96, h, :], krs, "kst%d" % h, gkeys, ["act%d" % h])
            for h in range(h0, h0 + 4):
                bk = Gbanks.next()
                for c in range(2):
                    MM(bank(bk)[0:64, :], wkvb[:, c, h * 128:h * 128 + 64], kvT[:, kvi, c, :], c == 0, c == 1, ["wkvb", "kvT%d" % kvi], ["ps%d" % bk])
                CP(cp_rot.next(), KTst[0:64, h, :], bank(bk)[0:64, :], ["ps%d" % bk], ["act%d" % h])
            DMA("pool", ksc[t_, h0:h0 + 4].rearrange("h p n -> p h n"), KTst[:, h0:h0 + 4, :], "kst_o%d" % (h0 // 4), kk_, ["ksc%d" % t_])
        for half in range(2):
            vk_ = ["act%d" % c for c in range(16 + 4 * half, 20 + 4 * half)]
            for blk in range(4):
                bk = Gbanks.next()
                for c in range(2):
                    MM(bank(bk)[:, :].rearrange("p (h d) -> p h d", d=64), kvT[:, kvi, c, blk * 128:(blk + 1) * 128],
                       wv_[:, c, half * 8:(half + 1) * 8, 64:128], c == 0, c == 1, ["wkvb", "kvT%d" % kvi], ["ps%d" % bk])
                CP(cp_rot.next(), Vst[:, half * 8:(half + 1) * 8, blk, :], bank(bk)[:, :].rearrange("p (h d) -> p h d", d=64), ["ps%d" % bk], vk_)
            for q4 in range(2):
                h0 = half * 8 + q4 * 4
                DMA("pool", vsc[t_, h0:h0 + 4].rearrange("h p (b d) -> p h b d", d=64), Vst[:, h0:h0 + 4, :, :], "vst_o%d" % (half * 2 + q4),
                    ["act%d" % c for c in range(16 + h0 // 2, 16 + h0 // 2 + 2)], ["vsc%d" % t_])

    def attn_finish_p2(ba, ncol, dest_ap, destW):
        CP("dve", den[64:65, 0:ncol], bank(ba)[64:65, 0:ncol], ["ps%d" % ba], ["den"])
        bb = Sbanks.next()
        MM(bank(bb)[0:64, 0:ncol], onesf[64:65, 0:64], den[64:65, 0:ncol], True, True, ["onesf", "den"], ["ps%d" % bb])
        RCP(rb[:, 0:ncol], bank(bb)[0:64, 0:ncol], ["ps%d" % bb], ["rb"])
        TT("dve", dest_ap, bank(ba)[0:64, 0:ncol], rb[:, 0:ncol], ALU.mult, ["ps%d" % ba, "rb"], destW)

    zb = PT[:, 0, 0:480]
    MSET("dve", zb, 0.0, ["PT0"])
    for s_ in range(2):
        DMA("sp", g_s[s_, 0:256, 1056:1536].rearrange("(c p) t -> p c t", p=128)[:, 0, :], zb, "zb%d" % s_, ["PT0"], ["g_s%d" % s_])
        DMA("sp", g_s[s_, 0:256, 1056:1536].rearrange("(c p) t -> p c t", p=128)[:, 1, :], zb, "zb%d" % s_, ["PT0"], ["g_s%d" % s_])
        DMA("sp", g_s[s_, 256:288, 1056:1536], zb[0:32, :], "zb%d" % s_, ["PT0"], ["g_s%d" % s_])
    tile_idx = 0
    for seg in range(NSEG if 'p1' in STAGES else 0):
        for a in range(-1, ST):
            slot = (a + 1) % 2
            if a < 0:
                phase1_tile("halo", TOK, 0, slot, seg, a, -1, 0)
            else:
                first_own[0] = (a == 0)
                phase1_tile("prompt", TOK, 0, slot, seg, a, tile_idx, 0)
                tile_idx += 1
    for s_ in range(2 if 's1' in STAGES else 0):
        phase1_tile("samp", 32, 1 + s_, 0, 0, 0, 0, s_)
        if SUB >= 10:
            sample_cache_latents(s_)
    if 'ag' in STAGES:
        for gc in range(NGC):
            S.op("pool", lambda e, gc=gc: e.collective_compute("AllGather", ALU.bypass, replica_groups=[[0, 1, 2, 3], [4, 5, 6, 7]],
                                                               ins=[g_in[gc].opt()], outs=[g_all[gc].opt()]), ["g_in%d" % gc], ["g_all%d" % gc])
    MSET("dve", Vh[:, :, :, 64:65], 1.0, ["VA0", "VA1", "VAones", "Vhones"])
    for s_ in range(2 if 's2' in STAGES else 0):
        kts = []
        for v in range(3):
            lat2 = g_s[s_, 0:256, v * TOK:(v + 1) * TOK].rearrange("(c p) t -> p c t", p=128)
            t_ = 32 + 3 * s_ + v
            expand_tile(t_, lat2, g_s[s_, 256:288, v * TOK:(v + 1) * TOK], ["g_s%d" % s_])
            kts.append((t_, kaugs_d[v]))
        phase2_tile("samp", 32, 1 + s_, NQT + s_, kts, lambda tb, nt, s_=s_: y_s[s_ * 32:s_ * 32 + 32, :])
    if 'p2' in STAGES:
        for v in range(32):
            jj, off = gloc(v)
            lt_ = off // TOK
            ga_ = g_all[lt_ // 2]; c0_ = (lt_ % 2) * TOK
            lat2 = ga_[jj * 288:jj * 288 + 256, c0_:c0_ + TOK].rearrange("(c p) t -> p c t", p=128)
            expand_tile(v, lat2, ga_[jj * 288 + 256:jj * 288 + 288, c0_:c0_ + TOK], ["g_all%d" % (lt_ // 2)])
    for qi in range(NQT if 'p2' in STAGES else 0):
        kts = [(v, kaug_d[qi, v]) for v in range(nv_of(qi))]
        phase2_tile("prompt", TOK, 0, qi, kts, lambda tb, nt, qi=qi: y_p[qi * TOK + tb * 128:qi * TOK + tb * 128 + nt, :])

    S.emit(st)
    st.close()
    return nc


_NC_CACHE = {}
STOP = 9
STAGES = {'s1', 'p1', 'ag', 's2', 'p2'}
OUTQ = 'sp'
SUB = 99


def _cpu():
    return jax.default_device(jax.devices("cpu")[0])


def _t5_bucket(rel):
    nb = 16
    rel = jnp.asarray(rel)
    ret = jnp.where(rel > 0, nb, 0)
    n = jnp.abs(rel)
    max_exact = nb // 2
    large = max_exact + (jnp.log(jnp.maximum(n, 1).astype(jnp.float32) / max_exact)
                         / np.log(128 / max_exact) * (nb - max_exact)).astype(jnp.int32)
    large = jnp.minimum(large, nb - 1)
    return np.asarray(ret + jnp.where(n < max_exact, n, large))


def _tables(rel_bias_a, t5_bias):
    tab = np.asarray(rel_bias_a[0], np.float32)
    t5 = np.asarray(t5_bias, np.float32)
    kk = np.arange(128)[:, None]
    col = np.arange(640)[None, :]
    u = col - 512
    rel = kk - col
    d = (kk >= 64).astype(np.int64) - np.floor_divide(u, 64)
    idx = np.clip(rel, -128, 128) + 128
    FA = np.where(((d >= 0) & (d <= 8))[None], tab[idx].transpose(2, 0, 1), np.float32(NEG)).astype(np.float32)
    col = np.arange(256)[None, :]
    u = col - 512
    rel = kk - col
    d = (kk >= 64).astype(np.int64) - np.floor_divide(u, 64)
    FB = np.where(((d >= 6) & (d <= 8))[None], t5[_t5_bucket(rel)].transpose(2, 0, 1), np.float32(NEG)).astype(np.float32)
    q = np.arange(32)[None, None, :]
    blk = np.arange(5)[None, :, None]
    kk3 = np.arange(128)[:, None, None]
    rel = 128 * blk + kk3 - 512 - q
    FAs = tab[np.clip(rel, -128, 128) + 128].transpose(3, 0, 1, 2).astype(np.float32)
    blk = np.arange(2)[None, :, None]
    rel = 128 * blk + kk3 - 128 - q
    FBs = t5[_t5_bucket(rel)].transpose(3, 0, 1, 2).astype(np.float32)
    return FA, FB, FAs, FBs


def _rope_tab(pos):
    half = 16
    inv = 10000.0 ** (-jnp.arange(half, dtype=jnp.float32) / half)
    ang = jnp.asarray(pos).astype(jnp.float32)[:, None] * inv[None, :]
    return np.asarray(jnp.cos(ang)), np.asarray(jnp.sin(ang))


def kernel(x_prompt, x_sample, c_prompt, c_sample, cache_a_k, cache_a_v, cache_b_k, cache_b_v,
           cache_c_kv, cache_c_kr, w_ada, b_ada, norm_g, final_norm_g, ffn_w_gate, ffn_w_up, ffn_w_down,
           w_in_ab, w_out_ab, rel_bias_a, t5_bias, sinks_b, w_in_c, c_q_norm_g, c_kv_norm_g, w_qb, w_kvb,
           w_out_c):
    if "nc" not in _NC_CACHE:
        _NC_CACHE["nc"] = build_program()
    nc = _NC_CACHE["nc"]
    in_maps = make_in_maps(x_prompt, x_sample, c_prompt, c_sample, cache_a_k, cache_a_v, cache_b_k, cache_b_v,
                           cache_c_kv, cache_c_kr, w_ada, b_ada, norm_g, final_norm_g, ffn_w_gate, ffn_w_up, ffn_w_down,
                           w_in_ab, w_out_ab, rel_bias_a, t5_bias, sinks_b, w_in_c, c_q_norm_g, c_kv_norm_g, w_qb, w_kvb,
                           w_out_c)
    res = run_bass_kernel_spmd(nc, in_maps, core_ids=list(range(8)))
    return assemble(res.results)


def make_in_maps(x_prompt, x_sample, c_prompt, c_sample, cache_a_k, cache_a_v, cache_b_k, cache_b_v,
                 cache_c_kv, cache_c_kr, w_ada, b_ada, norm_g, final_norm_g, ffn_w_gate, ffn_w_up, ffn_w_down,
                 w_in_ab, w_out_ab, rel_bias_a, t5_bias, sinks_b, w_in_c, c_q_norm_g, c_kv_norm_g, w_qb, w_kvb,
                 w_out_c):
    f = lambda a: np.ascontiguousarray(np.asarray(a, np.float32))
    x_prompt = f(x_prompt); x_sample = f(x_sample)
    with _cpu():
        FA, FB, FAs, FBs = _tables(f(rel_bias_a), f(t5_bias))
    b_adaT = f(b_ada).reshape(2, 72, 128).transpose(2, 0, 1)
    gT = np.concatenate([f(norm_g).reshape(6, 8, 128), f(final_norm_g).reshape(1, 8, 128)], 0).transpose(2, 0, 1)
    qaug = np.zeros((9, TOK), np.float32)
    qc = np.arange(TOK) // 64
    for r in range(8):
        qaug[r] = -(qc < r).astype(np.float32)
    qaug[8] = -1.0
    kaugs = np.zeros((3, 9, TOK), np.float32)
    kidx = np.arange(3 * TOK).reshape(3, TOK)
    kaugs[:, 8, :] = np.where(kidx >= 1056, BIG, 0.0)
    with _cpu():
        cS, sS = _rope_tab(1024 + np.arange(32))
    shared = dict(
        w_ada=f(w_ada), b_adaT=f(b_adaT), gT=f(gT), ffn_w_gate=f(ffn_w_gate), ffn_w_up=f(ffn_w_up), ffn_w_down=f(ffn_w_down),
        w_in_ab=f(w_in_ab)[0], w_out_ab=f(w_out_ab)[0], w_in_c=f(w_in_c)[0], w_qb=f(w_qb)[0], w_kvb=f(w_kvb)[0], w_out_c=f(w_out_c)[0],
        gq=f(np.broadcast_to(f(c_q_norm_g)[0][None], (128, 384))), gkv=f(np.broadcast_to(f(c_kv_norm_g)[0][None], (128, 256))),
        FA=FA, FB=FB, FAs=FAs, FBs=FBs, sinks=f(np.broadcast_to(f(sinks_b)[0][None], (128, 8))),
        cosS=f(np.tile(cS, (1, 16))), sinS=f(np.tile(sS, (1, 16))),
        kaugs=kaugs.astype(ml_dtypes.bfloat16), qaug=qaug.astype(ml_dtypes.bfloat16),
        identf=np.eye(128, dtype=np.float32), identb=np.eye(128, dtype=np.float32).astype(ml_dtypes.bfloat16),
    )
    in_maps = []
    for c in range(8):
        b, j = divmod(c, 4)
        xp = np.zeros((NSEG, (ST + 1) * TOK, D), np.float32)
        hvv = np.zeros((128, 4), np.float32)
        pos = np.zeros(NQT * TOK, np.int64)
        for k in range(NSEG):
            s0 = seg_of(j, k) * ST * TOK
            if s0 == 0:
                xp[k, TOK:] = x_prompt[b, 0:ST * TOK]
                hvv[:, k] = NEG
            else:
                xp[k] = x_prompt[b, s0 - TOK:s0 + ST * TOK]
            pos[k * ST * TOK:(k + 1) * ST * TOK] = s0 + np.arange(ST * TOK)
        with _cpu():
            cP, sP = _rope_tab(pos)
        cosP = np.tile(cP.reshape(NQT * 4, 128, 1, 16), (1, 1, 16, 1)).reshape(NQT * 4, 128, 256)
        sinP = np.tile(sP.reshape(NQT * 4, 128, 1, 16), (1, 1, 16, 1)).reshape(NQT * 4, 128, 256)
        cvec = np.stack([f(c_prompt)[b], f(c_sample)[2 * c], f(c_sample)[2 * c + 1]], -1)
        kaug = np.zeros((NQT, 32, 9, TOK), np.float32)
        kc = np.arange(TOK) // 64
        for qi in range(NQT):
            k, a = divmod(qi, ST)
            T = seg_of(j, k) * ST + a
            for v in range(nv_of(qi)):
                if v == T:
                    for r in range(8):
                        kaug[qi, v, r] = (kc == r) * BIG
                elif v > T:
                    kaug[qi, v, 8] = BIG
        m = dict(shared)
        m.update(
            xp=xp, xs=f(x_sample[2 * c:2 * c + 2]), cT=f(cvec.reshape(8, 128, 3).transpose(1, 0, 2)),
            cak=f(cache_a_k)[0, 2 * c:2 * c + 2].reshape(2, 512, 512), cav=f(cache_a_v)[0, 2 * c:2 * c + 2].reshape(2, 512, 512),
            cbk=f(cache_b_k)[0, 2 * c:2 * c + 2].reshape(2, 128, 128), cbv=f(cache_b_v)[0, 2 * c:2 * c + 2].reshape(2, 128, 128),
            cckv=f(cache_c_kv)[0, 2 * c:2 * c + 2], cckr=f(cache_c_kr)[0, 2 * c:2 * c + 2],
            hv=hvv, cosP=f(cosP), sinP=f(sinP), kaug=kaug.astype(ml_dtypes.bfloat16),
        )
        in_maps.append({k_: np.ascontiguousarray(v_) for k_, v_ in m.items()})
    return in_maps


def assemble(R):
    y_prompt = np.zeros((2, 16384, D), np.float32)
    ckv_p = np.zeros((1, 2, 16384, 256), np.float32)
    ckr_p = np.zeros((1, 2, 16384, 32), np.float32)
    for c in range(8):
        b, j = divmod(c, 4)
        for k in range(NSEG):
            s0 = seg_of(j, k) * ST * TOK
            sl = slice(k * ST * TOK, (k + 1) * ST * TOK)
            y_prompt[b, s0:s0 + ST * TOK] = R[c]["y_p"][sl]
            ckv_p[0, b, s0:s0 + ST * TOK] = R[c]["o_ckv"][sl]
            ckr_p[0, b, s0:s0 + ST * TOK] = R[c]["o_ckr"][sl]
    cat = lambda name: np.concatenate([R[c][name] for c in range(8)], 0)
    y_sample = cat("y_s").reshape(16, 32, D)
    last = [0, 4]
    a_k_p = np.stack([R[c]["o_ak"] for c in last], 0).reshape(1, 2, 512, 8, 64)
    a_v_p = np.stack([R[c]["o_av"] for c in last], 0).reshape(1, 2, 512, 8, 64)
    b_k_p = np.stack([R[c]["o_bk"] for c in last], 0).reshape(1, 2, 128, 2, 64)
    b_v_p = np.stack([R[c]["o_bv"] for c in last], 0).reshape(1, 2, 128, 2, 64)
    return (y_prompt, y_sample, a_k_p, a_v_p, b_k_p, b_v_p, ckv_p, ckr_p,
            cat("o_aks").reshape(1, 16, 32, 8, 64), cat("o_avs").reshape(1, 16, 32, 8, 64),
            cat("o_bks").reshape(1, 16, 32, 2, 64), cat("o_bvs").reshape(1, 16, 32, 2, 64),
            cat("o_ckvs").reshape(1, 16, 32, 256), cat("o_ckrs").reshape(1, 16, 32, 32))
```

```python
import numpy as np
import concourse.bass as bass
import concourse.mybir as mybir

F32 = mybir.dt.float32
BF16 = mybir.dt.bfloat16
AF = mybir.ActivationFunctionType
ALU = mybir.AluOpType

EPOCH = 16000
DMA_EPOCH = 1000


class Sched:
    ENGS = ("pe", "act", "dve", "pool", "sp")

    def __init__(self, nc, same_engine_sync=("act", "dve", "pool")):
        self.nc = nc
        self.ops = []
        self.lastw = {}
        self.lastr = {}
        self.same = set(same_engine_sync)
        self.dma_count = {}

    def _deps(self, reads, writes):
        deps = set()
        for k in reads:
            for d in self.lastw.get(k, {}).values():
                deps.add(d)
        for k in writes:
            for d in self.lastw.get(k, {}).values():
                deps.add(d)
            for d in self.lastr.get(k, {}).values():
                deps.add(d)
        return deps

    def _record(self, idx, agent, reads, writes):
        for k in reads:
            self.lastr.setdefault(k, {})[agent] = idx
        for k in writes:
            self.lastw.setdefault(k, {})[agent] = idx

    def op(self, eng, fn, reads=(), writes=()):
        idx = len(self.ops)
        deps = self._deps(reads, writes)
        self.ops.append(dict(kind="c", eng=eng, fn=fn, deps=deps))
        self._record(idx, eng, reads, writes)
        return idx

    def dma(self, queue, fn, key, reads=(), writes=(), final=False):
        idx = len(self.ops)
        key = key + "_" + queue
        deps = self._deps(reads, writes)
        n = self.dma_count.get(key, 0) + 1
        self.dma_count[key] = n
        self.ops.append(dict(kind="d", eng=queue, fn=fn, deps=deps, key=key, n=n, final=final))
        self._record(idx, "dma:" + key, reads, writes)
        return idx

    def emit(self, stack):
        nc = self.nc
        ops = self.ops
        needed = set()
        for i, o in enumerate(ops):
            for d in o["deps"]:
                od = ops[d]
                if od["kind"] == "c":
                    if od["eng"] == o["eng"] and od["eng"] not in self.same:
                        continue
                    needed.add(d)
        cnt = {e: 0 for e in self.ENGS}
        for i, o in enumerate(ops):
            if o["kind"] == "c" and i in needed:
                cnt[o["eng"]] += 1
                o["ms"] = cnt[o["eng"]]
        sems = {}

        def sem(name):
            if name not in sems:
                sems[name] = stack.enter_context(nc.semaphore(name))
            return sems[name]

        def target(d):
            od = ops[d]
            if od["kind"] == "c":
                m = od["ms"]
                return ("c_%s_%d" % (od["eng"], (m - 1) // EPOCH), (m - 1) % EPOCH + 1)
            n = od["n"]
            return ("d_%s_%d" % (od["key"], (n - 1) // DMA_EPOCH), ((n - 1) % DMA_EPOCH + 1) * 16)

        per_eng = {e: [] for e in self.ENGS}
        for i, o in enumerate(ops):
            per_eng[o["eng"]].append(i)
        for i, o in enumerate(ops):
            if o["kind"] == "c":
                if "ms" in o:
                    sem(target(i)[0])
            else:
                sem(target(i)[0])
        self.n_sems = len(sems)

        def run(engname, eng):
            waited = {}
            for i in per_eng[engname]:
                o = ops[i]
                wl = {}
                for d in o["deps"]:
                    od = ops[d]
                    if od["kind"] == "c" and od["eng"] == engname and engname not in self.same:
                        continue
                    s, v = target(d)
                    if waited.get(s, 0) >= v:
                        continue
                    wl[s] = max(wl.get(s, 0), v)
                for s, v in wl.items():
                    eng.wait_ge(sem(s), v)
                    waited[s] = v
                ins = o["fn"](eng)
                if o["kind"] == "c":
                    if "ms" in o:
                        ins.then_inc(sem(target(i)[0]), 1)
                else:
                    ins.then_inc(sem(target(i)[0]), 16)
            for i in per_eng[engname]:
                o = ops[i]
                if o["kind"] == "d" and o.get("final"):
                    s, v = target(i)
                    if waited.get(s, 0) < v:
                        eng.wait_ge(sem(s), v)
                        waited[s] = v

        with nc.Block() as block:
            @block.sync
            def _(e):
                run("sp", e)

            @block.scalar
            def _(e):
                run("act", e)

            @block.vector
            def _(e):
                run("dve", e)

            @block.gpsimd
            def _(e):
                run("pool", e)

            @block.tensor
            def _(e):
                run("pe", e)

from contextlib import ExitStack
import ml_dtypes
import jax
import jax.numpy as jnp
from concourse.bass_utils import run_bass_kernel_spmd

D = 1024
FF = 2816
NFC = 22
TOK = 512
ST = 4
NSEG = 2
EPS = 1e-6
BIG = 16384.0
MAXDESC = 512
NEG = -1e30
MLA_SCALE = 96 ** -0.5
NQT = NSEG * ST


def seg_of(j, k):
    return j if k == 0 else 7 - j


def nv_of(qi):
    k, a = divmod(qi, ST)
    return (ST * (3 if k == 0 else 7)) + a + 1


def gloc(v):
    s, a = divmod(v, ST)
    jj = s if s < 4 else 7 - s
    kk = 0 if s < 4 else 1
    return jj, (kk * ST + a) * TOK


class Rot:
    def __init__(self, items):
        self.items = list(items)
        self.i = 0

    def next(self):
        r = self.items[self.i % len(self.items)]
        self.i += 1
        return r


def build_program():
    nc = bass.Bass("TRN2", target_bir_lowering=False)
    st = ExitStack()
    S = Sched(nc)

    def din(name, shape, dt=F32):
        return nc.dram_tensor(name, list(shape), dt, kind="ExternalInput").ap()

    def dout(name, shape, dt=F32):
        return nc.dram_tensor(name, list(shape), dt, kind="ExternalOutput").ap()

    def dint(name, shape, dt):
        return nc.dram_tensor(name, list(shape), dt).ap()

    def sb(name, shape, dt):
        return st.enter_context(nc.sbuf_tensor(name, list(shape), dt))

    xp = din("xp", [NSEG, (ST + 1) * TOK, D])
    xs = din("xs", [2, 32, D])
    cT = din("cT", [128, 8, 3])
    cak = din("cak", [2, 512, 512]); cav = din("cav", [2, 512, 512])
    cbk = din("cbk", [2, 128, 128]); cbv = din("cbv", [2, 128, 128])
    cckv = din("cckv", [2, 1024, 256]); cckr = din("cckr", [2, 1024, 32])
    w_ada = din("w_ada", [2, D, 9 * D]); b_adaT = din("b_adaT", [128, 2, 72])
    gT = din("gT", [128, 7, 8])
    wg_d = din("ffn_w_gate", [2, 2, D, FF]); wu_d = din("ffn_w_up", [2, 2, D, FF]); wd_d = din("ffn_w_down", [2, 2, FF, D])
    w_in_ab = din("w_in_ab", [D, 2304]); w_out_ab = din("w_out_ab", [D, D])
    w_in_c = din("w_in_c", [D, 672]); w_qb = din("w_qb", [384, 1536]); w_kvb = din("w_kvb", [256, 2048]); w_out_c = din("w_out_c", [D, D])
    gq_d = din("gq", [128, 384]); gkv_d = din("gkv", [128, 256])
    FA_d = din("FA", [8, 128, 640]); FB_d = din("FB", [8, 128, 256])
    FAs_d = din("FAs", [8, 128, 5, 32]); FBs_d = din("FBs", [8, 128, 2, 32])
    sinks_d = din("sinks", [128, 8]); hv_d = din("hv", [128, 4])
    cosP = din("cosP", [NQT * 4, 128, 256]); sinP = din("sinP", [NQT * 4, 128, 256])
    cosS = din("cosS", [32, 256]); sinS = din("sinS", [32, 256])
    kaug_d = din("kaug", [NQT, 32, 9, TOK], BF16); kaugs_d = din("kaugs", [3, 9, TOK], BF16)
    qaug_d = din("qaug", [9, TOK], BF16)
    identf_d = din("identf", [128, 128]); identb_d = din("identb", [128, 128], BF16)

    y_p = dout("y_p", [NQT * TOK, D]); y_s = dout("y_s", [64, D])
    o_ak = dout("o_ak", [512, 512]); o_av = dout("o_av", [512, 512]); o_bk = dout("o_bk", [128, 128]); o_bv = dout("o_bv", [128, 128])
    o_ckv = dout("o_ckv", [NQT * TOK, 256]); o_ckr = dout("o_ckr", [NQT * TOK, 32])
    o_aks = dout("o_aks", [64, 512]); o_avs = dout("o_avs", [64, 512]); o_bks = dout("o_bks", [64, 128]); o_bvs = dout("o_bvs", [64, 128])
    o_ckvs = dout("o_ckvs", [64, 256]); o_ckrs = dout("o_ckrs", [64, 32])

    x_sp = dint("x_sp", [NQT + 2, 128, 8, TOK], F32)
    q_sp = dint("q_sp", [NQT + 2, 96, 16, TOK], BF16)
    NGC = NQT // 2
    g_in = [dint("g_in%d" % i, [288, 2 * TOK], BF16) for i in range(NGC)]
    g_all = [dint("g_all%d" % i, [4 * 288, 2 * TOK], BF16) for i in range(NGC)]
    g_s = dint("g_s", [2, 288, 3 * TOK], BF16)

    xT = sb("xT", [128, 8, TOK], F32)
    hT = sb("hT", [128, 8, TOK], BF16)
    act = sb("act", [128, 24 * TOK], BF16)
    actc = lambda c: act[:, c * TOK:(c + 1) * TOK]
    xin = act[:, 0:16 * TOK].bitcast(F32).rearrange("p (b d) -> p b d", d=D)
    sq = act[:, 16 * TOK:24 * TOK].rearrange("p (c t) -> p c t", t=TOK)
    yfin = act[:, 0:16 * TOK].bitcast(F32).rearrange("p (c t) -> p c t", t=TOK)
    scr = sb("scr", [128, 6, TOK], F32)
    qo = sb("qo", [128, 16, TOK], BF16)
    kTA = sb("kTA", [64, 2, 8, TOK], BF16)
    kTB = sb("kTB", [64, 2, 2, TOK], BF16)
    VA = sb("VA", [128, 2, 4, 8, 80], BF16)
    VB = sb("VB", [128, 2, 4, 2, 80], BF16)
    Sb = sb("Sb", [128, 2, TOK], F32)
    PT = sb("PT", [128, 3, TOK], BF16)
    fa = sb("fa", [128, 2, 640], F32)
    fb = sb("fb", [128, 2, 256], F32)
    NWP = 5
    wp = sb("wp", [128, NWP, 3072], BF16)
    M = sb("M", [128, 2, 9, 8, 3], F32)
    gTs = sb("gTs", [128, 7, 8], F32)
    scT = sb("scT", [128, 8, 3], F32)
    bT = sb("bT", [128, 2, 72], F32)
    identf = sb("identf_s", [128, 128], F32)
    identb = sb("identb_s", [128, 128], BF16)
    onesb = sb("onesb", [128, 128], BF16)
    onesf = sb("onesf", [128, 64], F32)
    zcol = sb("zcol", [128, 1], F32)
    sinkexp = sb("sinkexp", [128, 8], F32)
    hv = sb("hv_s", [128, 4], F32)
    gq = sb("gq_s", [128, 384], F32)
    gkv = sb("gkv_s", [128, 256], F32)
    den = sb("den", [128, TOK], F32)
    rb = sb("rb", [64, TOK], F32)
    small = sb("small", [128, 8], F32)
    qn_bf = sb("qn_bf", [128, 384], BF16)
    kvn_f = sb("kvn_f", [128, 256], F32)
    kvn_b = sb("kvn_b", [128, 288], BF16)
    kr_f = sb("kr_f", [128, 32], F32)
    q_bf = sb("q_bf", [128, 16, 96], BF16)
    qnT = sb("qnT", [128, 3, TOK], BF16)
    cs = sb("cs", [128, 2, 256], F32)
    rtmp = sb("rtmp", [128, 4, 256], F32)
    stg = sb("stg", [128, 2, 1024], F32)
    gT_sb = sb("gT_sb", [128, 3, TOK], BF16)
    VAflat = VA[:, :, :, :, :].rearrange("p a b c d -> p (a b c d)")
    KT = VAflat[0:105, 0:4 * TOK].rearrange("p (i t) -> p i t", t=TOK)
    Vh = VAflat[:, 4 * TOK:4 * TOK + 1280].rearrange("p (i b d) -> p i b d", b=4, d=80)
    qh = sb("qh", [105, 4, TOK], BF16)
    kvT = sb("kvT", [128, 2, 2, TOK], BF16)
    wkvb = sb("wkvb", [128, 2, 2048], BF16)
    cstage = scr[:, 0:4, :].rearrange("p a b -> p (a b)").rearrange("p (a b) -> p a b", b=256)

    ps = st.enter_context(nc.psum_tensor("ps", [128, 8 * 512], F32))
    bank = lambda i: ps[:, i * 512:(i + 1) * 512]
    Sbanks = Rot([0, 1]); Abanks = Rot([2, 3]); Gbanks = Rot([4, 5, 6, 7])
    wrot = Rot(range(NWP))

    def MM(out, lhsT, rhs, start, stop, R, W):
        S.op("pe", lambda e: e.matmul(out, lhsT, rhs, start=start, stop=stop), R, W)

    def TR(out, in_, ident, R, W):
        S.op("pe", lambda e: e.transpose(out, in_, ident), R, W)

    def ACT(out, in_, func, R, W, bias=None, scale=None, accum=None):
        kw = {}
        if bias is not None:
            kw["bias"] = bias
        if scale is not None:
            kw["scale"] = scale
        if accum is not None:
            kw["accum_out"] = accum
        S.op("act", lambda e: e.activation(out, in_, func, **kw), R, W)

    def TT(eng, out, in0, in1, op, R, W):
        S.op(eng, lambda e: e.tensor_tensor(out, in0, in1, op), R, W)

    def STT(eng, out, in0, scalar, in1, op0, op1, R, W):
        S.op(eng, lambda e: e.scalar_tensor_tensor(out, in0, scalar, in1, op0, op1), R, W)

    def TS(eng, out, in0, s1, s2, op0, op1, R, W):
        S.op(eng, lambda e: e.tensor_scalar(out, in0, s1, 0.0, op0, ALU.add), R, W)

    def CP(eng, out, in_, R, W):
        if eng == "act":
            S.op("act", lambda e: e.copy(out, in_), R, W)
        else:
            S.op(eng, lambda e: e.tensor_copy(out, in_), R, W)

    def RCP(out, in_, R, W):
        S.op("dve", lambda e: e.reciprocal(out, in_), R, W)

    def MSET(eng, ap, val, W):
        S.op(eng, lambda e: e.memset(ap, val), (), W)

    def DMA(q, out, in_, key, R, W, final=False):
        if type(out.tensor).__name__ == "DRamTensorHandle":
            q = "pool"
        S.dma(q, lambda e: e.dma_start(out=out, in_=in_), key, R, W, final=final)

    cp_rot = Rot(["act", "dve"])

    WREG = {}
    wscr_off = [0]
    wscr = dint("wscr", [48 * 1024 * 1024], BF16)

    def wreg(gkey, idx, src_ap):
        shp = list(src_ap.shape)
        n = 1
        for d in shp:
            n *= d
        off = wscr_off[0]
        wscr_off[0] += n
        flat = wscr[off:off + n]
        if len(shp) == 3:
            dst = flat.rearrange("(p a b) -> p a b", a=shp[1], b=shp[2])
            step = max(1, MAXDESC // shp[0])
            for a0 in range(0, shp[1], step):
                a1 = min(shp[1], a0 + step)
                DMA("pool", dst[:, a0:a1, :], src_ap[:, a0:a1, :], "cv_" + gkey, [], ["W_" + gkey])
        else:
            dst = flat.rearrange("(p a) -> p a", a=shp[1])
            DMA("pool", dst, src_ap, "cv_" + gkey, [], ["W_" + gkey])
        WREG[(gkey, idx)] = (flat.rearrange("(p a) -> p a", p=shp[0]), shp)

    def wpiece(gkey, idx):
        src2, shp = WREG[(gkey, idx)]
        i = wrot.next()
        n = src2.shape[1]
        assert n <= 3072, n
        dst = wp[0:shp[0], i, 0:n]
        DMA("sp", dst, src2, "wp%d" % i, ["W_" + gkey], ["wp%d" % i])
        if len(shp) == 3:
            dst = dst.rearrange("p (a b) -> p a b", b=shp[2])
        return dst, "wp%d" % i

    w_ab = w_in_ab.rearrange("(k p) n -> p k n", p=128)
    w_c = w_in_c.rearrange("(k p) n -> p k n", p=128)
    w_qbv = w_qb.rearrange("(k p) n -> p k n", p=128)

    def reg_ffn(l, i):
        wg = wg_d[l, i].rearrange("(k p) f -> p k f", p=128)
        wu = wu_d[l, i].rearrange("(k p) f -> p k f", p=128)
        wd = wd_d[l, i].rearrange("(f p) d -> p f d", p=128)
        for pc in range(11):
            wreg("wg%d%d" % (l, i), pc, wg[:, :, pc * 256:(pc + 1) * 256])
        for pc in range(11):
            wreg("wu%d%d" % (l, i), pc, wu[:, :, pc * 256:(pc + 1) * 256])
        for dc in range(8):
            wreg("wd%d%d" % (l, i), dc, wd[:, :, dc * 128:(dc + 1) * 128])

    def reg_all():
        reg_ffn(0, 0)
        for p0 in range(0, 2304, 256):
            wreg("wab", p0 // 256, w_ab[:, :, p0:p0 + 256])
        wv = w_out_ab.rearrange("(h p) d -> p h d", p=64)
        for dc in range(8):
            wreg("woab", dc, wv[:, :, dc * 128:(dc + 1) * 128])
        reg_ffn(0, 1)
        reg_ffn(1, 0)
        wreg("wc", 0, w_c[:, :, 0:384])
        wreg("wc", 1, w_c[:, :, 384:672])
        for cb in range(3):
            wreg("wqb", cb, w_qbv[:, :, cb * 512:(cb + 1) * 512])
        wv = w_out_c.rearrange("(h p) d -> p h d", p=64)
        for dc in range(8):
            wreg("woc", dc, wv[:, :, dc * 128:(dc + 1) * 128])
        reg_ffn(1, 1)

    reg_all()

    DMA("sp", identf[:], identf_d, "identf", [], ["identf"])
    DMA("sp", identb[:], identb_d, "identb", [], ["identb"])
    DMA("sp", gTs[:], gT, "gTs", [], ["gTs"])
    DMA("sp", bT[:], b_adaT, "bT", [], ["bT"])
    DMA("sp", scT[:], cT, "scT", [], ["scT"])
    DMA("sp", hv[:], hv_d, "hv", [], ["hv"])
    DMA("sp", gq[:], gq_d, "gq", [], ["gq"])
    DMA("sp", gkv[:], gkv_d, "gkv", [], ["gkv"])
    DMA("sp", sinkexp[:], sinks_d, "sinkexp", [], ["sinkexp"])
    MSET("dve", onesb[:], 1.0, ["onesb"])
    MSET("dve", onesf[:], 1.0, ["onesf"])
    MSET("dve", zcol[:], 0.0, ["zcol"])
    MSET("dve", VA[:, :, :, :, 64:65], 1.0, ["VAones"])
    MSET("dve", VB[:, :, :, :, 64:65], 1.0, ["VBones"])
    ACT(sinkexp[:], sinkexp[:], AF.Exp, ["sinkexp"], ["sinkexp"])
    ACT(scT[:], scT[:], AF.Silu, ["scT"], ["scT"])
    for hh in range(4):
        DMA("sp", qh[96:105, hh, :], qaug_d, "qhaug", [], ["qhaug"])
    DMA("pool", wkvb[:], w_kvb.rearrange("(c p) n -> p c n", p=128), "wkvb", [], ["wkvb"])

    for l in range(2):
        for m in range(9):
            for half in range(2):
                if (m * 2 + half) % 2 == 0:
                    wst = act[:, 0:16 * TOK].bitcast(F32).rearrange("p (k n) -> p k n", n=512)
                    wkeys = ["act%d" % c for c in range(16)]; wsk = "wstA"
                else:
                    wst = qo[:, :, :].rearrange("p a b -> p (a b)").bitcast(F32).rearrange("p (k n) -> p k n", n=512)
                    wkeys = ["qo%d" % c for c in range(16)]; wsk = "wstB"
                src = w_ada[l].rearrange("(k p) n -> p k n", p=128)[:, :, m * 1024 + half * 512: m * 1024 + half * 512 + 512]
                DMA("sp", wst, src, wsk, [], wkeys)
                bk = Gbanks.next()
                for oc in range(4):
                    for k in range(8):
                        MM(bank(bk)[:, oc * 4:oc * 4 + 3], wst[:, k, oc * 128:(oc + 1) * 128], scT[:, k, :], k == 0, k == 7,
                           wkeys + ["scT"], ["ps%d" % bk])
                for oc in range(4):
                    ch = half * 4 + oc
                    TS("dve", M[:, l, m, ch, :], bank(bk)[:, oc * 4:oc * 4 + 3], bT[:, l, m * 8 + ch:m * 8 + ch + 1], None, ALU.add, None,
                       ["ps%d" % bk, "bT"], ["M"])
    for l in range(2):
        for n in range(3):
            for si in range(3):
                STT("dve", M[:, l, 3 * n + 1, :, si], M[:, l, 3 * n + 1, :, si], 1.0, gTs[:, 3 * l + n, :], ALU.add, ALU.mult, ["M", "gTs"], ["M"])
        for m in (2, 8):
            TS("dve", M[:, l, m, :, :], M[:, l, m, :, :], 0.5, None, ALU.mult, None, ["M"], ["M"])

    def norm_mod(ncol, scale_of, shift_of, out_of, out_keys):
        ACT(sq[:, :, 0:ncol], xT[:, :, 0:ncol], AF.Square, ["xT"], ["act%d" % (16 + c) for c in range(8)])
        bk = Gbanks.next()
        for c in range(8):
            MM(bank(bk)[:, 0:ncol], onesb[:], sq[:, c, 0:ncol], c == 0, c == 7, ["onesb", "act%d" % (16 + c)], ["ps%d" % bk])
        ACT(scr[:, 0, 0:ncol], bank(bk)[:, 0:ncol], AF.Sqrt, ["ps%d" % bk], ["scr0"], bias=epscol[:], scale=1.0 / D)
        RCP(scr[:, 1, 0:ncol], scr[:, 0, 0:ncol], ["scr0"], ["scr1"])
        for c in range(8):
            tb_ = 2 + (c % 2)
            TT("dve", scr[:, tb_, 0:ncol], xT[:, c, 0:ncol], scr[:, 1, 0:ncol], ALU.mult, ["xT", "scr1"], ["scr%d" % tb_])
            kw = dict(scale=scale_of(c))
            sh = shift_of(c)
            ACT(out_of(c), scr[:, tb_, 0:ncol], AF.Identity, ["scr%d" % tb_, "M", "gTs"], out_keys(c), bias=(sh if sh is not None else zcol[:]), **kw)

    def ffn(l, i, ncol, si, gate_m):
        for pc in range(11):
            g_ap, gk = wpiece("wg%d%d" % (l, i), pc)
            u_ap, uk = wpiece("wu%d%d" % (l, i), pc)
            for sub in range(2):
                fc = pc * 2 + sub
                bg = Gbanks.next(); bu = Gbanks.next()
                for k in range(8):
                    MM(bank(bg)[:, 0:ncol], g_ap[:, k, sub * 128:(sub + 1) * 128], hT[:, k, 0:ncol], k == 0, k == 7, [gk, "hT"], ["ps%d" % bg])
                for k in range(8):
                    MM(bank(bu)[:, 0:ncol], u_ap[:, k, sub * 128:(sub + 1) * 128], hT[:, k, 0:ncol], k == 0, k == 7, [uk, "hT"], ["ps%d" % bu])
                sgi = 4 + (fc % 2)
                ACT(scr[:, sgi, 0:ncol], bank(bg)[:, 0:ncol], AF.Silu, ["ps%d" % bg], ["scr%d" % sgi])
                TT("dve", actc(fc)[:, 0:ncol], bank(bu)[:, 0:ncol], scr[:, sgi, 0:ncol], ALU.mult, ["ps%d" % bu, "scr%d" % sgi], ["act%d" % fc])
        for dc in range(8):
            d_ap, dk = wpiece("wd%d%d" % (l, i), dc)
            by = Gbanks.next()
            for fc in range(NFC):
                MM(bank(by)[:, 0:ncol], d_ap[:, fc, :], actc(fc)[:, 0:ncol], fc == 0, fc == NFC - 1, [dk, "act%d" % fc], ["ps%d" % by])
            STT("dve", xT[:, dc, 0:ncol], bank(by)[:, 0:ncol], M[:, l, gate_m, dc, si:si + 1], xT[:, dc, 0:ncol], ALU.mult, ALU.add,
                ["ps%d" % by, "M", "xT"], ["xT"])

    def mod_norm(l, n, ncol, si):
        norm_mod(ncol,
                 lambda c: M[:, l, 3 * n + 1, c, si:si + 1],
                 lambda c: M[:, l, 3 * n + 0, c, si:si + 1],
                 lambda c: hT[:, c, 0:ncol],
                 lambda c: ["hT"])

    def attn_S(kT_ap, q_ap, nk, ncols, scale, bias_ap, pbias_ap, R):
        bs = Sbanks.next()
        MM(bank(bs)[0:nk, 0:ncols], kT_ap, q_ap, True, True, R, ["ps%d" % bs])
        pi = PTrot.next()
        if bias_ap is not None:
            si_ = Sbrot.next()
            STT("dve", Sb[0:nk, si_, 0:ncols], bank(bs)[0:nk, 0:ncols], scale, bias_ap, ALU.mult, ALU.add, ["ps%d" % bs] + R, ["Sb%d" % si_])
            ACT(PT[0:nk, pi, 0:ncols], Sb[0:nk, si_, 0:ncols], AF.Exp, ["Sb%d" % si_, "hv"], ["PT%d" % pi], bias=pbias_ap)
        else:
            ACT(PT[0:nk, pi, 0:ncols], bank(bs)[0:nk, 0:ncols], AF.Exp, ["ps%d" % bs], ["PT%d" % pi], scale=scale)
        return pi

    def attn_PV(pi, nk, ncols, v_ap, acc_ap, first, R, accW, last):
        MM(acc_ap, v_ap, PT[0:nk, pi, 0:ncols], first, last, ["PT%d" % pi] + R, accW)

    def attn_block(kT_ap, q_ap, nk, ncols, scale, bias_ap, pbias_ap, v_ap, acc_ap, first, R, accW, last=False):
        pi = attn_S(kT_ap, q_ap, nk, ncols, scale, bias_ap, pbias_ap, R)
        attn_PV(pi, nk, ncols, v_ap, acc_ap, first, R, accW, last)

    def attn_finish(ba, ncol, dest_ap, destW, sink_ap=None):
        if sink_ap is not None:
            TS("dve", den[64:65, 0:ncol], bank(ba)[64:65, 0:ncol], sink_ap, None, ALU.add, None, ["ps%d" % ba, "sinkexp"], ["den"])
        else:
            CP("dve", den[64:65, 0:ncol], bank(ba)[64:65, 0:ncol], ["ps%d" % ba], ["den"])
        bb = Gbanks.next()
        MM(bank(bb)[0:64, 0:ncol], onesf[64:65, 0:64], den[64:65, 0:ncol], True, True, ["onesf", "den"], ["ps%d" % bb])
        RCP(rb[:, 0:ncol], bank(bb)[0:64, 0:ncol], ["ps%d" % bb], ["rb"])
        TT("dve", dest_ap, bank(ba)[0:64, 0:ncol], rb[:, 0:ncol], ALU.mult, ["ps%d" % ba, "rb"], destW)

    CS_ALL = ["scr0", "scr1", "scr2", "scr3"]
    PTrot = Rot(range(3)); Sbrot = Rot(range(2)); farot = Rot(range(2)); fbrot = Rot(range(2)); stgrot = Rot(range(2))
    epscol = sb("epscol", [128, 1], F32)
    MSET("dve", epscol[:], EPS, ["epscol"])

    def l0_mixer(kind, ncol, si, slot, tblocks, seg, is_last, samp_idx):
        do_q = kind != "halo"
        plan = []
        if do_q:
            plan += [("q", h, 0 + 64 * h) for h in range(8)]
            plan += [("q", 8 + h, 1536 + 64 * h) for h in range(8)]
        plan += [("ka", h, 512 + 64 * h) for h in range(8)]
        plan += [("kb", h, 2048 + 64 * h) for h in range(2)]
        cur_piece = None
        for (typ, h, col) in plan:
            p0 = (col // 256) * 256
            if cur_piece is None or cur_piece[0] != p0:
                ap_, k_ = wpiece("wab", p0 // 256)
                cur_piece = (p0, ap_, k_)
            _, w_ap, wk = cur_piece
            bk = Gbanks.next()
            for k in range(8):
                MM(bank(bk)[0:64, 0:ncol], w_ap[:, k, col - p0:col - p0 + 64], hT[:, k, 0:ncol], k == 0, k == 7, [wk, "hT"], ["ps%d" % bk])
            if typ == "q":
                dst, W = qo[0:64, h, 0:ncol], ["qo%d" % h]
            elif typ == "ka":
                dst, W = kTA[:, slot, h, 0:ncol], ["kTA%d" % slot]
            else:
                dst, W = kTB[:, slot, h, 0:ncol], ["kTB%d" % slot]
            CP(cp_rot.next(), dst, bank(bk)[0:64, 0:ncol], ["ps%d" % bk], W)
        if SUB <= 4.2:
            return
        want_out = (is_last or kind == "samp") and SUB != 4.41
        for (tbi, (t0, nt)) in enumerate(tblocks):
            jobs = [("va", 1024, 512)]
            jobs += [("vb", 2176, 128)]
            if want_out:
                jobs += [("ka_o", 512, 512), ("kb_o", 2048, 128)]
            for (typ, col, wdt) in jobs:
                bk = Gbanks.next()
                for half in range(wdt // 256 if wdt >= 256 else 1):
                    cw = min(256, wdt)
                    p0 = ((col + half * 256) // 256) * 256
                    w_ap, wk = wpiece("wab", p0 // 256)
                    off = col + half * 256 - p0
                    for k in range(8):
                        MM(bank(bk)[0:nt, half * 256:half * 256 + cw], hT[:, k, t0:t0 + nt], w_ap[:, k, off:off + cw], k == 0, k == 7,
                           [wk, "hT"], ["ps%d" % bk])
                if typ == "va":
                    CP("act", VA[0:nt, slot, tbi, :, 0:64], bank(bk)[0:nt, 0:512].rearrange("p (h d) -> p h d", d=64), ["ps%d" % bk], ["VA%d" % slot])
                elif typ == "vb":
                    CP("dve", VB[0:nt, slot, tbi, :, 0:64], bank(bk)[0:nt, 0:128].rearrange("p (h d) -> p h d", d=64), ["ps%d" % bk], ["VB%d" % slot])
                if want_out:
                    sgi = stgrot.next()
                    CP("act" if typ in ("va", "ka_o") else "dve", stg[0:nt, sgi, 0:wdt], bank(bk)[0:nt, 0:wdt], ["ps%d" % bk], ["stg%d" % sgi])
                    if SUB == 4.42:
                        continue
                    if kind == "samp":
                        r0 = samp_idx * 32
                        dst = {"va": o_avs, "vb": o_bvs, "ka_o": o_aks, "kb_o": o_bks}[typ][r0:r0 + 32, :]
                        DMA(OUTQ, dst, stg[0:nt, sgi, 0:wdt], "stg%d" % sgi, ["stg%d" % sgi], [], final=True)
                    else:
                        if typ in ("va", "ka_o"):
                            dst = (o_av if typ == "va" else o_ak)[t0:t0 + nt, :]
                            DMA(OUTQ, dst, stg[0:nt, sgi, 0:wdt], "stg%d" % sgi, ["stg%d" % sgi], [], final=True)
                        elif tbi == 3:
                            dst = (o_bv if typ == "vb" else o_bk)[:, :]
                            DMA(OUTQ, dst, stg[0:nt, sgi, 0:wdt], "stg%d" % sgi, ["stg%d" % sgi], [], final=True)
        if not do_q or SUB <= 4.42:
            return
        if kind == "prompt":
            prev = 1 - slot
            for h in range(8):
                fi = farot.next()
                DMA("sp", fa[:, fi, :], FA_d[h], "fa%d" % fi, [], ["fa%d" % fi])
                ba = Abanks.next()
                first = True
                pend = []
                for m in (3, 4, 0, 1, 2, 5, 6, 7):
                    q0 = max(0, 128 * m - 512); q1 = min(512, 128 * m + 128)
                    sl = prev if m < 4 else slot
                    blk = m % 4
                    pb = hv[:, seg:seg + 1] if (m < 4) else zcol[:]
                    if m < 4 and not first_own[0]:
                        pb = zcol[:]
                    R_ = ["kTA%d" % sl, "qo%d" % h, "fa%d" % fi, "VA%d" % sl, "VAones"]
                    pi_ = attn_S(kTA[:, sl, h, blk * 128:(blk + 1) * 128], qo[0:64, h, q0:q1], 128, q1 - q0, 0.125,
                                 fa[:, fi, q0 - 128 * m + 512:q1 - 128 * m + 512], pb, R_)
                    if len(pend) >= 2:
                        attn_PV(*pend.pop(0))
                    pend.append((pi_, 128, q1 - q0, VA[:, sl, blk, h, 0:65], bank(ba)[0:65, q0:q1], first, R_, ["ps%d" % ba], m == 7))
                    first = False
                while pend:
                    attn_PV(*pend.pop(0))
                attn_finish(ba, 512, qo[0:64, h, :], ["qo%d" % h])
            for h in range(8):
                fi = fbrot.next()
                DMA("sp", fb[:, fi, :], FB_d[h], "fb%d" % fi, [], ["fb%d" % fi])
                ba = Abanks.next()
                g = h // 4
                first = True
                pend = []
                for m in (4, 6, 3, 5, 7):
                    q0 = max(0, 128 * m - 512); q1 = min(512, 128 * m - 256)
                    sl = prev if m < 4 else slot
                    blk = m % 4
                    pb = hv[:, seg:seg + 1] if (m < 4 and first_own[0]) else zcol[:]
                    R_ = ["kTB%d" % sl, "qo%d" % (8 + h), "fb%d" % fi, "VB%d" % sl, "VBones"]
                    pi_ = attn_S(kTB[:, sl, g, blk * 128:(blk + 1) * 128], qo[0:64, 8 + h, q0:q1], 128, q1 - q0, 0.125,
                                 fb[:, fi, q0 - 128 * m + 512:q1 - 128 * m + 512], pb, R_)
                    if len(pend) >= 2:
                        attn_PV(*pend.pop(0))
                    pend.append((pi_, 128, q1 - q0, VB[:, sl, blk, g, 0:65], bank(ba)[0:65, q0:q1], first, R_, ["ps%d" % ba], m == 7))
                    first = False
                while pend:
                    attn_PV(*pend.pop(0))
                attn_finish(ba, 512, qo[0:64, 8 + h, :], ["qo%d" % (8 + h)], sink_ap=sinkexp[64:65, h:h + 1])
        else:
            s_ = samp_idx
            csl = 1 - slot
            ck = cstage[:, 0:8, :].rearrange("p a b -> p (a b)")[:, 0:2048].rearrange("p (b n) -> p b n", n=512)
            DMA("sp", ck, cak[s_].rearrange("(b p) n -> p b n", p=128), "cstage", [], CS_ALL)
            for blk in range(4):
                for h in range(8):
                    bk = Gbanks.next()
                    TR(bank(bk)[0:64, 0:128], ck[:, blk, h * 64:(h + 1) * 64], identf[:], CS_ALL + ["identf"], ["ps%d" % bk])
                    CP(cp_rot.next(), kTA[:, csl, h, blk * 128:(blk + 1) * 128], bank(bk)[0:64, 0:128], ["ps%d" % bk], ["kTA%d" % csl])
            for blk in range(4):
                for hh in range(2):
                    DMA("pool", VA[:, csl, blk, hh * 4:(hh + 1) * 4, 0:64],
                        cav[s_, blk * 128:(blk + 1) * 128, hh * 256:(hh + 1) * 256].rearrange("p (h d) -> p h d", d=64), "VA%d" % csl, [], ["VA%d" % csl])
            for h in range(8):
                fi = farot.next()
                DMA("sp", fa[:, fi, 0:160].rearrange("p (b q) -> p b q", q=32), FAs_d[h], "fa%d" % fi, [], ["fa%d" % fi])
                fav = fa[:, fi, 0:160].rearrange("p (b q) -> p b q", q=32)
                ba = Abanks.next()
                for blk in range(5):
                    if blk < 4:
                        kT_ap = kTA[:, csl, h, blk * 128:(blk + 1) * 128]; v_ap = VA[:, csl, blk, h, 0:65]; nk = 128
                        R = ["kTA%d" % csl, "VA%d" % csl]
                    else:
                        kT_ap = kTA[:, slot, h, 0:32]; v_ap = VA[0:32, slot, 0, h, 0:65]; nk = 32
                        R = ["kTA%d" % slot, "VA%d" % slot]
                    attn_block(kT_ap, qo[0:64, h, 0:32], nk, 32, 0.125, fav[0:nk, blk, :], zcol[0:nk, :], v_ap, bank(ba)[0:65, 0:32], blk == 0,
                               R + ["qo%d" % h, "fa%d" % fi, "VAones"], ["ps%d" % ba], last=(blk == 4))
                attn_finish(ba, 32, qo[0:64, h, 0:32], ["qo%d" % h])
            if SUB <= 4.6:
                return
            ckb = cstage[:, 0, 0:128]
            DMA("sp", ckb, cbk[s_], "cstage", [], CS_ALL)
            for g in range(2):
                bk = Gbanks.next()
                TR(bank(bk)[0:64, 0:128], ckb[:, g * 64:(g + 1) * 64], identf[:], CS_ALL + ["identf"], ["ps%d" % bk])
                CP(cp_rot.next(), kTB[:, csl, g, 0:128], bank(bk)[0:64, 0:128], ["ps%d" % bk], ["kTB%d" % csl])
            DMA("pool", VB[:, csl, 0, :, 0:64], cbv[s_].rearrange("p (h d) -> p h d", d=64), "VB%d" % csl, [], ["VB%d" % csl])
            for h in range(8):
                fi = fbrot.next()
                DMA("sp", fb[:, fi, 0:64].rearrange("p (b q) -> p b q", q=32), FBs_d[h], "fb%d" % fi, [], ["fb%d" % fi])
                fbv = fb[:, fi, 0:64].rearrange("p (b q) -> p b q", q=32)
                ba = Abanks.next()
                g = h // 4
                for blk in range(2):
                    if blk == 0:
                        kT_ap = kTB[:, csl, g, 0:128]; v_ap = VB[:, csl, 0, g, 0:65]; nk = 128; R = ["kTB%d" % csl, "VB%d" % csl]
                    else:
                        kT_ap = kTB[:, slot, g, 0:32]; v_ap = VB[0:32, slot, 0, g, 0:65]; nk = 32; R = ["kTB%d" % slot, "VB%d" % slot]
                    attn_block(kT_ap, qo[0:64, 8 + h, 0:32], nk, 32, 0.125, fbv[0:nk, blk, :], zcol[0:nk, :], v_ap, bank(ba)[0:65, 0:32], blk == 0,
                               R + ["qo%d" % (8 + h), "fb%d" % fi, "VBones"], ["ps%d" % ba], last=(blk == 1))
                attn_finish(ba, 32, qo[0:64, 8 + h, 0:32], ["qo%d" % (8 + h)], sink_ap=sinkexp[64:65, h:h + 1])

    def out_proj(w_dram, ncol, l, si, gate_m):
        for dc in range(8):
            w_ap, wk = wpiece(w_dram, dc)
            by = Gbanks.next()
            for h in range(16):
                MM(bank(by)[:, 0:ncol], w_ap[:, h, :], qo[0:64, h, 0:ncol], h == 0, h == 15, [wk, "qo%d" % h], ["ps%d" % by])
            STT("dve", xT[:, dc, 0:ncol], bank(by)[:, 0:ncol], M[:, l, gate_m, dc, si:si + 1], xT[:, dc, 0:ncol], ALU.mult, ALU.add,
                ["ps%d" % by, "M", "xT"], ["xT"])

    def l1_prep(kind, ncol, tblocks, tile_idx, samp_idx):
        for (tbi, (t0, nt)) in enumerate(tblocks):
            wq_ap, wqk = wpiece("wc", 0)
            bq = Gbanks.next()
            for k in range(8):
                MM(bank(bq)[0:nt, 0:384], hT[:, k, t0:t0 + nt], wq_ap[:, k, :], k == 0, k == 7, [wqk, "hT"], ["ps%d" % bq])
            wk_ap, wkk = wpiece("wc", 1)
            bk2 = Gbanks.next()
            for k in range(8):
                MM(bank(bk2)[0:nt, 0:288], hT[:, k, t0:t0 + nt], wk_ap[:, k, :], k == 0, k == 7, [wkk, "hT"], ["ps%d" % bk2])
            ACT(scr[0:nt, 2, 0:384], bank(bq)[0:nt, 0:384], AF.Square,
                ["ps%d" % bq], ["scr2", "small"], accum=small[0:nt, 0:1])
            ACT(small[0:nt, 1:2], small[0:nt, 0:1], AF.Sqrt, ["small"], ["small"], bias=epscol[0:nt, :], scale=1.0 / 384)
            RCP(small[0:nt, 2:3], small[0:nt, 1:2], ["small"], ["small"])
            STT("dve", qn_bf[0:nt, :], bank(bq)[0:nt, 0:384], small[0:nt, 2:3], gq[0:nt, :], ALU.mult, ALU.mult, ["ps%d" % bq, "small", "gq"], ["qn_bf"])
            ACT(scr[0:nt, 3, 0:256], bank(bk2)[0:nt, 0:256], AF.Square, ["ps%d" % bk2], ["scr3", "small"], accum=small[0:nt, 3:4])
            ACT(small[0:nt, 4:5], small[0:nt, 3:4], AF.Sqrt, ["small"], ["small"], bias=epscol[0:nt, :], scale=1.0 / 256)
            RCP(small[0:nt, 5:6], small[0:nt, 4:5], ["small"], ["small"])
            STT("dve", kvn_f[0:nt, :], bank(bk2)[0:nt, 0:256], small[0:nt, 5:6], gkv[0:nt, :], ALU.mult, ALU.mult, ["ps%d" % bk2, "small", "gkv"], ["kvn_f"])
            CP("act", kvn_b[0:nt, 0:256], kvn_f[0:nt, :], ["kvn_f"], ["kvn_b"])
            if kind == "samp":
                DMA("sp", cs[0:nt, 0, :], cosS, "cs", [], ["cs"])
                DMA("sp", cs[0:nt, 1, :], sinS, "cs", [], ["cs"])
            else:
                DMA("sp", cs[:, 0, :], cosP[tile_idx * 4 + tbi], "cs", [], ["cs"])
                DMA("sp", cs[:, 1, :], sinP[tile_idx * 4 + tbi], "cs", [], ["cs"])
            x1 = bank(bk2)[0:nt, 256:272]; x2 = bank(bk2)[0:nt, 272:288]
            c16 = cs[0:nt, 0, 0:16]; s16 = cs[0:nt, 1, 0:16]
            TT("dve", rtmp[0:nt, 0, 0:16], x1, c16, ALU.mult, ["ps%d" % bk2, "cs"], ["rtmp0"])
            TT("dve", rtmp[0:nt, 1, 0:16], x2, s16, ALU.mult, ["ps%d" % bk2, "cs"], ["rtmp1"])
            TT("dve", kr_f[0:nt, 0:16], rtmp[0:nt, 0, 0:16], rtmp[0:nt, 1, 0:16], ALU.subtract, ["rtmp0", "rtmp1"], ["kr_f"])
            TT("dve", rtmp[0:nt, 2, 0:16], x1, s16, ALU.mult, ["ps%d" % bk2, "cs"], ["rtmp2"])
            TT("dve", rtmp[0:nt, 3, 0:16], x2, c16, ALU.mult, ["ps%d" % bk2, "cs"], ["rtmp3"])
            TT("dve", kr_f[0:nt, 16:32], rtmp[0:nt, 2, 0:16], rtmp[0:nt, 3, 0:16], ALU.add, ["rtmp2", "rtmp3"], ["kr_f"])
            CP("act", kvn_b[0:nt, 256:288], kr_f[0:nt, :], ["kr_f"], ["kvn_b"])
            if kind == "samp":
                r0 = samp_idx * 32
                DMA("sp", o_ckvs[r0:r0 + 32, :], kvn_f[0:nt, :], "kvn_f", ["kvn_f"], [], final=True)
                DMA("sp", o_ckrs[r0:r0 + 32, :], kr_f[0:nt, :], "kr_f", ["kr_f"], [], final=True)
            else:
                r0 = tile_idx * TOK + t0
                DMA("sp", o_ckv[r0:r0 + nt, :], kvn_f[0:nt, :], "kvn_f", ["kvn_f"], [], final=True)
                DMA("sp", o_ckr[r0:r0 + nt, :], kr_f[0:nt, :], "kr_f", ["kr_f"], [], final=True)
            bt = Gbanks.next()
            btv = bank(bt).bitcast(BF16)
            for c in range(3):
                w_ = 128 if c < 2 else 32
                TR(btv[0:w_, c * 128:c * 128 + nt], kvn_b[0:nt, c * 128:c * 128 + w_], identb[0:nt, 0:nt], ["kvn_b", "identb"], ["ps%d" % bt])
            CP("dve", gT_sb[:, 0:2, t0:t0 + nt], btv[:, 0:256].rearrange("p (c t) -> p c t", t=128)[:, :, 0:nt], ["ps%d" % bt], ["gT_sb"])
            CP("dve", gT_sb[0:32, 2, t0:t0 + nt], btv[0:32, 256:256 + nt], ["ps%d" % bt], ["gT_sb"])
            bt2 = Gbanks.next()
            bt2v = bank(bt2).bitcast(BF16)
            for c in range(3):
                TR(bt2v[:, c * 128:c * 128 + nt], qn_bf[0:nt, c * 128:(c + 1) * 128], identb[0:nt, 0:nt], ["qn_bf", "identb"], ["ps%d" % bt2])
            CP("act", qnT[:, :, t0:t0 + nt], bt2v[:, 0:384].rearrange("p (c t) -> p c t", t=128)[:, :, 0:nt], ["ps%d" % bt2], ["qnT"])
            for cb in range(3):
                wq2, wq2k = wpiece("wqb", cb)
                bqq = Gbanks.next()
                for k in range(3):
                    MM(bank(bqq)[0:nt, 0:512], qnT[:, k, t0:t0 + nt], wq2[:, k, :], k == 0, k == 2, [wq2k, "qnT"], ["ps%d" % bqq])
                CP(cp_rot.next(), qraw[0:nt, cb * 512:(cb + 1) * 512],
                   bank(bqq)[0:nt, 0:512], ["ps%d" % bqq], ["stg0", "stg1"])
            qv = qraw[0:nt, :].rearrange("p (h d) -> p h d", d=96)
            CP("act", q_bf[0:nt, :, 0:64], qv[:, :, 0:64], ["stg0", "stg1"], ["q_bf"])
            cosv = cs[0:nt, 0, :].rearrange("p (h d) -> p h d", d=16); sinv = cs[0:nt, 1, :].rearrange("p (h d) -> p h d", d=16)
            r3 = lambda i: rtmp[0:nt, i, :].rearrange("p (h d) -> p h d", d=16)
            TT("dve", r3(0), qv[:, :, 64:80], cosv, ALU.mult, ["stg0", "stg1", "cs"], ["rtmp0"])
            TT("dve", r3(1), qv[:, :, 80:96], sinv, ALU.mult, ["stg0", "stg1", "cs"], ["rtmp1"])
            TT("dve", q_bf[0:nt, :, 64:80], r3(0), r3(1), ALU.subtract, ["rtmp0", "rtmp1"], ["q_bf"])
            TT("dve", r3(2), qv[:, :, 64:80], sinv, ALU.mult, ["stg0", "stg1", "cs"], ["rtmp2"])
            TT("dve", r3(3), qv[:, :, 80:96], cosv, ALU.mult, ["stg0", "stg1", "cs"], ["rtmp3"])
            TT("dve", q_bf[0:nt, :, 80:96], r3(2), r3(3), ALU.add, ["rtmp2", "rtmp3"], ["q_bf"])
            for hg in range(2):
                bt3 = Gbanks.next()
                bt3v = bank(bt3).bitcast(BF16)
                for hh in range(8):
                    h = hg * 8 + hh
                    TR(bt3v[0:96, hh * 128:hh * 128 + nt], q_bf[0:nt, h, :], identb[0:nt, 0:nt], ["q_bf", "identb"], ["ps%d" % bt3])
                CP(cp_rot.next(), qo[0:96, hg * 8:(hg + 1) * 8, t0:t0 + nt], bt3v[0:96, :].rearrange("p (h t) -> p h t", t=128)[:, :, 0:nt],
                   ["ps%d" % bt3], ["qo%d" % h_ for h_ in range(hg * 8, hg * 8 + 8)])

    qraw = stg[:, :, :].rearrange("p a b -> p (a b)")[:, 0:1536]
    first_own = [False]

    def phase1_tile(kind, ncol, si, slot, seg, a, tile_idx, samp_idx):
        tblocks = [(i * 128, 128) for i in range(4)] if kind != "samp" else [(0, 32)]
        if kind == "samp":
            DMA("sp", xin[0:32, 0, :], xs[samp_idx], "xin", [], ["act%d" % c for c in range(4)])
        else:
            r0 = (a + 1) * TOK if kind == "prompt" else 0
            DMA("sp", xin[:, :, :], xp[seg, r0:r0 + TOK, :].rearrange("(b p) d -> p b d", p=128), "xin", [], ["act%d" % c for c in range(16)])
        for (tbi, (t0, nt)) in enumerate(tblocks):
            for c2 in range(2):
                bk = Gbanks.next()
                for cc in range(4):
                    c = c2 * 4 + cc
                    TR(bank(bk)[:, cc * 128:cc * 128 + nt], xin[0:nt, tbi, c * 128:(c + 1) * 128], identf[0:nt, 0:nt],
                       ["act%d" % c_ for c_ in range(16)] + ["identf"], ["ps%d" % bk])
                CP(cp_rot.next(), xT[:, c2 * 4:(c2 + 1) * 4, t0:t0 + nt], bank(bk).rearrange("p (c t) -> p c t", t=128)[:, :, 0:nt], ["ps%d" % bk], ["xT"])
        if SUB <= 1:
            return
        mod_norm(0, 0, ncol, si)
        if SUB <= 2:
            return
        ffn(0, 0, ncol, si, 2)
        if SUB <= 3:
            return
        mod_norm(0, 1, ncol, si)
        if SUB <= 4:
            return
        l0_mixer(kind, ncol, si, slot, tblocks, seg, kind == "prompt" and seg == 1 and a == ST - 1, samp_idx)
        if kind == "halo" or SUB <= 5:
            return
        out_proj("woab", ncol, 0, si, 5)
        mod_norm(0, 2, ncol, si)
        ffn(0, 1, ncol, si, 8)
        if SUB <= 6:
            return
        mod_norm(1, 0, ncol, si)
        ffn(1, 0, ncol, si, 2)
        mod_norm(1, 1, ncol, si)
        if SUB <= 7:
            return
        l1_prep(kind, ncol, tblocks, tile_idx, samp_idx)
        if SUB <= 8:
            return
        sp_i = tile_idx if kind == "prompt" else NQT + samp_idx
        DMA("sp", x_sp[sp_i][:, :, 0:ncol], xT[:, :, 0:ncol], "xT", ["xT"], ["x_sp%d" % sp_i])
        DMA("sp", q_sp[sp_i][:, :, 0:ncol], qo[0:96, :, 0:ncol], "qo", ["qo%d" % h for h in range(16)], ["q_sp%d" % sp_i])
        if kind == "prompt":
            gi_ = g_in[tile_idx // 2]; c0_ = (tile_idx % 2) * TOK
            DMA("sp", gi_[0:256, c0_:c0_ + TOK].rearrange("(c p) t -> p c t", p=128), gT_sb[:, 0:2, :], "gT_sb", ["gT_sb"], ["g_in%d" % (tile_idx // 2)])
            DMA("sp", gi_[256:288, c0_:c0_ + TOK], gT_sb[0:32, 2, :], "gT_sb", ["gT_sb"], ["g_in%d" % (tile_idx // 2)])
        else:
            DMA("sp", g_s[samp_idx, 0:256, 1024:1056].rearrange("(c p) t -> p c t", p=128), gT_sb[:, 0:2, 0:32], "gT_sb", ["gT_sb"], ["g_s%d" % samp_idx])
            DMA("sp", g_s[samp_idx, 256:288, 1024:1056], gT_sb[0:32, 2, 0:32], "gT_sb", ["gT_sb"], ["g_s%d" % samp_idx])

    def sample_cache_latents(s_):
        for half in range(2):
            DMA("sp", cstage[:, 0:4, :], cckv[s_, half * 512:(half + 1) * 512, :].rearrange("(b p) n -> p b n", p=128), "cstage", [], CS_ALL)
            for c in range(2):
                bk = Gbanks.next()
                for blk in range(4):
                    TR(bank(bk)[:, blk * 128:(blk + 1) * 128], cstage[:, blk, c * 128:(c + 1) * 128], identf[:], CS_ALL + ["identf"], ["ps%d" % bk])
                CP(cp_rot.next(), gT_sb[:, c, :], bank(bk), ["ps%d" % bk], ["gT_sb"])
            DMA("sp", g_s[s_, 0:256, half * 512:(half + 1) * 512].rearrange("(c p) t -> p c t", p=128), gT_sb[:, 0:2, :], "gT_sb", ["gT_sb"], ["g_s%d" % s_])
        for half in range(2):
            crv = cstage[:, 4, :].rearrange("p (b n) -> p b n", n=32)[:, 0:4, :]
            DMA("sp", crv, cckr[s_, half * 512:(half + 1) * 512, :].rearrange("(b p) n -> p b n", p=128), "cstage4", [], ["scr2"])
            bk = Gbanks.next()
            for blk in range(4):
                TR(bank(bk)[0:32, blk * 128:(blk + 1) * 128], crv[:, blk, :], identf[:], ["scr2", "identf"], ["ps%d" % bk])
            CP(cp_rot.next(), gT_sb[0:32, 2, :], bank(bk)[0:32, :], ["ps%d" % bk], ["gT_sb"])
            DMA("sp", g_s[s_, 256:288, half * 512:(half + 1) * 512], gT_sb[0:32, 2, :], "gT_sb", ["gT_sb"], ["g_s%d" % s_])

    KTrot = Rot(range(4)); kvTrot = Rot(range(2))

    def phase2_tile(kind, ncol, si, sp_i, key_tiles, out_ap_of):
        DMA("sp", xT[:, :, 0:ncol], x_sp[sp_i][:, :, 0:ncol], "xT", ["x_sp%d" % sp_i], ["xT"])
        SB = [0, 1, 4, 5]
        nv = len(key_tiles)
        for hp in range(8):
            par = hp % 2
            qkey = "qh%d" % par
            DMA("sp", qh[0:96, 2 * par:2 * par + 2, 0:ncol], q_sp[sp_i][:, hp * 2:hp * 2 + 2, 0:ncol], qkey, ["q_sp%d" % sp_i], [qkey])
            bas = [2, 3]
            units = [(vi, hh) for vi in range(nv) for hh in range(2)]
            blocks = [(ui, blk) for ui in range(len(units)) for blk in range(4)]
            nb = len(blocks)
            kt_of = {}

            def stageA(ui):
                vi, hh = units[ui]
                t_, kaug_ap = key_tiles[vi]
                h = hp * 2 + hh
                kti = KTrot.next()
                DMA("sp", KT[0:96, kti, :], ksc[t_, h], "KTn%d" % kti, ["ksc%d" % t_], ["KTn%d" % kti])
                DMA("sp", KT[96:105, kti, :], kaug_ap, "KTa%d" % kti, [], ["KTa%d" % kti])
                DMA("sp", Vh[:, kti, :, 0:64], vsc[t_, h].rearrange("p (b d) -> p b d", d=64), "Vh%d" % kti, ["vsc%d" % t_], ["Vh%d" % kti])
                kt_of[ui] = kti

            def S_(g):
                ui, blk = blocks[g]
                vi, hh = units[ui]
                kti = kt_of[ui]
                bs = SB[g % 4]
                MM(bank(bs)[:, 0:ncol], KT[0:105, kti, blk * 128:(blk + 1) * 128], qh[0:105, 2 * par + hh, 0:ncol], True, True,
                   ["KTn%d" % kti, "KTa%d" % kti, qkey, "qhaug"], ["ps%d" % bs])

            def E_(g):
                bs = SB[g % 4]
                ACT(PT[:, g % 3, 0:ncol], bank(bs)[:, 0:ncol], AF.Exp, ["ps%d" % bs], ["PT%d" % (g % 3)], scale=MLA_SCALE)

            def P_(g):
                ui, blk = blocks[g]
                vi, hh = units[ui]
                kti = kt_of[ui]
                MM(bank(bas[hh])[0:65, 0:ncol], Vh[:, kti, blk, 0:65], PT[:, g % 3, 0:ncol], vi == 0 and blk == 0, vi == nv - 1 and blk == 3,
                   ["PT%d" % (g % 3), "Vh%d" % kti, "Vhones"], ["ps%d" % bas[hh]])

            stageA(0)
            if len(units) > 1:
                stageA(1)
            S_(0)
            S_(1)
            S_(2)
            for g in range(nb):
                ui, blk = blocks[g]
                if blk == 0 and ui + 2 < len(units):
                    stageA(ui + 2)
                E_(g)
                if g + 3 < nb:
                    S_(g + 3)
                P_(g)
            for hh in range(2):
                h = hp * 2 + hh
                attn_finish_p2(bas[hh], ncol, qo[0:64, h, 0:ncol], ["qo%d" % h])
        out_proj("woc", ncol, 1, si, 5)
        mod_norm(1, 2, ncol, si)
        ffn(1, 1, ncol, si, 8)
        norm_mod(ncol, lambda c: gTs[:, 6, c:c + 1], lambda c: None, lambda c: yfin[:, c, 0:ncol], lambda c: ["act%d" % (2 * c), "act%d" % (2 * c + 1)])
        nblk = (ncol + 127) // 128
        for tb in range(nblk):
            nt = min(128, ncol - tb * 128)
            sgi = stgrot.next()
            for c2 in range(2):
                bk = Gbanks.next()
                for cc in range(4):
                    c = c2 * 4 + cc
                    TR(bank(bk)[0:nt, cc * 128:(cc + 1) * 128], yfin[:, c, tb * 128:tb * 128 + nt], identf[:], ["act%d" % (2 * c), "act%d" % (2 * c + 1), "identf"], ["ps%d" % bk])
                CP(cp_rot.next(), stg[0:nt, sgi, c2 * 512:(c2 + 1) * 512], bank(bk)[0:nt, :], ["ps%d" % bk], ["stg%d" % sgi])
            DMA("sp", out_ap_of(tb, nt), stg[0:nt, sgi, :], "stg%d" % sgi, ["stg%d" % sgi], [], final=True)

    NKT = 32 + 6
    ksc = dint("ksc", [NKT, 16, 96, TOK], BF16)
    vsc = dint("vsc", [NKT, 16, 128, 256], BF16)
    KTst = act[0:96, 0:16 * TOK].rearrange("p (h t) -> p h t", t=TOK)
    Vst = act[:, 16 * TOK:24 * TOK].rearrange("p (h b d) -> p h b d", b=4, d=64)
    KST_K = ["act%d" % c for c in range(16)]
    VST_K = ["act%d" % c for c in range(16, 24)]

    def expand_tile(t_, lat2, krs, gkeys):
        kvi = kvTrot.next()
        DMA("sp", kvT[:, kvi, :, :], lat2, "kvT%d" % kvi, gkeys, ["kvT%d" % kvi])
        wv_ = wkvb[:, :, :].rearrange("p c (h e) -> p c h e", e=128)
        for h0 in range(0, 16, 4):
            kk_ = ["act%d" % h for h in range(h0, h0 + 4)]
            for h in range(h0, h0 + 4):
                DMA("sp", KTst[64:96, h, :], krs, "kst%d" % h, gkeys, ["act%d" % h])
            for h in range(h0, h0 + 4):
                bk = Gbanks.next()
                for c in range(2):
                    MM(bank(bk)[0:64, :], wkvb[:, c, h * 128:h * 128 + 64], kvT[:, kvi, c, :], c == 0, c == 1, ["wkvb", "kvT%d" % kvi], ["ps%d" % bk])
                CP(cp_rot.next(), KTst[0:64, h, :], bank(bk)[0:64, :], ["ps%d" % bk], ["act%d" % h])
            DMA("pool", ksc[t_, h0:h0 + 4].rearrange("h p n -> p h n"), KTst[:, h0:h0 + 4, :], "kst_o%d" % (h0 // 4), kk_, ["ksc%d" % t_])
        for half in range(2):
            vk_ = ["act%d" % c for c in range(16 + 4 * half, 20 + 4 * half)]
            for blk in range(4):
                bk = Gbanks.next()
                for c in range(2):
                    MM(bank(bk)[:, :].rearrange("p (h d) -> p h d", d=64), kvT[:, kvi, c, blk * 128:(blk + 1) * 128],
                       wv_[:, c, half * 8:(half + 1) * 8, 64:128], c == 0, c == 1, ["wkvb", "kvT%d" % kvi], ["ps%d" % bk])
                CP(cp_rot.next(), Vst[:, half * 8:(half + 1) * 8, blk, :], bank(bk)[:, :].rearrange("p (h d) -> p h d", d=64), ["ps%d" % bk], vk_)
            for q4 in range(2):
                h0 = half * 8 + q4 * 4
                DMA("pool", vsc[t_, h0:h0 + 4].rearrange("h p (b d) -> p h b d", d=64), Vst[:, h0:h0 + 4, :, :], "vst_o%d" % (half * 2 + q4),
                    ["act%d" % c for c in range(16 + h0 // 2, 16 + h0 // 2 + 2)], ["vsc%d" % t_])

    def attn_finish_p2(ba, ncol, dest_ap, destW):
        CP("dve", den[64:65, 0:ncol], bank(ba)[64:65, 0:ncol], ["ps%d" % ba], ["den"])
        bb = Sbanks.next()
        MM(bank(bb)[0:64, 0:ncol], onesf[64:65, 0:64], den[64:65, 0:ncol], True, True, ["onesf", "den"], ["ps%d" % bb])
        RCP(rb[:, 0:ncol], bank(bb)[0:64, 0:ncol], ["ps%d" % bb], ["rb"])
        TT("dve", dest_ap, bank(ba)[0:64, 0:ncol], rb[:, 0:ncol], ALU.mult, ["ps%d" % ba, "rb"], destW)

    zb = PT[:, 0, 0:480]
    MSET("dve", zb, 0.0, ["PT0"])
    for s_ in range(2):
        DMA("sp", g_s[s_, 0:256, 1056:1536].rearrange("(c p) t -> p c t", p=128)[:, 0, :], zb, "zb%d" % s_, ["PT0"], ["g_s%d" % s_])
        DMA("sp", g_s[s_, 0:256, 1056:1536].rearrange("(c p) t -> p c t", p=128)[:, 1, :], zb, "zb%d" % s_, ["PT0"], ["g_s%d" % s_])
        DMA("sp", g_s[s_, 256:288, 1056:1536], zb[0:32, :], "zb%d" % s_, ["PT0"], ["g_s%d" % s_])
    tile_idx = 0
    for seg in range(NSEG if 'p1' in STAGES else 0):
        for a in range(-1, ST):
            slot = (a + 1) % 2
            if a < 0:
                phase1_tile("halo", TOK, 0, slot, seg, a, -1, 0)
            else:
                first_own[0] = (a == 0)
                phase1_tile("prompt", TOK, 0, slot, seg, a, tile_idx, 0)
                tile_idx += 1
    for s_ in range(2 if 's1' in STAGES else 0):
        phase1_tile("samp", 32, 1 + s_, 0, 0, 0, 0, s_)
        if SUB >= 10:
            sample_cache_latents(s_)
    if 'ag' in STAGES:
        for gc in range(NGC):
            S.op("pool", lambda e, gc=gc: e.collective_compute("AllGather", ALU.bypass, replica_groups=[[0, 1, 2, 3], [4, 5, 6, 7]],
                                                               ins=[g_in[gc].opt()], outs=[g_all[gc].opt()]), ["g_in%d" % gc], ["g_all%d" % gc])
    MSET("dve", Vh[:, :, :, 64:65], 1.0, ["VA0", "VA1", "VAones", "Vhones"])
    for s_ in range(2 if 's2' in STAGES else 0):
        kts = []
        for v in range(3):
            lat2 = g_s[s_, 0:256, v * TOK:(v + 1) * TOK].rearrange("(c p) t -> p c t", p=128)
            t_ = 32 + 3 * s_ + v
            expand_tile(t_, lat2, g_s[s_, 256:288, v * TOK:(v + 1) * TOK], ["g_s%d" % s_])
            kts.append((t_, kaugs_d[v]))
        phase2_tile("samp", 32, 1 + s_, NQT + s_, kts, lambda tb, nt, s_=s_: y_s[s_ * 32:s_ * 32 + 32, :])
    if 'p2' in STAGES:
        for v in range(32):
            jj, off = gloc(v)
            lt_ = off // TOK
            ga_ = g_all[lt_ // 2]; c0_ = (lt_ % 2) * TOK
            lat2 = ga_[jj * 288:jj * 288 + 256, c0_:c0_ + TOK].rearrange("(c p) t -> p c t", p=128)
            expand_tile(v, lat2, ga_[jj * 288 + 256:jj * 288 + 288, c0_:c0_ + TOK], ["g_all%d" % (lt_ // 2)])
    for qi in range(NQT if 'p2' in STAGES else 0):
        kts = [(v, kaug_d[qi, v]) for v in range(nv_of(qi))]
        phase2_tile("prompt", TOK, 0, qi, kts, lambda tb, nt, qi=qi: y_p[qi * TOK + tb * 128:qi * TOK + tb * 128 + nt, :])

    S.emit(st)
    st.close()
    return nc


_NC_CACHE = {}
STOP = 9
STAGES = {'s1', 'p1', 'ag', 's2', 'p2'}
OUTQ = 'sp'
SUB = 99


def _cpu():
    return jax.default_device(jax.devices("cpu")[0])


def _t5_bucket(rel):
    nb = 16
    rel = jnp.asarray(rel)
    ret = jnp.where(rel > 0, nb, 0)
    n = jnp.abs(rel)
    max_exact = nb // 2
    large = max_exact + (jnp.log(jnp.maximum(n, 1).astype(jnp.float32) / max_exact)
                         / np.log(128 / max_exact) * (nb - max_exact)).astype(jnp.int32)
    large = jnp.minimum(large, nb - 1)
    return np.asarray(ret + jnp.where(n < max_exact, n, large))


def _tables(rel_bias_a, t5_bias):
    tab = np.asarray(rel_bias_a[0], np.float32)
    t5 = np.asarray(t5_bias, np.float32)
    kk = np.arange(128)[:, None]
    col = np.arange(640)[None, :]
    u = col - 512
    rel = kk - col
    d = (kk >= 64).astype(np.int64) - np.floor_divide(u, 64)
    idx = np.clip(rel, -128, 128) + 128
    FA = np.where(((d >= 0) & (d <= 8))[None], tab[idx].transpose(2, 0, 1), np.float32(NEG)).astype(np.float32)
    col = np.arange(256)[None, :]
    u = col - 512
    rel = kk - col
    d = (kk >= 64).astype(np.int64) - np.floor_divide(u, 64)
    FB = np.where(((d >= 6) & (d <= 8))[None], t5[_t5_bucket(rel)].transpose(2, 0, 1), np.float32(NEG)).astype(np.float32)
    q = np.arange(32)[None, None, :]
    blk = np.arange(5)[None, :, None]
    kk3 = np.arange(128)[:, None, None]
    rel = 128 * blk + kk3 - 512 - q
    FAs = tab[np.clip(rel, -128, 128) + 128].transpose(3, 0, 1, 2).astype(np.float32)
    blk = np.arange(2)[None, :, None]
    rel = 128 * blk + kk3 - 128 - q
    FBs = t5[_t5_bucket(rel)].transpose(3, 0, 1, 2).astype(np.float32)
    return FA, FB, FAs, FBs


def _rope_tab(pos):
    half = 16
    inv = 10000.0 ** (-jnp.arange(half, dtype=jnp.float32) / half)
    ang = jnp.asarray(pos).astype(jnp.float32)[:, None] * inv[None, :]
    return np.asarray(jnp.cos(ang)), np.asarray(jnp.sin(ang))


def kernel(x_prompt, x_sample, c_prompt, c_sample, cache_a_k, cache_a_v, cache_b_k, cache_b_v,
           cache_c_kv, cache_c_kr, w_ada, b_ada, norm_g, final_norm_g, ffn_w_gate, ffn_w_up, ffn_w_down,
           w_in_ab, w_out_ab, rel_bias_a, t5_bias, sinks_b, w_in_c, c_q_norm_g, c_kv_norm_g, w_qb, w_kvb,
           w_out_c):
    if "nc" not in _NC_CACHE:
        _NC_CACHE["nc"] = build_program()
    nc = _NC_CACHE["nc"]
    in_maps = make_in_maps(x_prompt, x_sample, c_prompt, c_sample, cache_a_k, cache_a_v, cache_b_k, cache_b_v,
                           cache_c_kv, cache_c_kr, w_ada, b_ada, norm_g, final_norm_g, ffn_w_gate, ffn_w_up, ffn_w_down,
                           w_in_ab, w_out_ab, rel_bias_a, t5_bias, sinks_b, w_in_c, c_q_norm_g, c_kv_norm_g, w_qb, w_kvb,
                           w_out_c)
    res = run_bass_kernel_spmd(nc, in_maps, core_ids=list(range(8)))
    return assemble(res.results)


def make_in_maps(x_prompt, x_sample, c_prompt, c_sample, cache_a_k, cache_a_v, cache_b_k, cache_b_v,
                 cache_c_kv, cache_c_kr, w_ada, b_ada, norm_g, final_norm_g, ffn_w_gate, ffn_w_up, ffn_w_down,
                 w_in_ab, w_out_ab, rel_bias_a, t5_bias, sinks_b, w_in_c, c_q_norm_g, c_kv_norm_g, w_qb, w_kvb,
                 w_out_c):
    f = lambda a: np.ascontiguousarray(np.asarray(a, np.float32))
    x_prompt = f(x_prompt); x_sample = f(x_sample)
    with _cpu():
        FA, FB, FAs, FBs = _tables(f(rel_bias_a), f(t5_bias))
    b_adaT = f(b_ada).reshape(2, 72, 128).transpose(2, 0, 1)
    gT = np.concatenate([f(norm_g).reshape(6, 8, 128), f(final_norm_g).reshape(1, 8, 128)], 0).transpose(2, 0, 1)
    qaug = np.zeros((9, TOK), np.float32)
    qc = np.arange(TOK) // 64
    for r in range(8):
        qaug[r] = -(qc < r).astype(np.float32)
    qaug[8] = -1.0
    kaugs = np.zeros((3, 9, TOK), np.float32)
    kidx = np.arange(3 * TOK).reshape(3, TOK)
    kaugs[:, 8, :] = np.where(kidx >= 1056, BIG, 0.0)
    with _cpu():
        cS, sS = _rope_tab(1024 + np.arange(32))
    shared = dict(
        w_ada=f(w_ada), b_adaT=f(b_adaT), gT=f(gT), ffn_w_gate=f(ffn_w_gate), ffn_w_up=f(ffn_w_up), ffn_w_down=f(ffn_w_down),
        w_in_ab=f(w_in_ab)[0], w_out_ab=f(w_out_ab)[0], w_in_c=f(w_in_c)[0], w_qb=f(w_qb)[0], w_kvb=f(w_kvb)[0], w_out_c=f(w_out_c)[0],
        gq=f(np.broadcast_to(f(c_q_norm_g)[0][None], (128, 384))), gkv=f(np.broadcast_to(f(c_kv_norm_g)[0][None], (128, 256))),
        FA=FA, FB=FB, FAs=FAs, FBs=FBs, sinks=f(np.broadcast_to(f(sinks_b)[0][None], (128, 8))),
        cosS=f(np.tile(cS, (1, 16))), sinS=f(np.tile(sS, (1, 16))),
        kaugs=kaugs.astype(ml_dtypes.bfloat16), qaug=qaug.astype(ml_dtypes.bfloat16),
        identf=np.eye(128, dtype=np.float32), identb=np.eye(128, dtype=np.float32).astype(ml_dtypes.bfloat16),
    )
    in_maps = []
    for c in range(8):
        b, j = divmod(c, 4)
        xp = np.zeros((NSEG, (ST + 1) * TOK, D), np.float32)
        hvv = np.zeros((128, 4), np.float32)
        pos = np.zeros(NQT * TOK, np.int64)
        for k in range(NSEG):
            s0 = seg_of(j, k) * ST * TOK
            if s0 == 0:
                xp[k, TOK:] = x_prompt[b, 0:ST * TOK]
                hvv[:, k] = NEG
            else:
                xp[k] = x_prompt[b, s0 - TOK:s0 + ST * TOK]
            pos[k * ST * TOK:(k + 1) * ST * TOK] = s0 + np.arange(ST * TOK)
        with _cpu():
            cP, sP = _rope_tab(pos)
        cosP = np.tile(cP.reshape(NQT * 4, 128, 1, 16), (1, 1, 16, 1)).reshape(NQT * 4, 128, 256)
        sinP = np.tile(sP.reshape(NQT * 4, 128, 1, 16), (1, 1, 16, 1)).reshape(NQT * 4, 128, 256)
        cvec = np.stack([f(c_prompt)[b], f(c_sample)[2 * c], f(c_sample)[2 * c + 1]], -1)
        kaug = np.zeros((NQT, 32, 9, TOK), np.float32)
        kc = np.arange(TOK) // 64
        for qi in range(NQT):
            k, a = divmod(qi, ST)
            T = seg_of(j, k) * ST + a
            for v in range(nv_of(qi)):
                if v == T:
                    for r in range(8):
                        kaug[qi, v, r] = (kc == r) * BIG
                elif v > T:
                    kaug[qi, v, 8] = BIG
        m = dict(shared)
        m.update(
            xp=xp, xs=f(x_sample[2 * c:2 * c + 2]), cT=f(cvec.reshape(8, 128, 3).transpose(1, 0, 2)),
            cak=f(cache_a_k)[0, 2 * c:2 * c + 2].reshape(2, 512, 512), cav=f(cache_a_v)[0, 2 * c:2 * c + 2].reshape(2, 512, 512),
            cbk=f(cache_b_k)[0, 2 * c:2 * c + 2].reshape(2, 128, 128), cbv=f(cache_b_v)[0, 2 * c:2 * c + 2].reshape(2, 128, 128),
            cckv=f(cache_c_kv)[0, 2 * c:2 * c + 2], cckr=f(cache_c_kr)[0, 2 * c:2 * c + 2],
            hv=hvv, cosP=f(cosP), sinP=f(sinP), kaug=kaug.astype(ml_dtypes.bfloat16),
        )
        in_maps.append({k_: np.ascontiguousarray(v_) for k_, v_ in m.items()})
    return in_maps


def assemble(R):
    y_prompt = np.zeros((2, 16384, D), np.float32)
    ckv_p = np.zeros((1, 2, 16384, 256), np.float32)
    ckr_p = np.zeros((1, 2, 16384, 32), np.float32)
    for c in range(8):
        b, j = divmod(c, 4)
        for k in range(NSEG):
            s0 = seg_of(j, k) * ST * TOK
            sl = slice(k * ST * TOK, (k + 1) * ST * TOK)
            y_prompt[b, s0:s0 + ST * TOK] = R[c]["y_p"][sl]
            ckv_p[0, b, s0:s0 + ST * TOK] = R[c]["o_ckv"][sl]
            ckr_p[0, b, s0:s0 + ST * TOK] = R[c]["o_ckr"][sl]
    cat = lambda name: np.concatenate([R[c][name] for c in range(8)], 0)
    y_sample = cat("y_s").reshape(16, 32, D)
    last = [0, 4]
    a_k_p = np.stack([R[c]["o_ak"] for c in last], 0).reshape(1, 2, 512, 8, 64)
    a_v_p = np.stack([R[c]["o_av"] for c in last], 0).reshape(1, 2, 512, 8, 64)
    b_k_p = np.stack([R[c]["o_bk"] for c in last], 0).reshape(1, 2, 128, 2, 64)
    b_v_p = np.stack([R[c]["o_bv"] for c in last], 0).reshape(1, 2, 128, 2, 64)
    return (y_prompt, y_sample, a_k_p, a_v_p, b_k_p, b_v_p, ckv_p, ckr_p,
            cat("o_aks").reshape(1, 16, 32, 8, 64), cat("o_avs").reshape(1, 16, 32, 8, 64),
            cat("o_bks").reshape(1, 16, 32, 2, 64), cat("o_bvs").reshape(1, 16, 32, 2, 64),
            cat("o_ckvs").reshape(1, 16, 32, 256), cat("o_ckrs").reshape(1, 16, 32, 32))
```
